# Optimizing a Trainium2 kernel written in Bass

```python
import math
import jax
import jax.numpy as jnp
from jax import lax
import numpy as np

D_MODEL = 1024
BATCH = 16
SEQ = 2048
DEPTH = 2

N_A_LAYERS = DEPTH // 2
N_B_LAYERS = DEPTH - N_A_LAYERS

SSM_EXPAND = 2
D_INNER = SSM_EXPAND * D_MODEL
SSM_HEAD_DIM = 64
SSM_HEADS = D_INNER // SSM_HEAD_DIM
SSM_GROUPS = 4
SSM_HEADS_PER_GROUP = SSM_HEADS // SSM_GROUPS
SSM_STATE = 128
SSM_CONV = 4
SSM_CHUNK = 128
SSM_CONV_DIM = D_INNER + 2 * SSM_GROUPS * SSM_STATE
SSM_IN_DIM = D_INNER + SSM_CONV_DIM + SSM_HEADS

ATT_HEAD_DIM = 64
ATT_HEADS = D_MODEL // (2 * ATT_HEAD_DIM)
ATT_V_DIM = 2 * ATT_HEAD_DIM
ATT_BLOCK = 128
K_DIM = ATT_HEADS * 2 * ATT_HEAD_DIM
V_TOTAL = ATT_HEADS * ATT_V_DIM

NUM_BUCKETS = 32
MAX_DISTANCE = 128

FFN_DIM = 256 * ((8 * D_MODEL // 3 + 255) // 256)
FFN_CONV = 3

EPS = 1e-6

kernel_name = 'yoco_mamba2_diffattn_convffn'


def rms_norm(x, g):
    xf = x.astype(jnp.float32)
    y = xf * lax.rsqrt(jnp.mean(xf * xf, axis=-1, keepdims=True) + EPS)
    return y.astype(x.dtype) * g


def causal_dwconv(u, w, b):
    k_w = w.shape[0]
    s = u.shape[1]
    up = jnp.pad(u, ((0, 0), (k_w - 1, 0), (0, 0)))
    y = up[:, 0:s] * w[0]
    for k in range(1, k_w):
        y = y + up[:, k:k + s] * w[k]
    return y + b


def segsum(a):
    t = a.shape[-1]
    cs = jnp.cumsum(a, axis=-1)
    d = cs[..., :, None] - cs[..., None, :]
    mask = jnp.tril(jnp.ones((t, t), dtype=bool))
    return jnp.where(mask, d, -jnp.inf)


def ssd_chunked(xdt, a, bm, cm):
    b, s, h, p = xdt.shape
    g, r, l = SSM_GROUPS, SSM_HEADS_PER_GROUP, SSM_CHUNK
    c = s // l
    xc = xdt.reshape(b, c, l, g, r, p)
    ac = a.reshape(b, c, l, g, r).transpose(0, 3, 4, 1, 2)
    bc = bm.reshape(b, c, l, g, SSM_STATE)
    cc = cm.reshape(b, c, l, g, SSM_STATE)
    a_cs = jnp.cumsum(ac, axis=-1)
    decay_in = jnp.exp(segsum(ac))
    cb = jnp.einsum('bclgn,bcsgn->bcgls', cc, bc)
    y_diag = jnp.einsum('bcgls,bgrcls,bcsgrp->bclgrp', cb, decay_in, xc)
    decay_states = jnp.exp(a_cs[..., -1:] - a_cs)
    states = jnp.einsum('bclgn,bgrcl,bclgrp->bcgrpn', bc, decay_states, xc)
    states0 = jnp.concatenate([jnp.zeros_like(states[:, :1]), states], axis=1)
    chunk_tot = jnp.pad(a_cs[..., -1], ((0, 0), (0, 0), (0, 0), (1, 0)))
    decay_chunk = jnp.exp(segsum(chunk_tot))
    new_states = jnp.einsum('bgrzc,bcgrpn->bzgrpn', decay_chunk, states0)
    states_in = new_states[:, :-1]
    y_off = jnp.einsum('bclgn,bcgrpn,bgrcl->bclgrp', cc, states_in, jnp.exp(a_cs))
    return (y_diag + y_off).reshape(b, s, h, p)


def mamba2_mixer(x, ln_g, in_w, conv_w, conv_b, dt_bias, a_log, d_skip, norm_g, out_w):
    b, s, _ = x.shape
    zxbcdt = rms_norm(x, ln_g) @ in_w
    z = zxbcdt[..., :D_INNER]
    xbc = zxbcdt[..., D_INNER:D_INNER + SSM_CONV_DIM]
    dt = zxbcdt[..., D_INNER + SSM_CONV_DIM:]
    xbc = jax.nn.silu(causal_dwconv(xbc, conv_w, conv_b))
    gn = SSM_GROUPS * SSM_STATE
    xs = xbc[..., :D_INNER].reshape(b, s, SSM_HEADS, SSM_HEAD_DIM)
    bm = xbc[..., D_INNER:D_INNER + gn].reshape(b, s, SSM_GROUPS, SSM_STATE)
    cm = xbc[..., D_INNER + gn:].reshape(b, s, SSM_GROUPS, SSM_STATE)
    dt = jax.nn.softplus(dt.astype(jnp.float32) + dt_bias.astype(jnp.float32))
    a = dt * (-jnp.exp(a_log.astype(jnp.float32)))
    xs32 = xs.astype(jnp.float32)
    y = ssd_chunked(xs32 * dt[..., None], a, bm.astype(jnp.float32), cm.astype(jnp.float32))
    y = y + xs32 * d_skip.astype(jnp.float32)[:, None]
    y = y.reshape(b, s, D_INNER) * jax.nn.silu(z.astype(jnp.float32))
    y = rms_norm(y.reshape(b, s, SSM_GROUPS, D_INNER // SSM_GROUPS),
                 norm_g.reshape(SSM_GROUPS, D_INNER // SSM_GROUPS)).reshape(b, s, D_INNER)
    return y @ out_w


def conv_ffn(x, ln_g, up_w, conv_w, conv_b, down_w):
    u = causal_dwconv(rms_norm(x, ln_g) @ up_w, conv_w, conv_b)
    gate, up = u[..., :FFN_DIM], u[..., FFN_DIM:]
    return (jax.nn.silu(gate) * up) @ down_w


def shared_kv(x, kv_ln_g, kv_w, k_norm_g):
    b, s, _ = x.shape
    kv = rms_norm(x, kv_ln_g) @ kv_w
    k = kv[..., :K_DIM].reshape(b, s, ATT_HEADS, 2, ATT_HEAD_DIM)
    v = kv[..., K_DIM:].reshape(b, s, ATT_HEADS, ATT_V_DIM)
    return rms_norm(k, k_norm_g), v


def t5_bucket(dist):
    n = jnp.maximum(dist, 0)
    max_exact = NUM_BUCKETS // 2
    nf = jnp.maximum(n, 1).astype(jnp.float32)
    large = max_exact + (jnp.log(nf / max_exact) / math.log(MAX_DISTANCE / max_exact)
                         * (NUM_BUCKETS - max_exact)).astype(jnp.int32)
    large = jnp.minimum(large, NUM_BUCKETS - 1)
    return jnp.where(n < max_exact, n, large)


def diff_attention(x, k, v, rel_bias, ln_g, q_w, q_norm_g, lam_vecs, subln_g, out_w, lam_init):
    b, s, _ = x.shape
    q = (rms_norm(x, ln_g) @ q_w).reshape(b, s, ATT_HEADS, 2, ATT_HEAD_DIM)
    q = rms_norm(q, q_norm_g)
    lv = lam_vecs.astype(jnp.float32)
    lam = jnp.exp(jnp.sum(lv[0] * lv[1])) - jnp.exp(jnp.sum(lv[2] * lv[3])) + lam_init
    scale = ATT_HEAD_DIM ** -0.5
    nb = s // ATT_BLOCK
    qb = q.reshape(b, nb, ATT_BLOCK, ATT_HEADS, 2, ATT_HEAD_DIM).transpose(1, 0, 2, 3, 4, 5)
    kpos = jnp.arange(s)

    def block(args):
        qblk, i = args
        qpos = i * ATT_BLOCK + jnp.arange(ATT_BLOCK)
        dist = qpos[:, None] - kpos[None, :]
        bias = rel_bias[t5_bucket(dist)].astype(jnp.float32).transpose(2, 0, 1)
        logits = jnp.einsum('bqhcd,bkhcd->bhcqk', qblk, k).astype(jnp.float32) * scale
        logits = logits + bias[None, :, None]
        logits = jnp.where(dist >= 0, logits, -jnp.inf)
        p = jax.nn.softmax(logits, axis=-1)
        attn = p[:, :, 0] - lam * p[:, :, 1]
        return jnp.einsum('bhqk,bkhd->bqhd', attn.astype(v.dtype), v)

    o = lax.map(block, (qb, jnp.arange(nb)))
    o = o.transpose(1, 0, 2, 3, 4).reshape(b, s, ATT_HEADS, ATT_V_DIM)
    o = rms_norm(o, subln_g) * (1.0 - lam_init)
    return o.reshape(b, s, V_TOTAL) @ out_w


def setup_inputs(seed: int = 0) -> dict:
    key = jax.random.key(seed)
    ks = jax.random.split(key, 32)
    f32 = jnp.float32

    def nrm(k, shape, scale):
        return jax.random.normal(k, shape, f32) * scale

    def gain(k, shape):
        return 1.0 + 0.02 * jax.random.normal(k, shape, f32)

    na, nb = N_A_LAYERS, N_B_LAYERS
    dt = jnp.exp(jax.random.uniform(ks[6], (na, SSM_HEADS), f32)
                 * (math.log(0.1) - math.log(0.001)) + math.log(0.001))
    return {
        'x': nrm(ks[0], (BATCH, SEQ, D_MODEL), 1.0),
        'ssm_ln_g': gain(ks[1], (na, D_MODEL)),
        'ssm_in_w': nrm(ks[2], (na, D_MODEL, SSM_IN_DIM), D_MODEL ** -0.5),
        'ssm_conv_w': nrm(ks[3], (na, SSM_CONV, SSM_CONV_DIM), SSM_CONV ** -0.5),
        'ssm_conv_b': nrm(ks[4], (na, SSM_CONV_DIM), 0.02),
        'ssm_dt_bias': dt + jnp.log(-jnp.expm1(-dt)),
        'ssm_a_log': jnp.log(jax.random.uniform(ks[7], (na, SSM_HEADS), f32, 1.0, 16.0)),
        'ssm_d': gain(ks[8], (na, SSM_HEADS)),
        'ssm_norm_g': gain(ks[9], (na, D_INNER)),
        'ssm_out_w': nrm(ks[10], (na, D_INNER, D_MODEL), D_INNER ** -0.5),
        'kv_ln_g': gain(ks[11], (D_MODEL,)),
        'kv_w': nrm(ks[12], (D_MODEL, K_DIM + V_TOTAL), D_MODEL ** -0.5),
        'k_norm_g': gain(ks[13], (ATT_HEAD_DIM,)),
        'rel_bias': nrm(ks[14], (NUM_BUCKETS, ATT_HEADS), 0.5),
        'attn_ln_g': gain(ks[15], (nb, D_MODEL)),
        'q_w': nrm(ks[16], (nb, D_MODEL, K_DIM), D_MODEL ** -0.5),
        'q_norm_g': gain(ks[17], (nb, ATT_HEAD_DIM)),
        'lam_vecs': nrm(ks[18], (nb, 4, ATT_HEAD_DIM), 0.1),
        'subln_g': gain(ks[19], (nb, ATT_V_DIM)),
        'attn_out_w': nrm(ks[20], (nb, V_TOTAL, D_MODEL), V_TOTAL ** -0.5),
        'ffn_ln_g': gain(ks[21], (DEPTH, D_MODEL)),
        'ffn_up_w': nrm(ks[22], (DEPTH, D_MODEL, 2 * FFN_DIM), D_MODEL ** -0.5),
        'ffn_conv_w': nrm(ks[23], (DEPTH, FFN_CONV, 2 * FFN_DIM), FFN_CONV ** -0.5),
        'ffn_conv_b': nrm(ks[24], (DEPTH, 2 * FFN_DIM), 0.02),
        'ffn_down_w': nrm(ks[25], (DEPTH, FFN_DIM, D_MODEL), FFN_DIM ** -0.5),
    }


def reference(x, ssm_ln_g, ssm_in_w, ssm_conv_w, ssm_conv_b, ssm_dt_bias, ssm_a_log, ssm_d,
              ssm_norm_g, ssm_out_w, kv_ln_g, kv_w, k_norm_g, rel_bias, attn_ln_g, q_w,
              q_norm_g, lam_vecs, subln_g, attn_out_w, ffn_ln_g, ffn_up_w, ffn_conv_w,
              ffn_conv_b, ffn_down_w):
    k_sh, v_sh = None, None
    for layer in range(DEPTH):
        if layer < N_A_LAYERS:
            i = layer
            mix = mamba2_mixer(x, ssm_ln_g[i], ssm_in_w[i], ssm_conv_w[i], ssm_conv_b[i],
                               ssm_dt_bias[i], ssm_a_log[i], ssm_d[i], ssm_norm_g[i],
                               ssm_out_w[i])
        else:
            j = layer - N_A_LAYERS
            if j == 0:
                k_sh, v_sh = shared_kv(x, kv_ln_g, kv_w, k_norm_g)
            lam_init = 0.8 - 0.6 * math.exp(-0.3 * layer)
            mix = diff_attention(x, k_sh, v_sh, rel_bias, attn_ln_g[j], q_w[j], q_norm_g[j],
                                 lam_vecs[j], subln_g[j], attn_out_w[j], lam_init)
        x = x + mix.astype(x.dtype)
        ff = conv_ffn(x, ffn_ln_g[layer], ffn_up_w[layer], ffn_conv_w[layer],
                      ffn_conv_b[layer], ffn_down_w[layer])
        x = x + ff.astype(x.dtype)
    return x
```

```python
import math
from contextlib import ExitStack
import numpy as np
import concourse.bass as bass
import concourse.mybir as mybir
from concourse.bass_utils import run_bass_kernel_spmd

F32 = mybir.dt.float32
BF16 = mybir.dt.bfloat16
AF = mybir.ActivationFunctionType
ALU = mybir.AluOpType
AX = mybir.AxisListType

NCORES = 8
D = 1024
SEQ = 2048
T = 512
NT = SEQ // T
FFN = 2816
DIN = 2048
EPS = 1e-6
LAM_INIT = 0.8 - 0.6 * math.exp(-0.3 * 1)
WSLOT = 5632
NEG = -30000.0

ENGS = ("pe", "act", "dve", "pool", "sp")


class _Op:
    __slots__ = ("id", "eng", "fn", "deps", "dma", "key", "signaled", "count", "pos", "rawdeps")


class Sched:
    def __init__(self, nc):
        self.nc = nc
        self.ops = []
        self.eng_ops = {e: [] for e in ENGS}
        self.last_w = {}
        self.readers = {}
        self.key_count = {}
        self.bar_deps = set()
        self.bar_pending = set()
        self.fence_dmas = []

    def _add(self, eng, fn, r, w, dma, key, fenced):
        op = _Op()
        op.id = len(self.ops)
        op.eng = eng
        op.fn = fn
        op.dma = dma
        op.key = key
        op.signaled = False
        op.count = 0
        op.pos = len(self.eng_ops[eng])
        deps = set()
        raw = set()
        for t in r:
            lw = self.last_w.get(t)
            if lw is not None:
                deps.add(lw)
                raw.add(lw)
        for t in w:
            lw = self.last_w.get(t)
            if lw is not None:
                deps.add(lw)
            for rd in self.readers.get(t, ()):
                deps.add(rd)
        for t in r:
            self.readers.setdefault(t, []).append(op.id)
        for t in w:
            self.last_w[t] = op.id
            self.readers[t] = []
        if dma:
            if fenced:
                deps |= self.bar_deps
                self.fence_dmas.append(op.id)
        elif eng in self.bar_pending:
            deps |= self.bar_deps
            self.bar_pending.discard(eng)
        deps.discard(op.id)
        op.deps = deps
        op.rawdeps = raw
        if dma:
            self.key_count[key] = self.key_count.get(key, 0) + 1
            op.count = self.key_count[key]
        self.ops.append(op)
        self.eng_ops[eng].append(op)
        return op

    def op(self, eng, fn, r=(), w=()):
        return self._add(eng, fn, tuple(r), tuple(w), False, None, False)

    def dma(self, q, fn, r=(), w=(), key=None, fenced=False):
        assert key is not None
        return self._add(q, fn, tuple(r), tuple(w), True, key, fenced)

    def barrier(self):
        deps = set()
        for e in ("pe", "act", "dve", "pool"):
            for op in reversed(self.eng_ops[e]):
                if not op.dma:
                    deps.add(op.id)
                    break
        deps |= set(self.fence_dmas)
        self.fence_dmas = []
        self.bar_deps = deps
        self.bar_pending = {"pe", "act", "dve", "pool"}

    def emit(self):
        nc = self.nc
        ops = self.ops
        need = {}
        for op in ops:
            lst = []
            for d in op.deps:
                p = ops[d]
                if p.dma:
                    lst.append(p)
                elif p.eng != op.eng or op.dma:
                    lst.append(p)
                    p.signaled = True
                else:
                    if op.eng != "pe" and d in op.rawdeps and op.pos - p.pos <= 2:
                        lst.append(p)
                        p.signaled = True
            need[op.id] = lst
        cnt = {e: 0 for e in ENGS}
        for op in ops:
            if not op.dma and op.signaled:
                cnt[op.eng] += 1
                op.count = cnt[op.eng]
        keys = sorted(self.key_count.keys(), key=str)
        with ExitStack() as st:
            esem = {e: st.enter_context(nc.semaphore("s_" + e)) for e in ENGS}
            ksem = {k: st.enter_context(nc.semaphore("d_%d" % i)) for i, k in enumerate(keys)}
            block = st.enter_context(nc.Block())

            def body(ename):
                def _f(e):
                    waited = {}
                    for op in self.eng_ops[ename]:
                        for p in need[op.id]:
                            if p.dma:
                                sem, val, sk = ksem[p.key], 16 * p.count, ("k", p.key)
                            else:
                                sem, val, sk = esem[p.eng], p.count, ("e", p.eng)
                            if waited.get(sk, 0) >= val:
                                continue
                            waited[sk] = val
                            e.wait_ge(sem, val)
                        ins = op.fn(e)
                        if op.dma:
                            ins.then_inc(ksem[op.key], 16)
                        elif op.signaled:
                            ins.then_inc(esem[op.eng], 1)
                    if ename == "sp":
                        for k in keys:
                            e.wait_ge(ksem[k], 16 * self.key_count[k])
                return _f

            block.tensor(body("pe"))
            block.scalar(body("act"))
            block.vector(body("dve"))
            block.gpsimd(body("pool"))
            block.sync(body("sp"))


WSPEC = {
    "in": (8, 4, 10),
    "dt": (8, None, 1),
    "out": (16, 2, 4),
    "up0": (8, 4, 11),
    "dn0": (22, 2, 4),
    "kv": (8, 4, 4),
    "q": (8, 4, 2),
    "ao": (8, 4, 2),
    "up1": (8, 4, 11),
    "dn1": (22, 2, 4),
}
WORDER = ["in", "dt", "out", "up0", "dn0", "kv", "q", "ao", "up1", "dn1"]


def _welems(name):
    kc, g, nb = WSPEC[name]
    return kc * (32 if g is None else g * 128)


def _blockify(w, g):
    K, N = w.shape
    kc = K // 128
    nb = N // (g * 128)
    a = w.reshape(kc, 128, nb, g * 128).transpose(2, 1, 0, 3)
    return np.ascontiguousarray(a).reshape(nb, 128, kc * g * 128)


def _cv_layout():
    off = {}
    n = 0

    def add(name, cnt):
        nonlocal n
        off[name] = n
        n += cnt

    add("g_ssm", 8)
    add("g_ffn0", 8)
    add("g_ffn1", 8)
    add("g_kv", 8)
    add("g_attn", 8)
    add("mcw", 24 * 4)
    add("mcb", 24)
    add("fcw0", 44 * 3)
    add("fcb0", 44)
    add("fcw1", 44 * 3)
    add("fcb1", 44)
    add("ng", 16)
    add("dsk", 16)
    add("kng", 1)
    add("qng", 1)
    add("slg", 1)
    add("b31", 8)
    add("dtb", 128)
    add("alog", 128)
    add("lamv", 256)
    return off, n


CVOFF, NCV = _cv_layout()


class Builder:
    def __init__(self, nc, ntiles=NT, nseq=2, phases="MFKAG", dbg=None):
        self.nc = nc
        self.s = Sched(nc)
        self.ntiles = ntiles
        self.nseq = nseq
        self.phases = phases
        self.dbg = dbg
        self._uid = 0
        self.dram()
        self.sbuf()

    def dram(self):
        nc = self.nc
        self.xT = nc.dram_tensor("xT", [2, D, SEQ], F32, kind="ExternalInput").ap()
        self.yT = nc.dram_tensor("yT", [2, D, SEQ], F32, kind="ExternalOutput").ap()
        self.cv_d = nc.dram_tensor("cvec", [128, NCV], F32, kind="ExternalInput").ap()
        self.bt_d = nc.dram_tensor("biast", [128, 8, 256], F32, kind="ExternalInput").ap()
        self.w32 = {}
        self.wbf = {}
        for n in WORDER:
            kc, g, nb = WSPEC[n]
            el = _welems(n)
            self.w32[n] = nc.dram_tensor("w_" + n, [nb, 128, el], F32, kind="ExternalInput").ap()
            self.wbf[n] = nc.dram_tensor("wb_" + n, [nb, 128, el], BF16).ap()
        self.kT_d = nc.dram_tensor("kT_s", [2, 8, 128, SEQ], BF16).ap()
        self.v_d = nc.dram_tensor("v_s", [2, SEQ, D], BF16).ap()

    def sb(self, name, shape, dt):
        return self.nc.alloc_sbuf_tensor(name, list(shape), dt)

    def sbuf(self):
        nc = self.nc
        self.x = self.sb("x", [128, 8, T], F32)
        self.xn = self.sb("xn", [128, 8, T], BF16)
        self.xsq = self.sb("xsq", [128, 8, T], BF16)
        self.rs = self.sb("rs", [128, T], F32)
        self.wsl = [self.sb("wsl%d" % i, [128, WSLOT], BF16) for i in range(3)]
        self.cv = self.sb("cv", [128, NCV], F32)
        self.ident = self.sb("ident", [128, 128], BF16)
        self.ones = self.sb("ones", [128, 128], BF16)
        self.bones = self.sb("bones", [128, 128], BF16)
        self.onesf = self.sb("onesf", [128, 128], F32)
        self.tri = self.sb("tri", [128, 128], F32)
        self.trib = self.sb("trib", [128, 128], BF16)
        self.ustr = self.sb("ustr", [128, 128], BF16)
        self.maskneg = self.sb("maskneg", [128, 128], F32)
        self.biast = self.sb("biast_sb", [128, 8, 256], F32)
        self.smallc = self.sb("smallc", [128, 64], F32)
        self.Ab = self.sb("Ab", [128, 128], F32)
        self.dgs = [self.sb("dg%d" % i, [128, 128], BF16) for i in range(8)]
        self.uext = [self.sb("uext%d" % i, [128, T + 4], BF16) for i in range(3)]
        self.halo_m = self.sb("halo_m", [128, 24, 4], BF16)
        self.halo_f = [self.sb("halo_f%d" % l, [128, 44, 2], BF16) for l in range(2)]
        ARENA = 52000
        self.arena = self.sb("arena", [128, ARENA], BF16)
        self.S = self.sb("S", [128, DIN], F32)
        self.Sb = self.sb("Sb", [128, DIN], BF16)
        self.banks = [nc.alloc_psum_tensor("bank%d" % i, [128, 512], F32) for i in range(8)]

    def carve(self, off, nelem_bf16, dt=BF16):
        ap = self.arena[:, off:off + nelem_bf16]
        if dt == F32:
            ap = ap.bitcast(F32)
        return ap

    def uid(self):
        self._uid += 1
        return self._uid

    def dump(self, name, ap, rtoks, shape, dt=F32):
        if not self.dbg:
            return
        d = self.nc.dram_tensor("dbg_" + name, list(shape), dt, kind="ExternalOutput").ap()
        self.s.dma("sp", lambda e: e.dma_start(out=d, in_=ap), r=rtoks, key=("dbg", name))

    def O(self, eng, fn, r=(), w=()):
        return self.s.op(eng, fn, r, w)

    @staticmethod
    def bc_last(ap, n):
        return bass.AP(ap.tensor, ap.offset, [list(ap.ap[0]), list(ap.ap[1]), [0, n]])

    @staticmethod
    def bc_mid(ap, n):
        return bass.AP(ap.tensor, ap.offset, [list(ap.ap[0]), [0, n], list(ap.ap[1])])

    def col(self, name, i=0):
        o = CVOFF[name] + i
        return self.cv[:, o:o + 1]

    def setup(self):
        s = self.s
        b = self
        s.dma("sp", lambda e: e.dma_start(out=b.cv[:], in_=b.cv_d), w=["cv"], key="cvl")
        s.dma("sp", lambda e: e.dma_start(out=b.biast[:], in_=b.bt_d), w=["biast"], key="btl")
        P = "pool"

        def sel(t, pattern, op, base, cm, fill=0.0):
            return lambda e: e.affine_select(out=t[:], in_=t[:], pattern=pattern, compare_op=op, fill=fill,
                                             base=base, channel_multiplier=cm)
        for t in (b.ident, b.ones, b.onesf, b.tri, b.trib, b.ustr):
            self.O(P, lambda e, t=t: e.memset(t[:], 1.0), w=["c0"])
        self.O(P, lambda e: e.memset(b.maskneg[:], 0.0), w=["c0"])
        self.O(P, lambda e: e.memset(b.bones[:], 0.0), w=["c0"])
        self.O(P, lambda e: e.memset(b.bones[0:64, 0:64], 1.0), w=["c0"])
        self.O(P, lambda e: e.memset(b.bones[64:128, 64:128], 1.0), w=["c0"])
        self.O(P, sel(b.ident, [[1, 128]], ALU.is_equal, 0, -1), r=["c0"], w=["c1"])
        self.O(P, sel(b.tri, [[1, 128]], ALU.is_ge, 0, -1), r=["c0"], w=["c1"])
        self.O(P, sel(b.trib, [[1, 128]], ALU.is_ge, 0, -1), r=["c0"], w=["c1"])
        self.O(P, sel(b.ustr, [[-1, 128]], ALU.is_ge, -1, 1), r=["c0"], w=["c1"])
        self.O(P, sel(b.maskneg, [[1, 128]], ALU.is_ge, 0, -1, fill=NEG), r=["c0"], w=["c1"])
        self.O(P, lambda e: e.memset(b.halo_m[:], 0.0), w=["halo_m"])
        for l in range(2):
            self.O(P, lambda e, l=l: e.memset(b.halo_f[l][:], 0.0), w=[("halo_f", l)])
        for h in range(8):
            self.O("dve", lambda e, h=h: e.tensor_tensor(out=b.biast[:, h, 0:128], in0=b.biast[:, h, 0:128],
                                                         in1=b.maskneg[:], op=ALU.add),
                   r=["biast", "c1"], w=["biast"])
        ao = CVOFF["alog"]
        self.O("act", lambda e: e.activation(out=b.Ab[:], in_=b.cv[:, ao:ao + 128], func=AF.Exp), r=["cv"], w=["Ab"])
        self.O("dve", lambda e: e.tensor_scalar(out=b.Ab[:], in0=b.Ab[:], scalar1=-1.0, scalar2=None, op0=ALU.mult),
               r=["Ab"], w=["Ab"])
        lo = CVOFF["lamv"]
        sc = b.smallc
        self.tmp64 = self.sb("tmp64", [128, 128], F32)
        self.O("dve", lambda e: e.tensor_tensor(out=b.tmp64[:, 0:64], in0=b.cv[:, lo:lo + 64], in1=b.cv[:, lo + 64:lo + 128],
                                                op=ALU.mult), r=["cv"], w=["t64a"])
        self.O("dve", lambda e: e.tensor_tensor(out=b.tmp64[:, 64:128], in0=b.cv[:, lo + 128:lo + 192],
                                                in1=b.cv[:, lo + 192:lo + 256], op=ALU.mult), r=["cv"], w=["t64b"])
        self.O("dve", lambda e: e.reduce_sum(out=sc[:, 2:3], in_=b.tmp64[:, 0:64], axis=AX.X), r=["t64a"], w=["sc2"])
        self.O("dve", lambda e: e.reduce_sum(out=sc[:, 3:4], in_=b.tmp64[:, 64:128], axis=AX.X), r=["t64b"], w=["sc3"])
        self.O("act", lambda e: e.activation(out=sc[:, 4:6], in_=sc[:, 2:4], func=AF.Exp), r=["sc2", "sc3"], w=["sc4"])
        self.O("dve", lambda e: e.scalar_tensor_tensor(out=sc[:, 0:1], in0=sc[:, 5:6], scalar=-LAM_INIT, in1=sc[:, 4:5],
                                                       op0=ALU.add, op1=ALU.subtract), r=["sc4"], w=["neglam"])
        self.O("pool", lambda e: e.memset(sc[:, 6:7], EPS), w=["epsc"])
        self.O("dve", lambda e: e.tensor_scalar(out=sc[:, 1:2], in0=b.col("slg"), scalar1=(1.0 - LAM_INIT), scalar2=None,
                                                op0=ALU.mult), r=["cv"], w=["slg2"])

    def casts(self):
        s = self.s
        i = 0
        for n in WORDER:
            kc, g, nb = WSPEC[n]
            el = _welems(n)
            cb = 2048 if el % 2048 == 0 else (1024 if el % 1024 == 0 else (el if el <= 2048 else 1408))
            assert el % cb == 0, (n, el, cb)
            for blk in range(nb):
                s.dma("pool",
                      lambda e, n=n, blk=blk, cb=cb: e.dma_start(
                          out=self.wbf[n][blk].rearrange("p (a b) -> p a b", b=cb),
                          in_=self.w32[n][blk].rearrange("p (a b) -> p a b", b=cb)),
                      r=[], w=[("wd", n, blk), ("castslot", i % 8)], key=("cast", n, blk))
                i += 1

    def ws_init(self, seq):
        self.wseq = seq
        self.wi = 0
        self.wissued = 0

    def ws_issue(self, idx):
        n, blk = self.wseq[idx]
        slot = idx % 3
        el = _welems(n)
        self.s.dma("sp", lambda e: e.dma_start(out=self.wsl[slot][:, 0:el], in_=self.wbf[n][blk]),
                   r=[("wd", n, blk)], w=[("ws", slot)], key=("ws", slot))

    def ws_next(self, n, blk):
        idx = self.wi
        assert self.wseq[idx] == (n, blk), (self.wseq[idx], n, blk)
        while self.wissued <= min(idx + 2, len(self.wseq) - 1):
            self.ws_issue(self.wissued)
            self.wissued += 1
        self.wi += 1
        return idx % 3

    def norm(self, gname):
        b = self
        for kc in range(8):
            self.O("act", lambda e, kc=kc: e.activation(out=b.xsq[:, kc, :], in_=b.x[:, kc, :], func=AF.Square),
                   r=[("x", kc)], w=[("xsq", kc)])
        for kc in range(8):
            self.O("pe", lambda e, kc=kc: e.matmul(b.banks[6][:], lhsT=b.ones[:], rhs=b.xsq[:, kc, :],
                                                   start=(kc == 0), stop=(kc == 7)),
                   r=[("xsq", kc), "c0"], w=[("b", 6)])
        self.rstd(b.rs[:], "rs", 6, 1.0 / D)
        for kc in range(8):
            self.O("dve", lambda e, kc=kc: e.scalar_tensor_tensor(out=b.xn[:, kc, :], in0=b.x[:, kc, :],
                                                                  scalar=b.col(gname, kc), in1=b.rs[:],
                                                                  op0=ALU.mult, op1=ALU.mult),
                   r=[("x", kc), "rs", "cv"], w=[("xn", kc)])

    def rstd(self, out, tok, bank, scale):
        b = self
        self.O("act", lambda e: e.activation(out=out, in_=b.banks[bank][:], func=AF.Ln, bias=b.smallc[:, 6:7], scale=scale),
               r=[("b", bank), "epsc"], w=[tok])
        self.O("act", lambda e: e.activation(out=out, in_=out, func=AF.Exp, scale=-0.5), r=[tok], w=[tok])

    def pbank(self):
        self._pb = (getattr(self, "_pb", -1) + 1) % 4
        return self._pb

    def cbank(self):
        self._cb = (getattr(self, "_cb", -1) + 1) % 2
        return 4 + self._cb

    def dg(self):
        self._dg = (getattr(self, "_dg", -1) + 1) % 8
        return self._dg

    def ue(self):
        self._ue = (getattr(self, "_ue", -1) + 1) % 3
        return self._ue

    def proj(self, slot, KC, stride, c0, act, acttok):
        b = self
        pb = self.pbank()
        for kc in range(KC):
            self.O("pe", lambda e, kc=kc: e.matmul(b.banks[pb][:], lhsT=b.wsl[slot][:, kc * stride + c0:kc * stride + c0 + 128],
                                                   rhs=act[:, kc, :], start=(kc == 0), stop=(kc == KC - 1)),
                   r=[("ws", slot), (acttok, kc)], w=[("b", pb)])
        return pb

    def conv(self, pb, K, halo, htok, cj, wname, first):
        b = self
        u = self.ue()
        ut = ("uext", u)
        H = K - 1
        ub = b.uext[u]
        if first:
            self.O("dve", lambda e: e.memset(ub[:, 0:H], 0.0), w=[ut])
        else:
            self.O("dve", lambda e: e.tensor_copy(out=ub[:, 0:H], in_=halo[:, cj, 0:H]), r=[htok], w=[ut])
        self.O("act", lambda e: e.activation(out=ub[:, H:H + T], in_=b.banks[pb][:], func=AF.Identity),
               r=[("b", pb)], w=[ut])
        self.O("dve", lambda e: e.tensor_copy(out=halo[:, cj, 0:H], in_=ub[:, T:T + H]), r=[ut], w=[htok])
        cb = self.cbank()
        for k in range(K):
            d = self.dg()
            self.O("dve", lambda e, d=d, k=k: e.tensor_scalar(out=b.dgs[d][:], in0=b.ident[:], scalar1=b.col(wname, cj * K + k),
                                                              scalar2=None, op0=ALU.mult),
                   r=["c1", "cv"], w=[("dg", d)])
            self.O("pe", lambda e, d=d, k=k: e.matmul(b.banks[cb][:], lhsT=b.dgs[d][:], rhs=ub[:, k:k + T],
                                                      start=(k == 0), stop=(k == K - 1)),
                   r=[("dg", d), ut], w=[("b", cb)])
        return cb

    def ffn(self, L, first):
        b = self
        self.norm("g_ffn%d" % L)
        h = self.carve(0, 22 * T).rearrange("p (a t) -> p a t", t=T)
        up = "up%d" % L
        dn = "dn%d" % L
        for blk in range(11):
            slot = self.ws_next(up, blk)
            for g in range(4):
                is_gate = g < 2
                hj = 2 * blk + (g % 2)
                cj = hj if is_gate else 22 + hj
                pb = self.proj(slot, 8, 512, g * 128, b.xn, "xn")
                cb = self.conv(pb, 3, b.halo_f[L], ("halo_f", L, cj), cj, "fcw%d" % L, first)
                bias = b.col("fcb%d" % L, cj)
                if is_gate:
                    self.O("act", lambda e, hj=hj, cb=cb, bias=bias: e.activation(out=h[:, hj, :], in_=b.banks[cb][:],
                                                                                   func=AF.Silu, bias=bias),
                           r=[("b", cb), "cv"], w=[("h", hj)])
                else:
                    self.O("dve", lambda e, hj=hj, cb=cb, bias=bias: e.scalar_tensor_tensor(
                        out=h[:, hj, :], in0=b.banks[cb][:], scalar=bias, in1=h[:, hj, :], op0=ALU.add, op1=ALU.mult),
                        r=[("b", cb), "cv", ("h", hj)], w=[("h", hj)])
        for blk in range(4):
            slot = self.ws_next(dn, blk)
            for g in range(2):
                j = blk * 2 + g
                pb = self.pbank()
                for fc in range(22):
                    self.O("pe", lambda e, fc=fc, pb=pb, g=g, slot=slot: e.matmul(
                        b.banks[pb][:], lhsT=b.wsl[slot][:, fc * 256 + g * 128:fc * 256 + g * 128 + 128],
                        rhs=h[:, fc, :], start=(fc == 0), stop=(fc == 21)),
                        r=[("ws", slot), ("h", fc)], w=[("b", pb)])
                self.O("dve", lambda e, j=j, pb=pb: e.tensor_tensor(out=b.x[:, j, :], in0=b.banks[pb][:], in1=b.x[:, j, :],
                                                                    op=ALU.add),
                       r=[("b", pb), ("x", j)], w=[("x", j)])

    def load_x(self, sq, ti):
        b = self
        src = b.xT[sq].rearrange("(kc p) t -> p kc t", p=128)[:, :, ti * T:(ti + 1) * T]
        self.s.dma("sp", lambda e: e.dma_start(out=b.x[:], in_=src), w=[("x", kc) for kc in range(8)], key="xl")

    def store_x(self, sq, ti):
        b = self
        dst = b.yT[sq].rearrange("(kc p) t -> p kc t", p=128)[:, :, ti * T:(ti + 1) * T]
        self.s.dma("sp", lambda e: e.dma_start(out=dst, in_=b.x[:]), r=[("x", kc) for kc in range(8)], key="xs")

    def mixer(self, sq, ti):
        b = self
        first = ti == 0
        self.norm("g_ssm")
        v3 = lambda off, n: self.carve(off, n * T).rearrange("p (a t) -> p a t", t=T)
        zs = v3(0, 16)
        xbc = v3(8192, 24)
        yg = zs
        xdt = self.carve(28672, 2048)
        xdtw = self.carve(30720, 2048)
        btok = self.carve(32768, 512).rearrange("p (g n) -> p g n", n=128)
        Ah = [self.carve(20480 + i * 1024, 1024).rearrange("p (h l) -> p h l", l=128) for i in range(4)]
        E = [self.carve(24576 + i * 1024, 1024).rearrange("p (h l) -> p h l", l=128) for i in range(4)]
        M = [self.carve(33280 + i * 1024, 1024).rearrange("p (h l) -> p h l", l=128) for i in range(4)]
        cbm = self.carve(39424, 512).rearrange("p (g l) -> p g l", l=128)
        ytok = self.carve(39936, 2048)
        t1 = [self.carve(41984 + i * 1024, 1024, F32) for i in range(2)]
        tmpf = [self.carve(44032 + i * 256, 256, F32) for i in range(2)]
        sm = [self.carve(44544 + i * 256, 256, F32) for i in range(8)]
        tmpf2 = [self.carve(46592 + i * 2048, 2048, F32) for i in range(2)]
        dtr, dtt, aa, e1 = sm[0], sm[1], sm[2], sm[3]
        acs_sb, eacs, wdec, dec = sm[4], sm[5], sm[6], sm[7]
        bf = lambda i: b.banks[i][:].bitcast(BF16)
        if first:
            self.O("dve", lambda e: e.memset(b.S[:], 0.0), w=["S"])
            self.O("dve", lambda e: e.memset(b.Sb[:], 0.0), w=["Sb"])
        for blk in range(10):
            slot = self.ws_next("in", blk)
            for g in range(4):
                j = blk * 4 + g
                pb = self.proj(slot, 8, 512, g * 128, b.xn, "xn")
                if j < 16:
                    self.O("act", lambda e, j=j, pb=pb: e.activation(out=zs[:, j, :], in_=b.banks[pb][:], func=AF.Silu),
                           r=[("b", pb)], w=[("zs", j)])
                else:
                    cj = j - 16
                    cb = self.conv(pb, 4, b.halo_m, ("halo_m", cj), cj, "mcw", first)
                    self.O("act", lambda e, cj=cj, cb=cb: e.activation(out=xbc[:, cj, :], in_=b.banks[cb][:], func=AF.Silu,
                                                                       bias=b.col("mcb", cj)),
                           r=[("b", cb), "cv"], w=[("xbc", cj)])
        slot = self.ws_next("dt", 0)
        for c in range(4):
            for kc in range(8):
                self.O("pe", lambda e, c=c, kc=kc, slot=slot: e.matmul(b.banks[7][:, c * 32:(c + 1) * 32],
                                                            lhsT=b.xn[:, kc, c * 128:(c + 1) * 128],
                                                            rhs=b.wsl[slot][:, kc * 32:(kc + 1) * 32],
                                                            start=(kc == 0), stop=(kc == 7)),
                       r=[("ws", slot), ("xn", kc)], w=[("b", 7)])
        do = CVOFF["dtb"]
        self.O("dve", lambda e: e.tensor_tensor(out=dtr[:], in0=b.banks[7][:, 0:128], in1=b.cv[:, do:do + 128], op=ALU.add),
               r=[("b", 7), "cv"], w=["dtr"])
        self.O("act", lambda e: e.activation(out=e1[:], in_=dtr[:], func=AF.Exp), r=["dtr"], w=["e1"])
        self.O("act", lambda e: e.activation(out=dtt[:], in_=e1[:], func=AF.Ln, bias=1.0), r=["e1"], w=["dt"])
        self.O("dve", lambda e: e.tensor_tensor(out=aa[:], in0=dtt[:], in1=b.Ab[:], op=ALU.mult), r=["dt", "Ab"], w=["aa"])
        self.dump("dtt", dtt[:], ["dt"], [128, 128])
        self.dump("dtr", dtr[:], ["dtr"], [128, 128])
        self.dump("x0", xbc[:, 0, :], [("xbc", 0)], [128, 512], BF16)
        self.dump("B0", xbc[:, 16, :], [("xbc", 16)], [128, 512], BF16)
        self.dump("z0", zs[:, 0, :], [("zs", 0)], [128, 512], BF16)
        def chunk(c):
            cs = slice(c * 128, (c + 1) * 128)
            a_c = aa[:, c * 32:(c + 1) * 32]
            dt_c = dtt[:, c * 32:(c + 1) * 32]
            self.O("pe", lambda e, a_c=a_c: e.matmul(b.banks[6][:, 0:32], lhsT=b.tri[:], rhs=a_c, start=True, stop=True),
                   r=["aa", "c1"], w=[("b", 6)])
            self.O("pe", lambda e, a_c=a_c: e.matmul(b.banks[6][:, 32:64], lhsT=b.onesf[:], rhs=a_c, start=True, stop=True),
                   r=["aa", "c0"], w=[("b", 6)])
            self.O("act", lambda e: e.activation(out=eacs[:, 0:32], in_=b.banks[6][:, 0:32], func=AF.Exp),
                   r=[("b", 6)], w=["eacs"])
            self.O("dve", lambda e: e.tensor_copy(out=acs_sb[:, 0:32], in_=b.banks[6][:, 0:32]), r=[("b", 6)], w=["acs"])
            self.O("dve", lambda e: e.tensor_tensor(out=wdec[:, 0:32], in0=b.banks[6][:, 32:64], in1=acs_sb[:, 0:32],
                                                    op=ALU.subtract), r=[("b", 6), "acs"], w=["wd"])
            self.O("act", lambda e: e.activation(out=wdec[:, 0:32], in_=wdec[:, 0:32], func=AF.Exp), r=["wd"], w=["wd"])
            self.O("act", lambda e: e.activation(out=dec[:, 0:32], in_=b.banks[6][:, 32:64], func=AF.Exp),
                   r=[("b", 6)], w=["dec"])
            for j in range(16):
                bk = 4 + j // 8
                self.O("pe", lambda e, j=j, bk=bk: e.transpose(out=bf(bk)[:, (j % 8) * 128:(j % 8 + 1) * 128],
                                                               in_=xbc[:, j, cs], identity=b.ident[:]),
                       r=[("xbc", j), "c1"], w=[("b", bk)])
            for half in range(2):
                self.O("dve", lambda e, half=half: e.tensor_tensor(
                    out=xdt[:, half * 1024:(half + 1) * 1024].rearrange("p (h d) -> p h d", d=64),
                    in0=bf(4 + half)[:, 0:1024].rearrange("p (h d) -> p h d", d=64),
                    in1=self.bc_last(dt_c[:, half * 16:(half + 1) * 16], 64), op=ALU.mult),
                    r=[("b", 4 + half), "dt"], w=[("xdt", 2 * half), ("xdt", 2 * half + 1)])
            self.O("pool", lambda e: e.tensor_tensor(out=xdtw[:].rearrange("p (h d) -> p h d", d=64),
                                                     in0=xdt[:].rearrange("p (h d) -> p h d", d=64),
                                                     in1=self.bc_last(wdec[:, 0:32], 64), op=ALU.mult),
                   r=[("xdt", q) for q in range(4)] + ["wd"], w=[("xdtw", q) for q in range(4)])
            for g in range(4):
                self.O("pe", lambda e, g=g: e.transpose(out=bf(4)[:, g * 128:(g + 1) * 128], in_=xbc[:, 16 + g, cs],
                                                        identity=b.ident[:]),
                       r=[("xbc", 16 + g), "c1"], w=[("b", 4)])
            self.O("act", lambda e: e.activation(out=btok[:].rearrange("p g n -> p (g n)"), in_=bf(4)[:, 0:512],
                                                 func=AF.Identity), r=[("b", 4)], w=["btok"])
            for g in range(4):
                self.O("pe", lambda e, g=g: e.matmul(b.banks[0][:, g * 128:(g + 1) * 128], lhsT=xbc[:, 16 + g, cs],
                                                     rhs=xbc[:, 20 + g, cs], start=True, stop=True),
                       r=[("xbc", 16 + g), ("xbc", 20 + g)], w=[("b", 0)])
            for g in range(4):
                self.O("dve", lambda e, g=g: e.tensor_tensor(out=cbm[:, g, :], in0=b.banks[0][:, g * 128:(g + 1) * 128],
                                                             in1=b.tri[:], op=ALU.mult),
                       r=[("b", 0), "c1"], w=[("cbm", g)])
            for g in range(4):
                self.O("pool", lambda e, g=g: e.tensor_tensor(out=Ah[g][:], in0=self.bc_mid(b.trib[:], 8),
                                                              in1=self.bc_last(a_c[:, g * 8:(g + 1) * 8], 128), op=ALU.mult),
                       r=["c1", "aa"], w=[("Ah", g)])
            for g in range(4):
                self.O("pool", lambda e, g=g: e.tensor_tensor(
                    out=b.S[:, g * 512:(g + 1) * 512].rearrange("p (h d) -> p h d", d=64),
                    in0=b.S[:, g * 512:(g + 1) * 512].rearrange("p (h d) -> p h d", d=64),
                    in1=self.bc_last(dec[:, g * 8:(g + 1) * 8], 64), op=ALU.mult),
                    r=["dec", ("S", g)], w=[("S", g)])
            for g in range(4):
                for i in range(2):
                    self.O("pe", lambda e, i=i, g=g: e.matmul(b.banks[1 + i][:], lhsT=b.ustr[:],
                                                              rhs=Ah[g][:, i * 4:(i + 1) * 4, :].rearrange("p h l -> p (h l)"),
                                                              start=True, stop=True),
                           r=[("Ah", g), "c1"], w=[("b", 1 + i)])
                    self.O("act", lambda e, i=i, g=g: e.activation(
                        out=E[g][:, i * 4:(i + 1) * 4, :].rearrange("p h l -> p (h l)"), in_=b.banks[1 + i][:], func=AF.Exp),
                        r=[("b", 1 + i)], w=[("E", g, i)])
            for g in range(4):
                for hh in range(8):
                    self.O("dve", lambda e, hh=hh, g=g: e.tensor_tensor(out=M[g][:, hh, :], in0=E[g][:, hh, :],
                                                                        in1=cbm[:, g, :], op=ALU.mult),
                           r=[("E", g, hh // 4), ("cbm", g)], w=[("M", g)])
            for g in range(4):
                ob = (3, 1)[g % 2]
                db = (7, 2)[g % 2]
                self.O("pe", lambda e, g=g, ob=ob: e.matmul(b.banks[ob][:], lhsT=xbc[:, 20 + g, cs],
                                                            rhs=b.Sb[:, g * 512:(g + 1) * 512], start=True, stop=True),
                       r=[("xbc", 20 + g), ("Sb", g)], w=[("b", ob)])
                for hh in range(8):
                    h = g * 8 + hh
                    self.O("pe", lambda e, hh=hh, h=h, g=g, db=db: e.matmul(b.banks[db][:, hh * 64:(hh + 1) * 64],
                                                                            lhsT=M[g][:, hh, :],
                                                                            rhs=xdt[:, h * 64:(h + 1) * 64], start=True, stop=True),
                           r=[("M", g), ("xdt", g)], w=[("b", db)])
                tt = t1[g % 2]
                self.O("dve", lambda e, g=g, tt=tt, ob=ob: e.tensor_tensor(
                    out=tt[:].rearrange("p (h d) -> p h d", d=64), in0=b.banks[ob][:].rearrange("p (h d) -> p h d", d=64),
                    in1=self.bc_last(eacs[:, g * 8:(g + 1) * 8], 64), op=ALU.mult),
                    r=[("b", ob), "eacs"], w=[("t1", g % 2)])
                self.O("dve", lambda e, g=g, tt=tt, db=db: e.tensor_tensor(out=ytok[:, g * 512:(g + 1) * 512], in0=b.banks[db][:],
                                                                           in1=tt[:], op=ALU.add),
                       r=[("b", db), ("t1", g % 2)], w=[("ytok", g)])
            for g in range(4):
                sbk = (0, 6)[g % 2]
                self.O("pe", lambda e, g=g, sbk=sbk: e.matmul(b.banks[sbk][:], lhsT=btok[:, g, :], rhs=xdtw[:, g * 512:(g + 1) * 512],
                                                              start=True, stop=True),
                       r=["btok", ("xdtw", g)] + [("cbm", q) for q in range(4)], w=[("b", sbk)])
                self.O("dve", lambda e, g=g, sbk=sbk: e.tensor_tensor(out=b.S[:, g * 512:(g + 1) * 512], in0=b.banks[sbk][:],
                                                                      in1=b.S[:, g * 512:(g + 1) * 512], op=ALU.add),
                       r=[("b", sbk), ("S", g)], w=[("S", g)])
                self.O("act", lambda e, g=g: e.activation(out=b.Sb[:, g * 512:(g + 1) * 512], in_=b.S[:, g * 512:(g + 1) * 512],
                                                          func=AF.Identity), r=[("S", g)], w=[("Sb", g)])
            if c == 0:
                self.dump("ytok0", ytok[:], [("ytok", q) for q in range(4)], [128, 2048], BF16)
                self.dump("eacs0", eacs[:, 0:32], ["eacs"], [128, 32])
                self.dump("acs0", acs_sb[:, 0:32], ["acs"], [128, 32])
                self.dump("S0", b.S[:], [("S", q) for q in range(4)], [128, 2048])
                self.dump("xdt0", xdt[:], [("xdt", q) for q in range(4)], [128, 2048], BF16)
            for j in range(16):
                bk = 4 + j // 8
                self.O("pe", lambda e, j=j, bk=bk: e.transpose(out=bf(bk)[:, (j % 8) * 128:(j % 8 + 1) * 128],
                                                               in_=ytok[:, j * 128:(j + 1) * 128], identity=b.ident[:]),
                       r=[("ytok", j // 4), "c1"], w=[("b", bk)])
            for half in range(2):
                j0 = half * 8
                tf = tmpf2[half]
                self.O("pool", lambda e, j0=j0, tf=tf: e.tensor_tensor(
                    out=tf[:].rearrange("p (j l) -> p j l", l=128), in0=xbc[:, j0:j0 + 8, cs],
                    in1=self.bc_last(b.cv[:, CVOFF["dsk"] + j0:CVOFF["dsk"] + j0 + 8], 128), op=ALU.mult),
                    r=[("xbc", j0 + q) for q in range(8)] + ["cv"], w=[("tmpf", half)])
                self.O("dve", lambda e, half=half, tf=tf: e.tensor_tensor(
                    out=tf[:].rearrange("p (j l) -> p j l", l=128), in0=bf(4 + half)[:, 0:1024].rearrange("p (j l) -> p j l", l=128),
                    in1=tf[:].rearrange("p (j l) -> p j l", l=128), op=ALU.add),
                    r=[("b", 4 + half), ("tmpf", half)], w=[("tmpf", half)])
                self.O("pool", lambda e, j0=j0, tf=tf: e.tensor_tensor(
                    out=yg[:, j0:j0 + 8, cs], in0=tf[:].rearrange("p (j l) -> p j l", l=128), in1=zs[:, j0:j0 + 8, cs],
                    op=ALU.mult),
                    r=[("tmpf", half)] + [("zs", j0 + q) for q in range(8)], w=[("zs", j0 + q) for q in range(8)])
        for c in range(4):
            chunk(c)
        for g in range(4):
            for q in range(4):
                j = g * 4 + q
                self.O("act", lambda e, j=j, q=q: e.activation(out=b.xsq[:, q, :], in_=yg[:, j, :], func=AF.Square),
                       r=[("zs", j)], w=[("xsq", q)])
            for q in range(4):
                self.O("pe", lambda e, q=q: e.matmul(b.banks[6][:], lhsT=b.ones[:], rhs=b.xsq[:, q, :], start=(q == 0),
                                                     stop=(q == 3)), r=[("xsq", q), "c0"], w=[("b", 6)])
            self.rstd(b.rs[:], "rs", 6, 1.0 / 512)
            for q in range(4):
                j = g * 4 + q
                self.O("dve", lambda e, j=j: e.scalar_tensor_tensor(out=yg[:, j, :], in0=yg[:, j, :], scalar=b.col("ng", j),
                                                                    in1=b.rs[:], op0=ALU.mult, op1=ALU.mult),
                       r=[("zs", j), "rs", "cv"], w=[("zs", j)])
        for blk in range(4):
            slot = self.ws_next("out", blk)
            for g2 in range(2):
                j = blk * 2 + g2
                pb = self.proj(slot, 16, 256, g2 * 128, yg, "zs")
                self.O("dve", lambda e, j=j, pb=pb: e.tensor_tensor(out=b.x[:, j, :], in0=b.banks[pb][:], in1=b.x[:, j, :],
                                                                    op=ALU.add),
                       r=[("b", pb), ("x", j)], w=[("x", j)])

    def headnorm(self, pb, gcol, outap, outtok, bufs, i):
        b = self
        ksq, rk = bufs
        self.O("act", lambda e: e.activation(out=ksq[i][:], in_=b.banks[pb][:], func=AF.Square), r=[("b", pb)], w=[("ksq", i)])
        self.O("pe", lambda e: e.matmul(b.banks[6][:], lhsT=b.bones[:], rhs=ksq[i][:], start=True, stop=True),
               r=[("ksq", i), "c0"], w=[("b", 6)])
        self.rstd(rk[i][:], ("rk", i), 6, 1.0 / 64)
        self.O("dve", lambda e: e.scalar_tensor_tensor(out=outap, in0=b.banks[pb][:], scalar=gcol, in1=rk[i][:],
                                                       op0=ALU.mult, op1=ALU.mult),
               r=[("b", pb), ("rk", i), "cv"], w=[outtok])

    def kvproj(self, sq, ti):
        b = self
        s = self.s
        self.norm("g_kv")
        kt = [self.carve(i * 512, 512) for i in range(2)]
        ksq = [self.carve(1024 + i * 512, 512) for i in range(2)]
        rk = [self.carve(2048 + i * 1024, 1024, F32) for i in range(2)]
        vt = [self.carve(4096 + i * 512, 512) for i in range(2)]
        for blk in range(2):
            slot = self.ws_next("kv", blk)
            for g in range(4):
                h = blk * 4 + g
                i = h % 2
                pb = self.proj(slot, 8, 512, g * 128, b.xn, "xn")
                self.headnorm(pb, b.col("kng"), kt[i][:], ("kt", i), (ksq, rk), i)
                s.dma("sp", lambda e, h=h, i=i: e.dma_start(out=b.kT_d[sq, h, :, ti * T:(ti + 1) * T], in_=kt[i][:]),
                      r=[("kt", i)], w=[("kTd", sq, h, ti)], key=("kst", i), fenced=True)
        for blk in range(2, 4):
            slot = self.ws_next("kv", blk)
            nh = blk - 2
            for c in range(4):
                pb = self.pbank()
                i = c % 2
                for kc in range(8):
                    self.O("pe", lambda e, kc=kc, c=c, pb=pb, slot=slot: e.matmul(b.banks[pb][:], lhsT=b.xn[:, kc, c * 128:(c + 1) * 128],
                                                                       rhs=b.wsl[slot][:, kc * 512:(kc + 1) * 512],
                                                                       start=(kc == 0), stop=(kc == 7)),
                           r=[("ws", slot), ("xn", kc)], w=[("b", pb)])
                self.O("act", lambda e, i=i, pb=pb: e.activation(out=vt[i][:], in_=b.banks[pb][:], func=AF.Identity),
                       r=[("b", pb)], w=[("vt", i)])
                s.dma("sp", lambda e, c=c, i=i, nh=nh: e.dma_start(
                    out=b.v_d[sq, ti * T + c * 128:ti * T + (c + 1) * 128, nh * 512:(nh + 1) * 512], in_=vt[i][:]),
                    r=[("vt", i)], w=[("vd", sq, ti, c, nh)], key=("vst", i), fenced=True)

    def attn(self, sq, ti):
        b = self
        s = self.s
        self.norm("g_attn")
        v3 = lambda off: self.carve(off, 8 * T).rearrange("p (a t) -> p a t", t=T)
        qT = [v3(0), v3(4096)]
        oT = v3(8192)
        KT = [self.carve(12288 + i * 2048, 2048) for i in range(2)]
        VV = [self.carve(16384 + i * 2048, 2048).rearrange("p (k d) -> p k d", d=128) for i in range(2)]
        PT = [self.carve(20480 + i * 512, 512) for i in range(4)]
        tmp = [self.carve(22528 + i * 1024, 1024, F32) for i in range(10)]
        ksq = [self.carve(32768 + i * 512, 512) for i in range(2)]
        rk = [tmp[8], tmp[9]]
        self.O("pool", lambda e: e.memset(qT[0][64:128, :, :], 0.0), w=[("qT0z",)])
        self.O("pool", lambda e: e.memset(qT[1][0:64, :, :], 0.0), w=[("qT1z",)])
        for blk in range(2):
            slot = self.ws_next("q", blk)
            for g in range(4):
                h = blk * 4 + g
                i = h % 2
                pb = self.proj(slot, 8, 512, g * 128, b.xn, "xn")
                self.O("act", lambda e, pb=pb, i=i: e.activation(out=ksq[i][:], in_=b.banks[pb][:], func=AF.Square),
                       r=[("b", pb)], w=[("ksq", i)])
                self.O("pe", lambda e, i=i: e.matmul(b.banks[6][:], lhsT=b.bones[:], rhs=ksq[i][:], start=True, stop=True),
                       r=[("ksq", i), "c0"], w=[("b", 6)])
                self.rstd(rk[i][:], ("rk", i), 6, 1.0 / 64)
                for c in range(2):
                    ps_ = slice(c * 64, (c + 1) * 64)
                    self.O("dve", lambda e, pb=pb, i=i, h=h, c=c, ps_=ps_: e.scalar_tensor_tensor(
                        out=qT[c][ps_, h, :], in0=b.banks[pb][ps_, :], scalar=b.cv[ps_, CVOFF["qng"]:CVOFF["qng"] + 1],
                        in1=rk[i][ps_, :], op0=ALU.mult, op1=ALU.mult),
                        r=[("b", pb), ("rk", i), "cv", ("qT0z",), ("qT1z",)], w=[("qT", h, c)])
        nk = (ti + 1) * T
        nkb = nk // 128
        tR = [tmp[2], tmp[3]]
        tAB = [tmp[4], tmp[5]]
        tC, tD = tmp[6], tmp[7]
        osq = ksq[0]
        its = [(h, c, kb) for h in range(8) for c in range(2) for kb in range(nkb)]
        N = len(its)
        ptt_of = {}
        deferred = []

        def loads(h):
            i = h % 2
            s.dma("sp", lambda e: e.dma_start(out=KT[i][:, 0:nk], in_=b.kT_d[sq, h, :, 0:nk]),
                  r=[("kTd", sq, h, t2) for t2 in range(ti + 1)], w=[("KT", i)], key=("kl", i), fenced=True)
            s.dma("sp", lambda e: e.dma_start(
                out=VV[i][:, 0:nkb, :],
                in_=b.v_d[sq, 0:nk, h * 128:(h + 1) * 128].rearrange("(k p) d -> p k d", p=128)),
                r=[("vd", sq, t2, c2, h // 4) for t2 in range(ti + 1) for c2 in range(4)], w=[("VV", i)],
                key=("vl", i), fenced=True)

        def qk(n):
            h, c, kb = its[n]
            i = h % 2
            c0 = max(0, kb - ti * 4) * 128
            sb = n % 3
            self.O("pe", lambda e: e.matmul(b.banks[sb][:, c0:T], lhsT=KT[i][:, kb * 128:(kb + 1) * 128],
                                            rhs=qT[c][:, h, c0:T], start=True, stop=True),
                   r=[("KT", i), ("qT", h, c)], w=[("b", sb)])

        def post(n):
            h, c, kb = its[n]
            c0 = max(0, kb - ti * 4) * 128
            sb = n % 3
            pt = PT[n % 4]
            ptt = ("PT", n % 4)
            far = kb <= ti * 4 - 2
            if far:
                self.O("act", lambda e: e.activation(out=pt[:], in_=b.banks[sb][:], func=AF.Exp, bias=b.col("b31", h),
                                                     scale=0.125), r=[("b", sb), "cv"], w=[ptt])
            else:
                tm = tmp[n % 2]
                tmt = ("tmp", n % 2)
                for qb in range(c0 // 128, 4):
                    delta = ti * 4 + qb - kb
                    cl = slice(qb * 128, (qb + 1) * 128)
                    if delta <= 1:
                        self.O("dve", lambda e, cl=cl, delta=delta: e.scalar_tensor_tensor(
                            out=tm[:, cl], in0=b.banks[sb][:, cl], scalar=0.125,
                            in1=b.biast[:, h, delta * 128:(delta + 1) * 128], op0=ALU.mult, op1=ALU.add),
                            r=[("b", sb), "biast"], w=[tmt])
                    else:
                        self.O("dve", lambda e, cl=cl: e.tensor_scalar(
                            out=tm[:, cl], in0=b.banks[sb][:, cl], scalar1=0.125, scalar2=b.col("b31", h),
                            op0=ALU.mult, op1=ALU.add), r=[("b", sb), "cv"], w=[tmt])
                self.O("act", lambda e: e.activation(out=pt[:, c0:T], in_=tm[:, c0:T], func=AF.Exp), r=[tmt], w=[ptt])

        def pv(n):
            h, c, kb = its[n]
            i = h % 2
            c0 = max(0, kb - ti * 4) * 128
            pt = PT[n % 4]
            ptt = ("PT", n % 4)
            self.O("pe", lambda e: e.matmul(b.banks[3 + c][:, c0:T], lhsT=VV[i][:, kb, :], rhs=pt[:, c0:T],
                                            start=(kb == 0), stop=(kb == nkb - 1)),
                   r=[("VV", i), ptt], w=[("b", 3 + c)])
            self.O("pe", lambda e: e.matmul(b.banks[5 + c][:, c0:T], lhsT=b.ones[:], rhs=pt[:, c0:T],
                                            start=(kb == 0), stop=(kb == nkb - 1)),
                   r=["c0", ptt], w=[("b", 5 + c)])

        def epi1(h, c):
            self.O("act", lambda e: e.activation(out=tR[c][:], in_=b.banks[5 + c][:], func=AF.Ln), r=[("b", 5 + c)],
                   w=[("tR", c)])
            self.O("act", lambda e: e.activation(out=tR[c][:], in_=tR[c][:], func=AF.Exp, scale=-1.0), r=[("tR", c)],
                   w=[("tR", c)])
            self.O("dve", lambda e: e.tensor_tensor(out=tAB[c][:], in0=b.banks[3 + c][:], in1=tR[c][:], op=ALU.mult),
                   r=[("b", 3 + c), ("tR", c)], w=[("tAB", c)])
            if c == 1:
                self.O("dve", lambda e: e.scalar_tensor_tensor(out=tC[:], in0=tAB[1][:], scalar=b.smallc[:, 0:1],
                                                               in1=tAB[0][:], op0=ALU.mult, op1=ALU.add),
                       r=[("tAB", 0), ("tAB", 1), "neglam"], w=["tC"])
                self.O("act", lambda e: e.activation(out=osq[:], in_=tC[:], func=AF.Square), r=["tC"], w=[("ksq", 0)])

        def epi2(h):
            self.O("pe", lambda e: e.matmul(b.banks[7][:], lhsT=b.ones[:], rhs=osq[:], start=True, stop=True),
                   r=[("ksq", 0), "c0"], w=[("b", 7)])
            self.rstd(tD[:], "tD", 7, 1.0 / 128)
            self.O("dve", lambda e: e.scalar_tensor_tensor(out=oT[:, h, :], in0=tC[:], scalar=b.smallc[:, 1:2], in1=tD[:],
                                                           op0=ALU.mult, op1=ALU.mult),
                   r=["tC", "tD", "slg2"], w=[("oT", h)])

        def run_deferred(n):
            keep = []
            for at, fn in deferred:
                if at <= n:
                    fn()
                else:
                    keep.append((at, fn))
            deferred[:] = keep

        loads(0)
        if 1 < 8:
            loads(1)
        qk(0)
        if N > 1:
            qk(1)
        for n in range(N):
            h, c, kb = its[n]
            post(n)
            run_deferred(n)
            pv(n)
            if n + 2 < N:
                h2, c2, kb2 = its[n + 2]
                qk(n + 2)
            if kb == nkb - 1:
                deferred.append((n + 1, lambda h=h, c=c: epi1(h, c)))
                if c == 1:
                    deferred.append((n + 5, lambda h=h: epi2(h)))
                    if h + 2 < 8:
                        deferred.append((n, lambda h=h: loads(h + 2)))
        run_deferred(10 ** 9)
        for blk in range(2):
            slot = self.ws_next("ao", blk)
            for g in range(4):
                j = blk * 4 + g
                pb = self.proj(slot, 8, 512, g * 128, oT, "oT")
                self.O("dve", lambda e, j=j, pb=pb: e.tensor_tensor(out=b.x[:, j, :], in0=b.banks[pb][:], in1=b.x[:, j, :],
                                                                    op=ALU.add),
                       r=[("b", pb), ("x", j)], w=[("x", j)])

    def build(self):
        s = self.s
        ph = self.phases
        self.setup()
        self.casts()
        per_tile = []
        if "M" in ph:
            per_tile += [("in", i) for i in range(10)] + [("dt", 0)] + [("out", i) for i in range(4)]
        if "F" in ph:
            per_tile += [("up0", i) for i in range(11)] + [("dn0", i) for i in range(4)]
        if "K" in ph:
            per_tile += [("kv", i) for i in range(4)]
        if "A" in ph:
            per_tile += [("q", i) for i in range(2)] + [("ao", i) for i in range(2)]
        if "G" in ph:
            per_tile += [("up1", i) for i in range(11)] + [("dn1", i) for i in range(4)]
        self.ws_init(per_tile * (self.nseq * self.ntiles))
        for sq in range(self.nseq):
            for ti in range(self.ntiles):
                first = ti == 0
                self.load_x(sq, ti)
                if "M" in ph:
                    self.mixer(sq, ti)
                    s.barrier()
                if "F" in ph:
                    self.ffn(0, first)
                    s.barrier()
                if "K" in ph:
                    self.kvproj(sq, ti)
                    s.barrier()
                if "A" in ph:
                    self.attn(sq, ti)
                    s.barrier()
                if "G" in ph:
                    self.ffn(1, first)
                    s.barrier()
                self.store_x(sq, ti)
        s.emit()


def _t5_bucket_np(n):
    n = np.maximum(n, 0)
    max_exact = 16
    nf = np.maximum(n, 1).astype(np.float32)
    large = max_exact + (np.log(nf / max_exact) / math.log(128 / max_exact) * (32 - max_exact)).astype(np.int32)
    large = np.minimum(large, 31)
    return np.where(n < max_exact, n, large)


def _cols(v):
    return np.ascontiguousarray(v.reshape(-1, 128).T)


def prepare_shared(inp):
    f = np.float32
    sh = {}
    in_w = inp["ssm_in_w"][0]
    sh["w_in"] = _blockify(in_w[:, :5120], 4)
    sh["w_dt"] = np.ascontiguousarray(in_w[:, 5120:5152].reshape(8, 128, 32).transpose(1, 0, 2)).reshape(1, 128, 256)
    sh["w_out"] = _blockify(inp["ssm_out_w"][0], 2)
    for L in range(2):
        up = inp["ffn_up_w"][L]
        order = []
        for blk in range(11):
            order += [2 * blk, 2 * blk + 1, 22 + 2 * blk, 22 + 2 * blk + 1]
        idx = np.concatenate([np.arange(c * 128, (c + 1) * 128) for c in order])
        sh["w_up%d" % L] = _blockify(up[:, idx], 4)
        sh["w_dn%d" % L] = _blockify(inp["ffn_down_w"][L], 2)
    sh["w_kv"] = _blockify(inp["kv_w"], 4)
    sh["w_q"] = _blockify(inp["q_w"][0], 4)
    sh["w_ao"] = _blockify(inp["attn_out_w"][0], 4)
    cv = np.zeros((128, NCV), f)

    def put(name, arr):
        arr = np.asarray(arr, f)
        cv[:, CVOFF[name]:CVOFF[name] + arr.shape[1]] = arr

    put("g_ssm", _cols(inp["ssm_ln_g"][0]))
    put("g_ffn0", _cols(inp["ffn_ln_g"][0]))
    put("g_ffn1", _cols(inp["ffn_ln_g"][1]))
    put("g_kv", _cols(inp["kv_ln_g"]))
    put("g_attn", _cols(inp["attn_ln_g"][0]))
    mcw = inp["ssm_conv_w"][0]
    put("mcw", mcw.reshape(4, 24, 128).transpose(2, 1, 0).reshape(128, 96))
    put("mcb", _cols(inp["ssm_conv_b"][0]))
    for L in range(2):
        fw = inp["ffn_conv_w"][L]
        put("fcw%d" % L, fw.reshape(3, 44, 128).transpose(2, 1, 0).reshape(128, 132))
        put("fcb%d" % L, _cols(inp["ffn_conv_b"][L]))
    put("ng", _cols(inp["ssm_norm_g"][0]))
    put("dsk", _cols(np.repeat(inp["ssm_d"][0], 64)))
    put("kng", np.tile(inp["k_norm_g"], 2).reshape(128, 1))
    put("qng", np.tile(inp["q_norm_g"][0], 2).reshape(128, 1))
    put("slg", inp["subln_g"][0].reshape(128, 1))
    put("b31", np.tile(inp["rel_bias"][31][None, :], (128, 1)))
    put("dtb", np.tile(inp["ssm_dt_bias"][0][None, :], (128, 4)))
    put("alog", np.tile(inp["ssm_a_log"][0][None, :], (128, 4)))
    put("lamv", np.tile(inp["lam_vecs"][0].reshape(1, 256), (128, 1)))
    sh["cvec"] = cv
    k = np.arange(128)[:, None]
    j = np.arange(256)[None, :]
    bucket = _t5_bucket_np(j - k)
    rb = inp["rel_bias"]
    sh["biast"] = np.ascontiguousarray(rb[bucket].transpose(0, 2, 1)).astype(f)
    return sh


_NC_CACHE = {}


def kernel(**inputs):
    inp = {k: np.asarray(v) for k, v in inputs.items()}
    sh = prepare_shared(inp)
    x = inp["x"]
    if "nc" not in _NC_CACHE:
        nc = bass.Bass("TRN2", target_bir_lowering=False)
        Builder(nc).build()
        _NC_CACHE["nc"] = nc
    nc = _NC_CACHE["nc"]
    in_maps = []
    for c in range(NCORES):
        m = dict(sh)
        m["xT"] = np.ascontiguousarray(x[2 * c:2 * c + 2].transpose(0, 2, 1))
        in_maps.append(m)
    res = run_bass_kernel_spmd(nc, in_maps, core_ids=list(range(NCORES)))
    out = np.empty_like(x)
    for c in range(NCORES):
        out[2 * c:2 * c + 2] = res.results[c]["yT"].transpose(0, 2, 1)
    return out
```

```python
import math
from contextlib import ExitStack
import numpy as np
import concourse.bass as bass
import concourse.mybir as mybir
from concourse.bass_utils import run_bass_kernel_spmd

F32 = mybir.dt.float32
BF16 = mybir.dt.bfloat16
AF = mybir.ActivationFunctionType
ALU = mybir.AluOpType
AX = mybir.AxisListType

NCORES = 8
D = 1024
SEQ = 2048
T = 512
NT = SEQ // T
FFN = 2816
DIN = 2048
EPS = 1e-6
LAM_INIT = 0.8 - 0.6 * math.exp(-0.3 * 1)
WSLOT = 5632
NEG = -30000.0

ENGS = ("pe", "act", "dve", "pool", "sp")


class _Op:
    __slots__ = ("id", "eng", "fn", "deps", "dma", "key", "signaled", "count", "pos", "rawdeps")


class Sched:
    def __init__(self, nc):
        self.nc = nc
        self.ops = []
        self.eng_ops = {e: [] for e in ENGS}
        self.last_w = {}
        self.readers = {}
        self.key_count = {}
        self.bar_deps = set()
        self.bar_pending = set()
        self.fence_dmas = []

    def _add(self, eng, fn, r, w, dma, key, fenced):
        op = _Op()
        op.id = len(self.ops)
        op.eng = eng
        op.fn = fn
        op.dma = dma
        op.key = key
        op.signaled = False
        op.count = 0
        op.pos = len(self.eng_ops[eng])
        deps = set()
        raw = set()
        for t in r:
            lw = self.last_w.get(t)
            if lw is not None:
                deps.add(lw)
                raw.add(lw)
        for t in w:
            lw = self.last_w.get(t)
            if lw is not None:
                deps.add(lw)
            for rd in self.readers.get(t, ()):
                deps.add(rd)
        for t in r:
            self.readers.setdefault(t, []).append(op.id)
        for t in w:
            self.last_w[t] = op.id
            self.readers[t] = []
        if dma:
            if fenced:
                deps |= self.bar_deps
                self.fence_dmas.append(op.id)
        elif eng in self.bar_pending:
            deps |= self.bar_deps
            self.bar_pending.discard(eng)
        deps.discard(op.id)
        op.deps = deps
        op.rawdeps = raw
        if dma:
            self.key_count[key] = self.key_count.get(key, 0) + 1
            op.count = self.key_count[key]
        self.ops.append(op)
        self.eng_ops[eng].append(op)
        return op

    def op(self, eng, fn, r=(), w=()):
        return self._add(eng, fn, tuple(r), tuple(w), False, None, False)

    def dma(self, q, fn, r=(), w=(), key=None, fenced=False):
        assert key is not None
        return self._add(q, fn, tuple(r), tuple(w), True, key, fenced)

    def barrier(self):
        deps = set()
        for e in ("pe", "act", "dve", "pool"):
            for op in reversed(self.eng_ops[e]):
                if not op.dma:
                    deps.add(op.id)
                    break
        deps |= set(self.fence_dmas)
        self.fence_dmas = []
        self.bar_deps = deps
        self.bar_pending = {"pe", "act", "dve", "pool"}

    def emit(self):
        nc = self.nc
        ops = self.ops
        need = {}
        for op in ops:
            lst = []
            for d in op.deps:
                p = ops[d]
                if p.dma:
                    lst.append(p)
                elif p.eng != op.eng or op.dma:
                    lst.append(p)
                    p.signaled = True
                else:
                    if op.eng != "pe" and d in op.rawdeps and op.pos - p.pos <= 2:
                        lst.append(p)
                        p.signaled = True
            need[op.id] = lst
        cnt = {e: 0 for e in ENGS}
        for op in ops:
            if not op.dma and op.signaled:
                cnt[op.eng] += 1
                op.count = cnt[op.eng]
        keys = sorted(self.key_count.keys(), key=str)
        with ExitStack() as st:
            esem = {e: st.enter_context(nc.semaphore("s_" + e)) for e in ENGS}
            ksem = {k: st.enter_context(nc.semaphore("d_%d" % i)) for i, k in enumerate(keys)}
            block = st.enter_context(nc.Block())

            def body(ename):
                def _f(e):
                    waited = {}
                    for op in self.eng_ops[ename]:
                        for p in need[op.id]:
                            if p.dma:
                                sem, val, sk = ksem[p.key], 16 * p.count, ("k", p.key)
                            else:
                                sem, val, sk = esem[p.eng], p.count, ("e", p.eng)
                            if waited.get(sk, 0) >= val:
                                continue
                            waited[sk] = val
                            e.wait_ge(sem, val)
                        ins = op.fn(e)
                        if op.dma:
                            ins.then_inc(ksem[op.key], 16)
                        elif op.signaled:
                            ins.then_inc(esem[op.eng], 1)
                    if ename == "sp":
                        for k in keys:
                            e.wait_ge(ksem[k], 16 * self.key_count[k])
                return _f

            block.tensor(body("pe"))
            block.scalar(body("act"))
            block.vector(body("dve"))
            block.gpsimd(body("pool"))
            block.sync(body("sp"))


WSPEC = {
    "in": (8, 4, 10),
    "dt": (8, None, 1),
    "out": (16, 2, 4),
    "up0": (8, 4, 11),
    "dn0": (22, 2, 4),
    "kv": (8, 4, 4),
    "q": (8, 4, 2),
    "ao": (8, 4, 2),
    "up1": (8, 4, 11),
    "dn1": (22, 2, 4),
}
WORDER = ["in", "dt", "out", "up0", "dn0", "kv", "q", "ao", "up1", "dn1"]


def _welems(name):
    kc, g, nb = WSPEC[name]
    return kc * (32 if g is None else g * 128)


def _blockify(w, g):
    K, N = w.shape
    kc = K // 128
    nb = N // (g * 128)
    a = w.reshape(kc, 128, nb, g * 128).transpose(2, 1, 0, 3)
    return np.ascontiguousarray(a).reshape(nb, 128, kc * g * 128)


def _cv_layout():
    off = {}
    n = 0

    def add(name, cnt):
        nonlocal n
        off[name] = n
        n += cnt

    add("g_ssm", 8)
    add("g_ffn0", 8)
    add("g_ffn1", 8)
    add("g_kv", 8)
    add("g_attn", 8)
    add("mcw", 24 * 4)
    add("mcb", 24)
    add("fcw0", 44 * 3)
    add("fcb0", 44)
    add("fcw1", 44 * 3)
    add("fcb1", 44)
    add("ng", 16)
    add("dsk", 16)
    add("kng", 1)
    add("qng", 1)
    add("slg", 1)
    add("b31", 8)
    add("dtb", 128)
    add("alog", 128)
    add("lamv", 256)
    return off, n


CVOFF, NCV = _cv_layout()


class Builder:
    def __init__(self, nc, ntiles=NT, nseq=2, phases="MFKAG", dbg=None):
        self.nc = nc
        self.s = Sched(nc)
        self.ntiles = ntiles
        self.nseq = nseq
        self.phases = phases
        self.dbg = dbg
        self._uid = 0
        self.dram()
        self.sbuf()

    def dram(self):
        nc = self.nc
        self.xT = nc.dram_tensor("xT", [2, D, SEQ], F32, kind="ExternalInput").ap()
        self.yT = nc.dram_tensor("yT", [2, D, SEQ], F32, kind="ExternalOutput").ap()
        self.cv_d = nc.dram_tensor("cvec", [128, NCV], F32, kind="ExternalInput").ap()
        self.bt_d = nc.dram_tensor("biast", [128, 8, 256], F32, kind="ExternalInput").ap()
        self.w32 = {}
        self.wbf = {}
        for n in WORDER:
            kc, g, nb = WSPEC[n]
            el = _welems(n)
            self.w32[n] = nc.dram_tensor("w_" + n, [nb, 128, el], F32, kind="ExternalInput").ap()
            self.wbf[n] = nc.dram_tensor("wb_" + n, [nb, 128, el], BF16).ap()
        self.kT_d = nc.dram_tensor("kT_s", [2, 8, 128, SEQ], BF16).ap()
        self.v_d = nc.dram_tensor("v_s", [2, SEQ, D], BF16).ap()

    def sb(self, name, shape, dt):
        return self.nc.alloc_sbuf_tensor(name, list(shape), dt)

    def sbuf(self):
        nc = self.nc
        self.x = self.sb("x", [128, 8, T], F32)
        self.xn = self.sb("xn", [128, 8, T], BF16)
        self.xsq = self.sb("xsq", [128, 8, T], BF16)
        self.rs = self.sb("rs", [128, T], F32)
        self.wsl = [self.sb("wsl%d" % i, [128, WSLOT], BF16) for i in range(3)]
        self.cv = self.sb("cv", [128, NCV], F32)
        self.ident = self.sb("ident", [128, 128], BF16)
        self.ones = self.sb("ones", [128, 128], BF16)
        self.bones = self.sb("bones", [128, 128], BF16)
        self.onesf = self.sb("onesf", [128, 128], F32)
        self.tri = self.sb("tri", [128, 128], F32)
        self.trib = self.sb("trib", [128, 128], BF16)
        self.ustr = self.sb("ustr", [128, 128], BF16)
        self.maskneg = self.sb("maskneg", [128, 128], F32)
        self.biast = self.sb("biast_sb", [128, 8, 256], F32)
        self.smallc = self.sb("smallc", [128, 64], F32)
        self.Ab = self.sb("Ab", [128, 128], F32)
        self.dgs = [self.sb("dg%d" % i, [128, 128], BF16) for i in range(8)]
        self.uext = [self.sb("uext%d" % i, [128, T + 4], BF16) for i in range(3)]
        self.halo_m = self.sb("halo_m", [128, 24, 4], BF16)
        self.halo_f = [self.sb("halo_f%d" % l, [128, 44, 2], BF16) for l in range(2)]
        ARENA = 52000
        self.arena = self.sb("arena", [128, ARENA], BF16)
        self.S = self.sb("S", [128, DIN], F32)
        self.Sb = self.sb("Sb", [128, DIN], BF16)
        self.banks = [nc.alloc_psum_tensor("bank%d" % i, [128, 512], F32) for i in range(8)]

    def carve(self, off, nelem_bf16, dt=BF16):
        ap = self.arena[:, off:off + nelem_bf16]
        if dt == F32:
            ap = ap.bitcast(F32)
        return ap

    def uid(self):
        self._uid += 1
        return self._uid

    def dump(self, name, ap, rtoks, shape, dt=F32):
        if not self.dbg:
            return
        d = self.nc.dram_tensor("dbg_" + name, list(shape), dt, kind="ExternalOutput").ap()
        self.s.dma("sp", lambda e: e.dma_start(out=d, in_=ap), r=rtoks, key=("dbg", name))

    def O(self, eng, fn, r=(), w=()):
        return self.s.op(eng, fn, r, w)

    @staticmethod
    def bc_last(ap, n):
        return bass.AP(ap.tensor, ap.offset, [list(ap.ap[0]), list(ap.ap[1]), [0, n]])

    @staticmethod
    def bc_mid(ap, n):
        return bass.AP(ap.tensor, ap.offset, [list(ap.ap[0]), [0, n], list(ap.ap[1])])

    def col(self, name, i=0):
        o = CVOFF[name] + i
        return self.cv[:, o:o + 1]

    def setup(self):
        s = self.s
        b = self
        s.dma("sp", lambda e: e.dma_start(out=b.cv[:], in_=b.cv_d), w=["cv"], key="cvl")
        s.dma("sp", lambda e: e.dma_start(out=b.biast[:], in_=b.bt_d), w=["biast"], key="btl")
        P = "pool"

        def sel(t, pattern, op, base, cm, fill=0.0):
            return lambda e: e.affine_select(out=t[:], in_=t[:], pattern=pattern, compare_op=op, fill=fill,
                                             base=base, channel_multiplier=cm)
        for t in (b.ident, b.ones, b.onesf, b.tri, b.trib, b.ustr):
            self.O(P, lambda e, t=t: e.memset(t[:], 1.0), w=["c0"])
        self.O(P, lambda e: e.memset(b.maskneg[:], 0.0), w=["c0"])
        self.O(P, lambda e: e.memset(b.bones[:], 0.0), w=["c0"])
        self.O(P, lambda e: e.memset(b.bones[0:64, 0:64], 1.0), w=["c0"])
        self.O(P, lambda e: e.memset(b.bones[64:128, 64:128], 1.0), w=["c0"])
        self.O(P, sel(b.ident, [[1, 128]], ALU.is_equal, 0, -1), r=["c0"], w=["c1"])
        self.O(P, sel(b.tri, [[1, 128]], ALU.is_ge, 0, -1), r=["c0"], w=["c1"])
        self.O(P, sel(b.trib, [[1, 128]], ALU.is_ge, 0, -1), r=["c0"], w=["c1"])
        self.O(P, sel(b.ustr, [[-1, 128]], ALU.is_ge, -1, 1), r=["c0"], w=["c1"])
        self.O(P, sel(b.maskneg, [[1, 128]], ALU.is_ge, 0, -1, fill=NEG), r=["c0"], w=["c1"])
        self.O(P, lambda e: e.memset(b.halo_m[:], 0.0), w=["halo_m"])
        for l in range(2):
            self.O(P, lambda e, l=l: e.memset(b.halo_f[l][:], 0.0), w=[("halo_f", l)])
        for h in range(8):
            self.O("dve", lambda e, h=h: e.tensor_tensor(out=b.biast[:, h, 0:128], in0=b.biast[:, h, 0:128],
                                                         in1=b.maskneg[:], op=ALU.add),
                   r=["biast", "c1"], w=["biast"])
        ao = CVOFF["alog"]
        self.O("act", lambda e: e.activation(out=b.Ab[:], in_=b.cv[:, ao:ao + 128], func=AF.Exp), r=["cv"], w=["Ab"])
        self.O("dve", lambda e: e.tensor_scalar(out=b.Ab[:], in0=b.Ab[:], scalar1=-1.0, scalar2=None, op0=ALU.mult),
               r=["Ab"], w=["Ab"])
        lo = CVOFF["lamv"]
        sc = b.smallc
        self.tmp64 = self.sb("tmp64", [128, 128], F32)
        self.O("dve", lambda e: e.tensor_tensor(out=b.tmp64[:, 0:64], in0=b.cv[:, lo:lo + 64], in1=b.cv[:, lo + 64:lo + 128],
                                                op=ALU.mult), r=["cv"], w=["t64a"])
        self.O("dve", lambda e: e.tensor_tensor(out=b.tmp64[:, 64:128], in0=b.cv[:, lo + 128:lo + 192],
                                                in1=b.cv[:, lo + 192:lo + 256], op=ALU.mult), r=["cv"], w=["t64b"])
        self.O("dve", lambda e: e.reduce_sum(out=sc[:, 2:3], in_=b.tmp64[:, 0:64], axis=AX.X), r=["t64a"], w=["sc2"])
        self.O("dve", lambda e: e.reduce_sum(out=sc[:, 3:4], in_=b.tmp64[:, 64:128], axis=AX.X), r=["t64b"], w=["sc3"])
        self.O("act", lambda e: e.activation(out=sc[:, 4:6], in_=sc[:, 2:4], func=AF.Exp), r=["sc2", "sc3"], w=["sc4"])
        self.O("dve", lambda e: e.scalar_tensor_tensor(out=sc[:, 0:1], in0=sc[:, 5:6], scalar=-LAM_INIT, in1=sc[:, 4:5],
                                                       op0=ALU.add, op1=ALU.subtract), r=["sc4"], w=["neglam"])
        self.O("pool", lambda e: e.memset(sc[:, 6:7], EPS), w=["epsc"])
        self.O("dve", lambda e: e.tensor_scalar(out=sc[:, 1:2], in0=b.col("slg"), scalar1=(1.0 - LAM_INIT), scalar2=None,
                                                op0=ALU.mult), r=["cv"], w=["slg2"])

    def casts(self):
        s = self.s
        i = 0
        for n in WORDER:
            kc, g, nb = WSPEC[n]
            el = _welems(n)
            cb = 2048 if el % 2048 == 0 else (1024 if el % 1024 == 0 else (el if el <= 2048 else 1408))
            assert el % cb == 0, (n, el, cb)
            for blk in range(nb):
                s.dma("pool",
                      lambda e, n=n, blk=blk, cb=cb: e.dma_start(
                          out=self.wbf[n][blk].rearrange("p (a b) -> p a b", b=cb),
                          in_=self.w32[n][blk].rearrange("p (a b) -> p a b", b=cb)),
                      r=[], w=[("wd", n, blk), ("castslot", i % 8)], key=("cast", n, blk))
                i += 1

    def ws_init(self, seq):
        self.wseq = seq
        self.wi = 0
        self.wissued = 0

    def ws_issue(self, idx):
        n, blk = self.wseq[idx]
        slot = idx % 3
        el = _welems(n)
        self.s.dma("sp", lambda e: e.dma_start(out=self.wsl[slot][:, 0:el], in_=self.wbf[n][blk]),
                   r=[("wd", n, blk)], w=[("ws", slot)], key=("ws", slot))

    def ws_next(self, n, blk):
        idx = self.wi
        assert self.wseq[idx] == (n, blk), (self.wseq[idx], n, blk)
        while self.wissued <= min(idx + 2, len(self.wseq) - 1):
            self.ws_issue(self.wissued)
            self.wissued += 1
        self.wi += 1
        return idx % 3

    def norm(self, gname):
        b = self
        for kc in range(8):
            self.O("act", lambda e, kc=kc: e.activation(out=b.xsq[:, kc, :], in_=b.x[:, kc, :], func=AF.Square),
                   r=[("x", kc)], w=[("xsq", kc)])
        for kc in range(8):
            self.O("pe", lambda e, kc=kc: e.matmul(b.banks[6][:], lhsT=b.ones[:], rhs=b.xsq[:, kc, :],
                                                   start=(kc == 0), stop=(kc == 7)),
                   r=[("xsq", kc), "c0"], w=[("b", 6)])
        self.rstd(b.rs[:], "rs", 6, 1.0 / D)
        for kc in range(8):
            self.O("dve", lambda e, kc=kc: e.scalar_tensor_tensor(out=b.xn[:, kc, :], in0=b.x[:, kc, :],
                                                                  scalar=b.col(gname, kc), in1=b.rs[:],
                                                                  op0=ALU.mult, op1=ALU.mult),
                   r=[("x", kc), "rs", "cv"], w=[("xn", kc)])

    def rstd(self, out, tok, bank, scale):
        b = self
        self.O("act", lambda e: e.activation(out=out, in_=b.banks[bank][:], func=AF.Ln, bias=b.smallc[:, 6:7], scale=scale),
               r=[("b", bank), "epsc"], w=[tok])
        self.O("act", lambda e: e.activation(out=out, in_=out, func=AF.Exp, scale=-0.5), r=[tok], w=[tok])

    def pipeline(self, tasks):
        res = {}
        if tasks:
            res[0] = tasks[0][0]()
        for i in range(len(tasks)):
            if i + 1 < len(tasks):
                res[i + 1] = tasks[i + 1][0]()
            tasks[i][1](res.pop(i))

    def pbank(self):
        self._pb = (getattr(self, "_pb", -1) + 1) % 4
        return self._pb

    def cbank(self):
        self._cb = (getattr(self, "_cb", -1) + 1) % 2
        return 4 + self._cb

    def dg(self):
        self._dg = (getattr(self, "_dg", -1) + 1) % 8
        return self._dg

    def ue(self):
        self._ue = (getattr(self, "_ue", -1) + 1) % 3
        return self._ue

    def proj(self, slot, KC, stride, c0, act, acttok):
        b = self
        pb = self.pbank()
        for kc in range(KC):
            self.O("pe", lambda e, kc=kc: e.matmul(b.banks[pb][:], lhsT=b.wsl[slot][:, kc * stride + c0:kc * stride + c0 + 128],
                                                   rhs=act[:, kc, :], start=(kc == 0), stop=(kc == KC - 1)),
                   r=[("ws", slot), (acttok, kc)], w=[("b", pb)])
        return pb

    def conv(self, pb, K, halo, htok, cj, wname, first):
        b = self
        u = self.ue()
        ut = ("uext", u)
        H = K - 1
        ub = b.uext[u]
        if first:
            self.O("dve", lambda e: e.memset(ub[:, 0:H], 0.0), w=[ut])
        else:
            self.O("dve", lambda e: e.tensor_copy(out=ub[:, 0:H], in_=halo[:, cj, 0:H]), r=[htok], w=[ut])
        self.O("act", lambda e: e.activation(out=ub[:, H:H + T], in_=b.banks[pb][:], func=AF.Identity),
               r=[("b", pb)], w=[ut])
        self.O("dve", lambda e: e.tensor_copy(out=halo[:, cj, 0:H], in_=ub[:, T:T + H]), r=[ut], w=[htok])
        cb = self.cbank()
        for k in range(K):
            d = self.dg()
            self.O("dve", lambda e, d=d, k=k: e.tensor_scalar(out=b.dgs[d][:], in0=b.ident[:], scalar1=b.col(wname, cj * K + k),
                                                              scalar2=None, op0=ALU.mult),
                   r=["c1", "cv"], w=[("dg", d)])
            self.O("pe", lambda e, d=d, k=k: e.matmul(b.banks[cb][:], lhsT=b.dgs[d][:], rhs=ub[:, k:k + T],
                                                      start=(k == 0), stop=(k == K - 1)),
                   r=[("dg", d), ut], w=[("b", cb)])
        return cb

    def ffn(self, L, first):
        b = self
        self.norm("g_ffn%d" % L)
        h = self.carve(0, 22 * T).rearrange("p (a t) -> p a t", t=T)
        up = "up%d" % L
        dn = "dn%d" % L
        tasks = []
        slots = {}
        for blk in range(11):
            for g in range(4):
                is_gate = g < 2
                hj = 2 * blk + (g % 2)
                cj = hj if is_gate else 22 + hj

                def pj(blk=blk, g=g):
                    if g == 0:
                        slots[blk] = self.ws_next(up, blk)
                    return self.proj(slots[blk], 8, 512, g * 128, b.xn, "xn")

                def post(pb, is_gate=is_gate, hj=hj, cj=cj):
                    cb = self.conv(pb, 3, b.halo_f[L], ("halo_f", L, cj), cj, "fcw%d" % L, first)
                    bias = b.col("fcb%d" % L, cj)
                    if is_gate:
                        self.O("act", lambda e: e.activation(out=h[:, hj, :], in_=b.banks[cb][:], func=AF.Silu, bias=bias),
                               r=[("b", cb), "cv"], w=[("h", hj)])
                    else:
                        self.O("dve", lambda e: e.scalar_tensor_tensor(
                            out=h[:, hj, :], in0=b.banks[cb][:], scalar=bias, in1=h[:, hj, :], op0=ALU.add, op1=ALU.mult),
                            r=[("b", cb), "cv", ("h", hj)], w=[("h", hj)])
                tasks.append((pj, post))
        self.pipeline(tasks)
        for blk in range(4):
            slot = self.ws_next(dn, blk)
            for g in range(2):
                j = blk * 2 + g
                pb = self.pbank()
                for fc in range(22):
                    self.O("pe", lambda e, fc=fc, pb=pb, g=g, slot=slot: e.matmul(
                        b.banks[pb][:], lhsT=b.wsl[slot][:, fc * 256 + g * 128:fc * 256 + g * 128 + 128],
                        rhs=h[:, fc, :], start=(fc == 0), stop=(fc == 21)),
                        r=[("ws", slot), ("h", fc)], w=[("b", pb)])
                self.O("dve", lambda e, j=j, pb=pb: e.tensor_tensor(out=b.x[:, j, :], in0=b.banks[pb][:], in1=b.x[:, j, :],
                                                                    op=ALU.add),
                       r=[("b", pb), ("x", j)], w=[("x", j)])

    def load_x(self, sq, ti):
        b = self
        src = b.xT[sq].rearrange("(kc p) t -> p kc t", p=128)[:, :, ti * T:(ti + 1) * T]
        self.s.dma("sp", lambda e: e.dma_start(out=b.x[:], in_=src), w=[("x", kc) for kc in range(8)], key="xl")

    def store_x(self, sq, ti):
        b = self
        dst = b.yT[sq].rearrange("(kc p) t -> p kc t", p=128)[:, :, ti * T:(ti + 1) * T]
        self.s.dma("sp", lambda e: e.dma_start(out=dst, in_=b.x[:]), r=[("x", kc) for kc in range(8)], key="xs")

    def mixer(self, sq, ti):
        b = self
        first = ti == 0
        self.norm("g_ssm")
        v3 = lambda off, n: self.carve(off, n * T).rearrange("p (a t) -> p a t", t=T)
        zs = v3(0, 16)
        xbc = v3(8192, 24)
        yg = zs
        xdt = self.carve(28672, 2048)
        xdtw = self.carve(30720, 2048)
        btok = self.carve(32768, 512).rearrange("p (g n) -> p g n", n=128)
        Ah = [self.carve(20480 + i * 1024, 1024).rearrange("p (h l) -> p h l", l=128) for i in range(4)]
        E = [self.carve(24576 + i * 1024, 1024).rearrange("p (h l) -> p h l", l=128) for i in range(4)]
        M = [self.carve(33280 + i * 1024, 1024).rearrange("p (h l) -> p h l", l=128) for i in range(4)]
        cbm = self.carve(39424, 512).rearrange("p (g l) -> p g l", l=128)
        ytok = self.carve(39936, 2048)
        t1 = [self.carve(41984 + i * 1024, 1024, F32) for i in range(2)]
        tmpf = [self.carve(44032 + i * 256, 256, F32) for i in range(2)]
        sm = [self.carve(44544 + i * 256, 256, F32) for i in range(8)]
        tmpf2 = [self.carve(46592 + i * 2048, 2048, F32) for i in range(2)]
        dtr, dtt, aa, e1 = sm[0], sm[1], sm[2], sm[3]
        acs_sb, eacs, wdec, dec = sm[4], sm[5], sm[6], sm[7]
        bf = lambda i: b.banks[i][:].bitcast(BF16)
        if first:
            self.O("dve", lambda e: e.memset(b.S[:], 0.0), w=["S"])
            self.O("dve", lambda e: e.memset(b.Sb[:], 0.0), w=["Sb"])
        tasks = []
        slots = {}
        for blk in range(10):
            for g in range(4):
                j = blk * 4 + g

                def pj(blk=blk, g=g):
                    if g == 0:
                        slots[blk] = self.ws_next("in", blk)
                    return self.proj(slots[blk], 8, 512, g * 128, b.xn, "xn")

                def post(pb, j=j):
                    if j < 16:
                        self.O("act", lambda e: e.activation(out=zs[:, j, :], in_=b.banks[pb][:], func=AF.Silu),
                               r=[("b", pb)], w=[("zs", j)])
                    else:
                        cj = j - 16
                        cb = self.conv(pb, 4, b.halo_m, ("halo_m", cj), cj, "mcw", first)
                        self.O("act", lambda e: e.activation(out=xbc[:, cj, :], in_=b.banks[cb][:], func=AF.Silu,
                                                             bias=b.col("mcb", cj)),
                               r=[("b", cb), "cv"], w=[("xbc", cj)])
                tasks.append((pj, post))
        self.pipeline(tasks)
        slot = self.ws_next("dt", 0)
        for c in range(4):
            for kc in range(8):
                self.O("pe", lambda e, c=c, kc=kc, slot=slot: e.matmul(b.banks[7][:, c * 32:(c + 1) * 32],
                                                            lhsT=b.xn[:, kc, c * 128:(c + 1) * 128],
                                                            rhs=b.wsl[slot][:, kc * 32:(kc + 1) * 32],
                                                            start=(kc == 0), stop=(kc == 7)),
                       r=[("ws", slot), ("xn", kc)], w=[("b", 7)])
        do = CVOFF["dtb"]
        self.O("dve", lambda e: e.tensor_tensor(out=dtr[:], in0=b.banks[7][:, 0:128], in1=b.cv[:, do:do + 128], op=ALU.add),
               r=[("b", 7), "cv"], w=["dtr"])
        self.O("act", lambda e: e.activation(out=e1[:], in_=dtr[:], func=AF.Exp), r=["dtr"], w=["e1"])
        self.O("act", lambda e: e.activation(out=dtt[:], in_=e1[:], func=AF.Ln, bias=1.0), r=["e1"], w=["dt"])
        self.O("dve", lambda e: e.tensor_tensor(out=aa[:], in0=dtt[:], in1=b.Ab[:], op=ALU.mult), r=["dt", "Ab"], w=["aa"])
        self.dump("dtt", dtt[:], ["dt"], [128, 128])
        self.dump("dtr", dtr[:], ["dtr"], [128, 128])
        self.dump("x0", xbc[:, 0, :], [("xbc", 0)], [128, 512], BF16)
        self.dump("B0", xbc[:, 16, :], [("xbc", 16)], [128, 512], BF16)
        self.dump("z0", zs[:, 0, :], [("zs", 0)], [128, 512], BF16)
        def chunk(c):
            cs = slice(c * 128, (c + 1) * 128)
            a_c = aa[:, c * 32:(c + 1) * 32]
            dt_c = dtt[:, c * 32:(c + 1) * 32]
            self.O("pe", lambda e, a_c=a_c: e.matmul(b.banks[6][:, 0:32], lhsT=b.tri[:], rhs=a_c, start=True, stop=True),
                   r=["aa", "c1"], w=[("b", 6)])
            self.O("pe", lambda e, a_c=a_c: e.matmul(b.banks[6][:, 32:64], lhsT=b.onesf[:], rhs=a_c, start=True, stop=True),
                   r=["aa", "c0"], w=[("b", 6)])
            self.O("act", lambda e: e.activation(out=eacs[:, 0:32], in_=b.banks[6][:, 0:32], func=AF.Exp),
                   r=[("b", 6)], w=["eacs"])
            self.O("dve", lambda e: e.tensor_copy(out=acs_sb[:, 0:32], in_=b.banks[6][:, 0:32]), r=[("b", 6)], w=["acs"])
            self.O("dve", lambda e: e.tensor_tensor(out=wdec[:, 0:32], in0=b.banks[6][:, 32:64], in1=acs_sb[:, 0:32],
                                                    op=ALU.subtract), r=[("b", 6), "acs"], w=["wd"])
            self.O("act", lambda e: e.activation(out=wdec[:, 0:32], in_=wdec[:, 0:32], func=AF.Exp), r=["wd"], w=["wd"])
            self.O("act", lambda e: e.activation(out=dec[:, 0:32], in_=b.banks[6][:, 32:64], func=AF.Exp),
                   r=[("b", 6)], w=["dec"])
            for j in range(16):
                bk = 4 + j // 8
                self.O("pe", lambda e, j=j, bk=bk: e.transpose(out=bf(bk)[:, (j % 8) * 128:(j % 8 + 1) * 128],
                                                               in_=xbc[:, j, cs], identity=b.ident[:]),
                       r=[("xbc", j), "c1"], w=[("b", bk)])
            for half in range(2):
                self.O("dve", lambda e, half=half: e.tensor_tensor(
                    out=xdt[:, half * 1024:(half + 1) * 1024].rearrange("p (h d) -> p h d", d=64),
                    in0=bf(4 + half)[:, 0:1024].rearrange("p (h d) -> p h d", d=64),
                    in1=self.bc_last(dt_c[:, half * 16:(half + 1) * 16], 64), op=ALU.mult),
                    r=[("b", 4 + half), "dt"], w=[("xdt", 2 * half), ("xdt", 2 * half + 1)])
            self.O("pool", lambda e: e.tensor_tensor(out=xdtw[:].rearrange("p (h d) -> p h d", d=64),
                                                     in0=xdt[:].rearrange("p (h d) -> p h d", d=64),
                                                     in1=self.bc_last(wdec[:, 0:32], 64), op=ALU.mult),
                   r=[("xdt", q) for q in range(4)] + ["wd"], w=[("xdtw", q) for q in range(4)])
            for g in range(4):
                self.O("pe", lambda e, g=g: e.transpose(out=bf(4)[:, g * 128:(g + 1) * 128], in_=xbc[:, 16 + g, cs],
                                                        identity=b.ident[:]),
                       r=[("xbc", 16 + g), "c1"], w=[("b", 4)])
            self.O("act", lambda e: e.activation(out=btok[:].rearrange("p g n -> p (g n)"), in_=bf(4)[:, 0:512],
                                                 func=AF.Identity), r=[("b", 4)], w=["btok"])
            for g in range(4):
                self.O("pe", lambda e, g=g: e.matmul(b.banks[0][:, g * 128:(g + 1) * 128], lhsT=xbc[:, 16 + g, cs],
                                                     rhs=xbc[:, 20 + g, cs], start=True, stop=True),
                       r=[("xbc", 16 + g), ("xbc", 20 + g)], w=[("b", 0)])
            for g in range(4):
                self.O("dve", lambda e, g=g: e.tensor_tensor(out=cbm[:, g, :], in0=b.banks[0][:, g * 128:(g + 1) * 128],
                                                             in1=b.tri[:], op=ALU.mult),
                       r=[("b", 0), "c1"], w=[("cbm", g)])
            for g in range(4):
                self.O("pool", lambda e, g=g: e.tensor_tensor(out=Ah[g][:], in0=self.bc_mid(b.trib[:], 8),
                                                              in1=self.bc_last(a_c[:, g * 8:(g + 1) * 8], 128), op=ALU.mult),
                       r=["c1", "aa"], w=[("Ah", g)])
            for g in range(4):
                self.O("pool", lambda e, g=g: e.tensor_tensor(
                    out=b.S[:, g * 512:(g + 1) * 512].rearrange("p (h d) -> p h d", d=64),
                    in0=b.S[:, g * 512:(g + 1) * 512].rearrange("p (h d) -> p h d", d=64),
                    in1=self.bc_last(dec[:, g * 8:(g + 1) * 8], 64), op=ALU.mult),
                    r=["dec", ("S", g)], w=[("S", g)])
            for g in range(4):
                for i in range(2):
                    self.O("pe", lambda e, i=i, g=g: e.matmul(b.banks[1 + i][:], lhsT=b.ustr[:],
                                                              rhs=Ah[g][:, i * 4:(i + 1) * 4, :].rearrange("p h l -> p (h l)"),
                                                              start=True, stop=True),
                           r=[("Ah", g), "c1"], w=[("b", 1 + i)])
                    self.O("act", lambda e, i=i, g=g: e.activation(
                        out=E[g][:, i * 4:(i + 1) * 4, :].rearrange("p h l -> p (h l)"), in_=b.banks[1 + i][:], func=AF.Exp),
                        r=[("b", 1 + i)], w=[("E", g, i)])
            for g in range(4):
                for hh in range(8):
                    self.O("dve", lambda e, hh=hh, g=g: e.tensor_tensor(out=M[g][:, hh, :], in0=E[g][:, hh, :],
                                                                        in1=cbm[:, g, :], op=ALU.mult),
                           r=[("E", g, hh // 4), ("cbm", g)], w=[("M", g)])
            for g in range(4):
                ob = (3, 1)[g % 2]
                db = (7, 2)[g % 2]
                self.O("pe", lambda e, g=g, ob=ob: e.matmul(b.banks[ob][:], lhsT=xbc[:, 20 + g, cs],
                                                            rhs=b.Sb[:, g * 512:(g + 1) * 512], start=True, stop=True),
                       r=[("xbc", 20 + g), ("Sb", g)], w=[("b", ob)])
                for hh in range(8):
                    h = g * 8 + hh
                    self.O("pe", lambda e, hh=hh, h=h, g=g, db=db: e.matmul(b.banks[db][:, hh * 64:(hh + 1) * 64],
                                                                            lhsT=M[g][:, hh, :],
                                                                            rhs=xdt[:, h * 64:(h + 1) * 64], start=True, stop=True),
                           r=[("M", g), ("xdt", g)], w=[("b", db)])
                tt = t1[g % 2]
                self.O("dve", lambda e, g=g, tt=tt, ob=ob: e.tensor_tensor(
                    out=tt[:].rearrange("p (h d) -> p h d", d=64), in0=b.banks[ob][:].rearrange("p (h d) -> p h d", d=64),
                    in1=self.bc_last(eacs[:, g * 8:(g + 1) * 8], 64), op=ALU.mult),
                    r=[("b", ob), "eacs"], w=[("t1", g % 2)])
                self.O("dve", lambda e, g=g, tt=tt, db=db: e.tensor_tensor(out=ytok[:, g * 512:(g + 1) * 512], in0=b.banks[db][:],
                                                                           in1=tt[:], op=ALU.add),
                       r=[("b", db), ("t1", g % 2)], w=[("ytok", g)])
            for g in range(4):
                sbk = (0, 6)[g % 2]
                self.O("pe", lambda e, g=g, sbk=sbk: e.matmul(b.banks[sbk][:], lhsT=btok[:, g, :], rhs=xdtw[:, g * 512:(g + 1) * 512],
                                                              start=True, stop=True),
                       r=["btok", ("xdtw", g)] + [("cbm", q) for q in range(4)], w=[("b", sbk)])
                self.O("dve", lambda e, g=g, sbk=sbk: e.tensor_tensor(out=b.S[:, g * 512:(g + 1) * 512], in0=b.banks[sbk][:],
                                                                      in1=b.S[:, g * 512:(g + 1) * 512], op=ALU.add),
                       r=[("b", sbk), ("S", g)], w=[("S", g)])
                self.O("act", lambda e, g=g: e.activation(out=b.Sb[:, g * 512:(g + 1) * 512], in_=b.S[:, g * 512:(g + 1) * 512],
                                                          func=AF.Identity), r=[("S", g)], w=[("Sb", g)])
            if c == 0:
                self.dump("ytok0", ytok[:], [("ytok", q) for q in range(4)], [128, 2048], BF16)
                self.dump("eacs0", eacs[:, 0:32], ["eacs"], [128, 32])
                self.dump("acs0", acs_sb[:, 0:32], ["acs"], [128, 32])
                self.dump("S0", b.S[:], [("S", q) for q in range(4)], [128, 2048])
                self.dump("xdt0", xdt[:], [("xdt", q) for q in range(4)], [128, 2048], BF16)
            for j in range(16):
                bk = 4 + j // 8
                self.O("pe", lambda e, j=j, bk=bk: e.transpose(out=bf(bk)[:, (j % 8) * 128:(j % 8 + 1) * 128],
                                                               in_=ytok[:, j * 128:(j + 1) * 128], identity=b.ident[:]),
                       r=[("ytok", j // 4), "c1"], w=[("b", bk)])
            for half in range(2):
                j0 = half * 8
                tf = tmpf2[half]
                self.O("pool", lambda e, j0=j0, tf=tf: e.tensor_tensor(
                    out=tf[:].rearrange("p (j l) -> p j l", l=128), in0=xbc[:, j0:j0 + 8, cs],
                    in1=self.bc_last(b.cv[:, CVOFF["dsk"] + j0:CVOFF["dsk"] + j0 + 8], 128), op=ALU.mult),
                    r=[("xbc", j0 + q) for q in range(8)] + ["cv"], w=[("tmpf", half)])
                self.O("dve", lambda e, half=half, tf=tf: e.tensor_tensor(
                    out=tf[:].rearrange("p (j l) -> p j l", l=128), in0=bf(4 + half)[:, 0:1024].rearrange("p (j l) -> p j l", l=128),
                    in1=tf[:].rearrange("p (j l) -> p j l", l=128), op=ALU.add),
                    r=[("b", 4 + half), ("tmpf", half)], w=[("tmpf", half)])
                self.O("pool", lambda e, j0=j0, tf=tf: e.tensor_tensor(
                    out=yg[:, j0:j0 + 8, cs], in0=tf[:].rearrange("p (j l) -> p j l", l=128), in1=zs[:, j0:j0 + 8, cs],
                    op=ALU.mult),
                    r=[("tmpf", half)] + [("zs", j0 + q) for q in range(8)], w=[("zs", j0 + q) for q in range(8)])
        for c in range(4):
            chunk(c)
        for g in range(4):
            for q in range(4):
                j = g * 4 + q
                self.O("act", lambda e, j=j, q=q: e.activation(out=b.xsq[:, q, :], in_=yg[:, j, :], func=AF.Square),
                       r=[("zs", j)], w=[("xsq", q)])
            for q in range(4):
                self.O("pe", lambda e, q=q: e.matmul(b.banks[6][:], lhsT=b.ones[:], rhs=b.xsq[:, q, :], start=(q == 0),
                                                     stop=(q == 3)), r=[("xsq", q), "c0"], w=[("b", 6)])
            self.rstd(b.rs[:], "rs", 6, 1.0 / 512)
            for q in range(4):
                j = g * 4 + q
                self.O("dve", lambda e, j=j: e.scalar_tensor_tensor(out=yg[:, j, :], in0=yg[:, j, :], scalar=b.col("ng", j),
                                                                    in1=b.rs[:], op0=ALU.mult, op1=ALU.mult),
                       r=[("zs", j), "rs", "cv"], w=[("zs", j)])
        for blk in range(4):
            slot = self.ws_next("out", blk)
            for g2 in range(2):
                j = blk * 2 + g2
                pb = self.proj(slot, 16, 256, g2 * 128, yg, "zs")
                self.O("dve", lambda e, j=j, pb=pb: e.tensor_tensor(out=b.x[:, j, :], in0=b.banks[pb][:], in1=b.x[:, j, :],
                                                                    op=ALU.add),
                       r=[("b", pb), ("x", j)], w=[("x", j)])

    def headnorm(self, pb, gcol, outap, outtok, bufs, i):
        b = self
        ksq, rk = bufs
        self.O("act", lambda e: e.activation(out=ksq[i][:], in_=b.banks[pb][:], func=AF.Square), r=[("b", pb)], w=[("ksq", i)])
        self.O("pe", lambda e: e.matmul(b.banks[6][:], lhsT=b.bones[:], rhs=ksq[i][:], start=True, stop=True),
               r=[("ksq", i), "c0"], w=[("b", 6)])
        self.rstd(rk[i][:], ("rk", i), 6, 1.0 / 64)
        self.O("dve", lambda e: e.scalar_tensor_tensor(out=outap, in0=b.banks[pb][:], scalar=gcol, in1=rk[i][:],
                                                       op0=ALU.mult, op1=ALU.mult),
               r=[("b", pb), ("rk", i), "cv"], w=[outtok])

    def kvproj(self, sq, ti):
        b = self
        s = self.s
        self.norm("g_kv")
        kt = [self.carve(i * 512, 512) for i in range(2)]
        ksq = [self.carve(1024 + i * 512, 512) for i in range(2)]
        rk = [self.carve(2048 + i * 1024, 1024, F32) for i in range(2)]
        vt = [self.carve(4096 + i * 512, 512) for i in range(2)]
        tasks = []
        slots = {}
        for blk in range(2):
            for g in range(4):
                h = blk * 4 + g

                def pj(blk=blk, g=g):
                    if g == 0:
                        slots[blk] = self.ws_next("kv", blk)
                    return self.proj(slots[blk], 8, 512, g * 128, b.xn, "xn")

                def post(pb, h=h):
                    i = h % 2
                    self.headnorm(pb, b.col("kng"), kt[i][:], ("kt", i), (ksq, rk), i)
                    s.dma("sp", lambda e: e.dma_start(out=b.kT_d[sq, h, :, ti * T:(ti + 1) * T], in_=kt[i][:]),
                          r=[("kt", i)], w=[("kTd", sq, h, ti)], key=("kst", i), fenced=True)
                tasks.append((pj, post))
        self.pipeline(tasks)
        for blk in range(2, 4):
            slot = self.ws_next("kv", blk)
            nh = blk - 2
            for c in range(4):
                pb = self.pbank()
                i = c % 2
                for kc in range(8):
                    self.O("pe", lambda e, kc=kc, c=c, pb=pb, slot=slot: e.matmul(b.banks[pb][:], lhsT=b.xn[:, kc, c * 128:(c + 1) * 128],
                                                                       rhs=b.wsl[slot][:, kc * 512:(kc + 1) * 512],
                                                                       start=(kc == 0), stop=(kc == 7)),
                           r=[("ws", slot), ("xn", kc)], w=[("b", pb)])
                self.O("act", lambda e, i=i, pb=pb: e.activation(out=vt[i][:], in_=b.banks[pb][:], func=AF.Identity),
                       r=[("b", pb)], w=[("vt", i)])
                s.dma("sp", lambda e, c=c, i=i, nh=nh: e.dma_start(
                    out=b.v_d[sq, ti * T + c * 128:ti * T + (c + 1) * 128, nh * 512:(nh + 1) * 512], in_=vt[i][:]),
                    r=[("vt", i)], w=[("vd", sq, ti, c, nh)], key=("vst", i), fenced=True)

    def attn(self, sq, ti):
        b = self
        s = self.s
        self.norm("g_attn")
        v3 = lambda off: self.carve(off, 8 * T).rearrange("p (a t) -> p a t", t=T)
        qT = [v3(0), v3(4096)]
        oT = v3(8192)
        KT = [self.carve(12288 + i * 2048, 2048) for i in range(2)]
        VV = [self.carve(16384 + i * 2048, 2048).rearrange("p (k d) -> p k d", d=128) for i in range(2)]
        PT = [self.carve(20480 + i * 512, 512) for i in range(4)]
        tmp = [self.carve(22528 + i * 1024, 1024, F32) for i in range(10)]
        ksq = [self.carve(32768 + i * 512, 512) for i in range(2)]
        rk = [tmp[8], tmp[9]]
        self.O("pool", lambda e: e.memset(qT[0][64:128, :, :], 0.0), w=[("qT0z",)])
        self.O("pool", lambda e: e.memset(qT[1][0:64, :, :], 0.0), w=[("qT1z",)])
        tasks = []
        slots = {}
        for blk in range(2):
            for g in range(4):
                h = blk * 4 + g

                def pj(blk=blk, g=g):
                    if g == 0:
                        slots[blk] = self.ws_next("q", blk)
                    return self.proj(slots[blk], 8, 512, g * 128, b.xn, "xn")

                def post(pb, h=h):
                    i = h % 2
                    self.O("act", lambda e: e.activation(out=ksq[i][:], in_=b.banks[pb][:], func=AF.Square),
                           r=[("b", pb)], w=[("ksq", i)])
                    self.O("pe", lambda e: e.matmul(b.banks[6][:], lhsT=b.bones[:], rhs=ksq[i][:], start=True, stop=True),
                           r=[("ksq", i), "c0"], w=[("b", 6)])
                    self.rstd(rk[i][:], ("rk", i), 6, 1.0 / 64)
                    for c in range(2):
                        ps_ = slice(c * 64, (c + 1) * 64)
                        self.O("dve", lambda e, c=c, ps_=ps_: e.scalar_tensor_tensor(
                            out=qT[c][ps_, h, :], in0=b.banks[pb][ps_, :], scalar=b.cv[ps_, CVOFF["qng"]:CVOFF["qng"] + 1],
                            in1=rk[i][ps_, :], op0=ALU.mult, op1=ALU.mult),
                            r=[("b", pb), ("rk", i), "cv", ("qT0z",), ("qT1z",)], w=[("qT", h, c)])
                tasks.append((pj, post))
        self.pipeline(tasks)
        nk = (ti + 1) * T
        nkb = nk // 128
        tR = [tmp[2], tmp[3]]
        tAB = [tmp[4], tmp[5]]
        tC, tD = tmp[6], tmp[7]
        osq = ksq[0]
        its = [(h, c, kb) for h in range(8) for c in range(2) for kb in range(nkb)]
        N = len(its)
        ptt_of = {}
        deferred = []

        def loads(h):
            i = h % 2
            s.dma("sp", lambda e: e.dma_start(out=KT[i][:, 0:nk], in_=b.kT_d[sq, h, :, 0:nk]),
                  r=[("kTd", sq, h, t2) for t2 in range(ti + 1)], w=[("KT", i)], key=("kl", i), fenced=True)
            s.dma("sp", lambda e: e.dma_start(
                out=VV[i][:, 0:nkb, :],
                in_=b.v_d[sq, 0:nk, h * 128:(h + 1) * 128].rearrange("(k p) d -> p k d", p=128)),
                r=[("vd", sq, t2, c2, h // 4) for t2 in range(ti + 1) for c2 in range(4)], w=[("VV", i)],
                key=("vl", i), fenced=True)

        def qk(n):
            h, c, kb = its[n]
            i = h % 2
            c0 = max(0, kb - ti * 4) * 128
            sb = n % 3
            self.O("pe", lambda e: e.matmul(b.banks[sb][:, c0:T], lhsT=KT[i][:, kb * 128:(kb + 1) * 128],
                                            rhs=qT[c][:, h, c0:T], start=True, stop=True),
                   r=[("KT", i), ("qT", h, c)], w=[("b", sb)])

        def post(n):
            h, c, kb = its[n]
            c0 = max(0, kb - ti * 4) * 128
            sb = n % 3
            pt = PT[n % 4]
            ptt = ("PT", n % 4)
            far = kb <= ti * 4 - 2
            if far:
                self.O("act", lambda e: e.activation(out=pt[:], in_=b.banks[sb][:], func=AF.Exp, bias=b.col("b31", h),
                                                     scale=0.125), r=[("b", sb), "cv"], w=[ptt])
            else:
                tm = tmp[n % 2]
                tmt = ("tmp", n % 2)
                for qb in range(c0 // 128, 4):
                    delta = ti * 4 + qb - kb
                    cl = slice(qb * 128, (qb + 1) * 128)
                    if delta <= 1:
                        self.O("dve", lambda e, cl=cl, delta=delta: e.scalar_tensor_tensor(
                            out=tm[:, cl], in0=b.banks[sb][:, cl], scalar=0.125,
                            in1=b.biast[:, h, delta * 128:(delta + 1) * 128], op0=ALU.mult, op1=ALU.add),
                            r=[("b", sb), "biast"], w=[tmt])
                    else:
                        self.O("dve", lambda e, cl=cl: e.tensor_scalar(
                            out=tm[:, cl], in0=b.banks[sb][:, cl], scalar1=0.125, scalar2=b.col("b31", h),
                            op0=ALU.mult, op1=ALU.add), r=[("b", sb), "cv"], w=[tmt])
                self.O("act", lambda e: e.activation(out=pt[:, c0:T], in_=tm[:, c0:T], func=AF.Exp), r=[tmt], w=[ptt])

        def pv(n):
            h, c, kb = its[n]
            i = h % 2
            c0 = max(0, kb - ti * 4) * 128
            pt = PT[n % 4]
            ptt = ("PT", n % 4)
            self.O("pe", lambda e: e.matmul(b.banks[3 + c][:, c0:T], lhsT=VV[i][:, kb, :], rhs=pt[:, c0:T],
                                            start=(kb == 0), stop=(kb == nkb - 1)),
                   r=[("VV", i), ptt], w=[("b", 3 + c)])
            self.O("pe", lambda e: e.matmul(b.banks[5 + c][:, c0:T], lhsT=b.ones[:], rhs=pt[:, c0:T],
                                            start=(kb == 0), stop=(kb == nkb - 1)),
                   r=["c0", ptt], w=[("b", 5 + c)])

        def epi1(h, c):
            self.O("act", lambda e: e.activation(out=tR[c][:], in_=b.banks[5 + c][:], func=AF.Ln), r=[("b", 5 + c)],
                   w=[("tR", c)])
            self.O("act", lambda e: e.activation(out=tR[c][:], in_=tR[c][:], func=AF.Exp, scale=-1.0), r=[("tR", c)],
                   w=[("tR", c)])
            self.O("dve", lambda e: e.tensor_tensor(out=tAB[c][:], in0=b.banks[3 + c][:], in1=tR[c][:], op=ALU.mult),
                   r=[("b", 3 + c), ("tR", c)], w=[("tAB", c)])
            if c == 1:
                self.O("dve", lambda e: e.scalar_tensor_tensor(out=tC[:], in0=tAB[1][:], scalar=b.smallc[:, 0:1],
                                                               in1=tAB[0][:], op0=ALU.mult, op1=ALU.add),
                       r=[("tAB", 0), ("tAB", 1), "neglam"], w=["tC"])
                self.O("act", lambda e: e.activation(out=osq[:], in_=tC[:], func=AF.Square), r=["tC"], w=[("ksq", 0)])

        def epi2(h):
            self.O("pe", lambda e: e.matmul(b.banks[7][:], lhsT=b.ones[:], rhs=osq[:], start=True, stop=True),
                   r=[("ksq", 0), "c0"], w=[("b", 7)])
            self.rstd(tD[:], "tD", 7, 1.0 / 128)
            self.O("dve", lambda e: e.scalar_tensor_tensor(out=oT[:, h, :], in0=tC[:], scalar=b.smallc[:, 1:2], in1=tD[:],
                                                           op0=ALU.mult, op1=ALU.mult),
                   r=["tC", "tD", "slg2"], w=[("oT", h)])

        def run_deferred(n):
            keep = []
            for at, fn in deferred:
                if at <= n:
                    fn()
                else:
                    keep.append((at, fn))
            deferred[:] = keep

        loads(0)
        if 1 < 8:
            loads(1)
        qk(0)
        if N > 1:
            qk(1)
        for n in range(N):
            h, c, kb = its[n]
            post(n)
            run_deferred(n)
            pv(n)
            if n + 2 < N:
                h2, c2, kb2 = its[n + 2]
                qk(n + 2)
            if kb == nkb - 1:
                deferred.append((n + 1, lambda h=h, c=c: epi1(h, c)))
                if c == 1:
                    deferred.append((n + 5, lambda h=h: epi2(h)))
                    if h + 2 < 8:
                        deferred.append((n, lambda h=h: loads(h + 2)))
        run_deferred(10 ** 9)
        for blk in range(2):
            slot = self.ws_next("ao", blk)
            for g in range(4):
                j = blk * 4 + g
                pb = self.proj(slot, 8, 512, g * 128, oT, "oT")
                self.O("dve", lambda e, j=j, pb=pb: e.tensor_tensor(out=b.x[:, j, :], in0=b.banks[pb][:], in1=b.x[:, j, :],
                                                                    op=ALU.add),
                       r=[("b", pb), ("x", j)], w=[("x", j)])

    def build(self):
        s = self.s
        ph = self.phases
        self.setup()
        self.casts()
        per_tile = []
        if "M" in ph:
            per_tile += [("in", i) for i in range(10)] + [("dt", 0)] + [("out", i) for i in range(4)]
        if "F" in ph:
            per_tile += [("up0", i) for i in range(11)] + [("dn0", i) for i in range(4)]
        if "K" in ph:
            per_tile += [("kv", i) for i in range(4)]
        if "A" in ph:
            per_tile += [("q", i) for i in range(2)] + [("ao", i) for i in range(2)]
        if "G" in ph:
            per_tile += [("up1", i) for i in range(11)] + [("dn1", i) for i in range(4)]
        self.ws_init(per_tile * (self.nseq * self.ntiles))
        for sq in range(self.nseq):
            for ti in range(self.ntiles):
                first = ti == 0
                self.load_x(sq, ti)
                if "M" in ph:
                    self.mixer(sq, ti)
                    s.barrier()
                if "F" in ph:
                    self.ffn(0, first)
                    s.barrier()
                if "K" in ph:
                    self.kvproj(sq, ti)
                    s.barrier()
                if "A" in ph:
                    self.attn(sq, ti)
                    s.barrier()
                if "G" in ph:
                    self.ffn(1, first)
                    s.barrier()
                self.store_x(sq, ti)
        s.emit()


def _t5_bucket_np(n):
    n = np.maximum(n, 0)
    max_exact = 16
    nf = np.maximum(n, 1).astype(np.float32)
    large = max_exact + (np.log(nf / max_exact) / math.log(128 / max_exact) * (32 - max_exact)).astype(np.int32)
    large = np.minimum(large, 31)
    return np.where(n < max_exact, n, large)


def _cols(v):
    return np.ascontiguousarray(v.reshape(-1, 128).T)


def prepare_shared(inp):
    f = np.float32
    sh = {}
    in_w = inp["ssm_in_w"][0]
    sh["w_in"] = _blockify(in_w[:, :5120], 4)
    sh["w_dt"] = np.ascontiguousarray(in_w[:, 5120:5152].reshape(8, 128, 32).transpose(1, 0, 2)).reshape(1, 128, 256)
    sh["w_out"] = _blockify(inp["ssm_out_w"][0], 2)
    for L in range(2):
        up = inp["ffn_up_w"][L]
        order = []
        for blk in range(11):
            order += [2 * blk, 2 * blk + 1, 22 + 2 * blk, 22 + 2 * blk + 1]
        idx = np.concatenate([np.arange(c * 128, (c + 1) * 128) for c in order])
        sh["w_up%d" % L] = _blockify(up[:, idx], 4)
        sh["w_dn%d" % L] = _blockify(inp["ffn_down_w"][L], 2)
    sh["w_kv"] = _blockify(inp["kv_w"], 4)
    sh["w_q"] = _blockify(inp["q_w"][0], 4)
    sh["w_ao"] = _blockify(inp["attn_out_w"][0], 4)
    cv = np.zeros((128, NCV), f)

    def put(name, arr):
        arr = np.asarray(arr, f)
        cv[:, CVOFF[name]:CVOFF[name] + arr.shape[1]] = arr

    put("g_ssm", _cols(inp["ssm_ln_g"][0]))
    put("g_ffn0", _cols(inp["ffn_ln_g"][0]))
    put("g_ffn1", _cols(inp["ffn_ln_g"][1]))
    put("g_kv", _cols(inp["kv_ln_g"]))
    put("g_attn", _cols(inp["attn_ln_g"][0]))
    mcw = inp["ssm_conv_w"][0]
    put("mcw", mcw.reshape(4, 24, 128).transpose(2, 1, 0).reshape(128, 96))
    put("mcb", _cols(inp["ssm_conv_b"][0]))
    for L in range(2):
        fw = inp["ffn_conv_w"][L]
        put("fcw%d" % L, fw.reshape(3, 44, 128).transpose(2, 1, 0).reshape(128, 132))
        put("fcb%d" % L, _cols(inp["ffn_conv_b"][L]))
    put("ng", _cols(inp["ssm_norm_g"][0]))
    put("dsk", _cols(np.repeat(inp["ssm_d"][0], 64)))
    put("kng", np.tile(inp["k_norm_g"], 2).reshape(128, 1))
    put("qng", np.tile(inp["q_norm_g"][0], 2).reshape(128, 1))
    put("slg", inp["subln_g"][0].reshape(128, 1))
    put("b31", np.tile(inp["rel_bias"][31][None, :], (128, 1)))
    put("dtb", np.tile(inp["ssm_dt_bias"][0][None, :], (128, 4)))
    put("alog", np.tile(inp["ssm_a_log"][0][None, :], (128, 4)))
    put("lamv", np.tile(inp["lam_vecs"][0].reshape(1, 256), (128, 1)))
    sh["cvec"] = cv
    k = np.arange(128)[:, None]
    j = np.arange(256)[None, :]
    bucket = _t5_bucket_np(j - k)
    rb = inp["rel_bias"]
    sh["biast"] = np.ascontiguousarray(rb[bucket].transpose(0, 2, 1)).astype(f)
    return sh


_NC_CACHE = {}


def kernel(**inputs):
    inp = {k: np.asarray(v) for k, v in inputs.items()}
    sh = prepare_shared(inp)
    x = inp["x"]
    if "nc" not in _NC_CACHE:
        nc = bass.Bass("TRN2", target_bir_lowering=False)
        Builder(nc).build()
        _NC_CACHE["nc"] = nc
    nc = _NC_CACHE["nc"]
    in_maps = []
    for c in range(NCORES):
        m = dict(sh)
        m["xT"] = np.ascontiguousarray(x[2 * c:2 * c + 2].transpose(0, 2, 1))
        in_maps.append(m)
    res = run_bass_kernel_spmd(nc, in_maps, core_ids=list(range(NCORES)))
    out = np.empty_like(x)
    for c in range(NCORES):
        out[2 * c:2 * c + 2] = res.results[c]["yT"].transpose(0, 2, 1)
    return out
```

```python
import math
from contextlib import ExitStack
import numpy as np
import concourse.bass as bass
import concourse.mybir as mybir
from concourse.bass_utils import run_bass_kernel_spmd

F32 = mybir.dt.float32
BF16 = mybir.dt.bfloat16
AF = mybir.ActivationFunctionType
ALU = mybir.AluOpType
AX = mybir.AxisListType

NCORES = 8
D = 1024
SEQ = 2048
T = 512
NT = SEQ // T
FFN = 2816
DIN = 2048
EPS = 1e-6
LAM_INIT = 0.8 - 0.6 * math.exp(-0.3 * 1)
WSLOT = 5632
NEG = -30000.0

ENGS = ("pe", "act", "dve", "pool", "sp")


class _Op:
    __slots__ = ("id", "eng", "fn", "deps", "dma", "key", "signaled", "count", "pos", "rawdeps")


class Sched:
    def __init__(self, nc):
        self.nc = nc
        self.ops = []
        self.eng_ops = {e: [] for e in ENGS}
        self.last_w = {}
        self.readers = {}
        self.key_count = {}
        self.bar_deps = set()
        self.bar_pending = set()
        self.fence_dmas = []

    def _add(self, eng, fn, r, w, dma, key, fenced):
        op = _Op()
        op.id = len(self.ops)
        op.eng = eng
        op.fn = fn
        op.dma = dma
        op.key = key
        op.signaled = False
        op.count = 0
        op.pos = len(self.eng_ops[eng])
        deps = set()
        raw = set()
        for t in r:
            lw = self.last_w.get(t)
            if lw is not None:
                deps.add(lw)
                raw.add(lw)
        for t in w:
            lw = self.last_w.get(t)
            if lw is not None:
                deps.add(lw)
            for rd in self.readers.get(t, ()):
                deps.add(rd)
        for t in r:
            self.readers.setdefault(t, []).append(op.id)
        for t in w:
            self.last_w[t] = op.id
            self.readers[t] = []
        if dma:
            if fenced:
                deps |= self.bar_deps
                self.fence_dmas.append(op.id)
        elif eng in self.bar_pending:
            deps |= self.bar_deps
            self.bar_pending.discard(eng)
        deps.discard(op.id)
        op.deps = deps
        op.rawdeps = raw
        if dma:
            self.key_count[key] = self.key_count.get(key, 0) + 1
            op.count = self.key_count[key]
        self.ops.append(op)
        self.eng_ops[eng].append(op)
        return op

    def op(self, eng, fn, r=(), w=()):
        return self._add(eng, fn, tuple(r), tuple(w), False, None, False)

    def dma(self, q, fn, r=(), w=(), key=None, fenced=False):
        assert key is not None
        return self._add(q, fn, tuple(r), tuple(w), True, key, fenced)

    def barrier(self):
        deps = set()
        for e in ("pe", "act", "dve", "pool"):
            for op in reversed(self.eng_ops[e]):
                if not op.dma:
                    deps.add(op.id)
                    break
        deps |= set(self.fence_dmas)
        self.fence_dmas = []
        self.bar_deps = deps
        self.bar_pending = {"pe", "act", "dve", "pool"}

    def emit(self):
        nc = self.nc
        ops = self.ops
        need = {}
        for op in ops:
            lst = []
            for d in op.deps:
                p = ops[d]
                if p.dma:
                    lst.append(p)
                elif p.eng != op.eng or op.dma:
                    lst.append(p)
                    p.signaled = True
                else:
                    if op.eng != "pe" and d in op.rawdeps and op.pos - p.pos <= 2:
                        lst.append(p)
                        p.signaled = True
            need[op.id] = lst
        cnt = {e: 0 for e in ENGS}
        for op in ops:
            if not op.dma and op.signaled:
                cnt[op.eng] += 1
                op.count = cnt[op.eng]
        keys = sorted(self.key_count.keys(), key=str)
        with ExitStack() as st:
            esem = {e: st.enter_context(nc.semaphore("s_" + e)) for e in ENGS}
            ksem = {k: st.enter_context(nc.semaphore("d_%d" % i)) for i, k in enumerate(keys)}
            block = st.enter_context(nc.Block())

            def body(ename):
                def _f(e):
                    waited = {}
                    for op in self.eng_ops[ename]:
                        for p in need[op.id]:
                            if p.dma:
                                sem, val, sk = ksem[p.key], 16 * p.count, ("k", p.key)
                            else:
                                sem, val, sk = esem[p.eng], p.count, ("e", p.eng)
                            if waited.get(sk, 0) >= val:
                                continue
                            waited[sk] = val
                            e.wait_ge(sem, val)
                        ins = op.fn(e)
                        if op.dma:
                            ins.then_inc(ksem[op.key], 16)
                        elif op.signaled:
                            ins.then_inc(esem[op.eng], 1)
                    if ename == "sp":
                        for k in keys:
                            e.wait_ge(ksem[k], 16 * self.key_count[k])
                return _f

            block.tensor(body("pe"))
            block.scalar(body("act"))
            block.vector(body("dve"))
            block.gpsimd(body("pool"))
            block.sync(body("sp"))


WSPEC = {
    "in": (8, 4, 10),
    "dt": (8, None, 1),
    "out": (16, 2, 4),
    "up0": (8, 4, 11),
    "dn0": (22, 2, 4),
    "kv": (8, 4, 4),
    "q": (8, 4, 2),
    "ao": (8, 4, 2),
    "up1": (8, 4, 11),
    "dn1": (22, 2, 4),
}
WORDER = ["in", "dt", "out", "up0", "dn0", "kv", "q", "ao", "up1", "dn1"]


def _welems(name):
    kc, g, nb = WSPEC[name]
    return kc * (32 if g is None else g * 128)


def _blockify(w, g):
    K, N = w.shape
    kc = K // 128
    nb = N // (g * 128)
    a = w.reshape(kc, 128, nb, g * 128).transpose(2, 1, 0, 3)
    return np.ascontiguousarray(a).reshape(nb, 128, kc * g * 128)


def _cv_layout():
    off = {}
    n = 0

    def add(name, cnt):
        nonlocal n
        off[name] = n
        n += cnt

    add("g_ssm", 8)
    add("g_ffn0", 8)
    add("g_ffn1", 8)
    add("g_kv", 8)
    add("g_attn", 8)
    add("mcw", 24 * 4)
    add("mcb", 24)
    add("fcw0", 44 * 3)
    add("fcb0", 44)
    add("fcw1", 44 * 3)
    add("fcb1", 44)
    add("ng", 16)
    add("dsk", 16)
    add("kng", 1)
    add("qng", 1)
    add("slg", 1)
    add("b31", 8)
    add("dtb", 128)
    add("alog", 128)
    add("lamv", 256)
    return off, n


CVOFF, NCV = _cv_layout()


class Builder:
    def __init__(self, nc, ntiles=NT, nseq=2, phases="MFKAG", dbg=None):
        self.nc = nc
        self.s = Sched(nc)
        self.ntiles = ntiles
        self.nseq = nseq
        self.phases = phases
        self.dbg = dbg
        self._uid = 0
        self.dram()
        self.sbuf()

    def dram(self):
        nc = self.nc
        self.xT = nc.dram_tensor("xT", [2, D, SEQ], F32, kind="ExternalInput").ap()
        self.yT = nc.dram_tensor("yT", [2, D, SEQ], F32, kind="ExternalOutput").ap()
        self.cv_d = nc.dram_tensor("cvec", [128, NCV], F32, kind="ExternalInput").ap()
        self.bt_d = nc.dram_tensor("biast", [128, 8, 256], F32, kind="ExternalInput").ap()
        self.w32 = {}
        self.wbf = {}
        for n in WORDER:
            kc, g, nb = WSPEC[n]
            el = _welems(n)
            self.w32[n] = nc.dram_tensor("w_" + n, [nb, 128, el], F32, kind="ExternalInput").ap()
            self.wbf[n] = nc.dram_tensor("wb_" + n, [nb, 128, el], BF16).ap()
        self.kT_d = nc.dram_tensor("kT_s", [2, 8, 128, SEQ], BF16).ap()
        self.v_d = nc.dram_tensor("v_s", [2, SEQ, D], BF16).ap()

    def sb(self, name, shape, dt):
        return self.nc.alloc_sbuf_tensor(name, list(shape), dt)

    def sbuf(self):
        nc = self.nc
        self.x = self.sb("x", [128, 8, T], F32)
        self.xn = self.sb("xn", [128, 8, T], BF16)
        self.xsq = self.sb("xsq", [128, 8, T], BF16)
        self.rs = self.sb("rs", [128, T], F32)
        self.wsl = [self.sb("wsl%d" % i, [128, WSLOT], BF16) for i in range(3)]
        self.cv = self.sb("cv", [128, NCV], F32)
        self.ident = self.sb("ident", [128, 128], BF16)
        self.ones = self.sb("ones", [128, 128], BF16)
        self.bones = self.sb("bones", [128, 128], BF16)
        self.onesf = self.sb("onesf", [128, 128], F32)
        self.tri = self.sb("tri", [128, 128], F32)
        self.trib = self.sb("trib", [128, 128], BF16)
        self.ustr = self.sb("ustr", [128, 128], BF16)
        self.maskneg = self.sb("maskneg", [128, 128], F32)
        self.bias8 = self.sb("bias8", [128, 8, 640], BF16)
        self.ones3 = self.sb("ones3", [128, 384], BF16)
        self.smallc = self.sb("smallc", [128, 64], F32)
        self.Ab = self.sb("Ab", [128, 128], F32)
        self.dgs = [self.sb("dg%d" % i, [128, 128], BF16) for i in range(8)]
        self.uext = [self.sb("uext%d" % i, [128, T + 4], BF16) for i in range(3)]
        self.halo_m = self.sb("halo_m", [128, 24, 4], BF16)
        self.halo_f = [self.sb("halo_f%d" % l, [128, 44, 2], BF16) for l in range(2)]
        ARENA = 52000
        self.arena = self.sb("arena", [128, ARENA], BF16)
        self.S = self.sb("S", [128, DIN], F32)
        self.Sb = self.sb("Sb", [128, DIN], BF16)
        self.banks = [nc.alloc_psum_tensor("bank%d" % i, [128, 512], F32) for i in range(8)]

    def carve(self, off, nelem_bf16, dt=BF16):
        ap = self.arena[:, off:off + nelem_bf16]
        if dt == F32:
            ap = ap.bitcast(F32)
        return ap

    def uid(self):
        self._uid += 1
        return self._uid

    def dump(self, name, ap, rtoks, shape, dt=F32):
        if not self.dbg:
            return
        d = self.nc.dram_tensor("dbg_" + name, list(shape), dt, kind="ExternalOutput").ap()
        self.s.dma("sp", lambda e: e.dma_start(out=d, in_=ap), r=rtoks, key=("dbg", name))

    def O(self, eng, fn, r=(), w=()):
        return self.s.op(eng, fn, r, w)

    @staticmethod
    def bc_last(ap, n):
        return bass.AP(ap.tensor, ap.offset, [list(ap.ap[0]), list(ap.ap[1]), [0, n]])

    @staticmethod
    def bc_mid(ap, n):
        return bass.AP(ap.tensor, ap.offset, [list(ap.ap[0]), [0, n], list(ap.ap[1])])

    def col(self, name, i=0):
        o = CVOFF[name] + i
        return self.cv[:, o:o + 1]

    def setup(self):
        s = self.s
        b = self
        s.dma("sp", lambda e: e.dma_start(out=b.cv[:], in_=b.cv_d), w=["cv"], key="cvl")
        b.biast = self.carve(0, 8 * 256 * 2, F32).rearrange("p (h j) -> p h j", j=256)
        s.dma("sp", lambda e: e.dma_start(out=b.biast, in_=b.bt_d), w=["biast"], key="btl")
        P = "pool"

        def sel(t, pattern, op, base, cm, fill=0.0):
            return lambda e: e.affine_select(out=t[:], in_=t[:], pattern=pattern, compare_op=op, fill=fill,
                                             base=base, channel_multiplier=cm)
        for t in (b.ident, b.ones, b.onesf, b.tri, b.trib, b.ustr):
            self.O(P, lambda e, t=t: e.memset(t[:], 1.0), w=["c0"])
        self.O(P, lambda e: e.memset(b.maskneg[:], 0.0), w=["c0"])
        self.O(P, lambda e: e.memset(b.bones[:], 0.0), w=["c0"])
        self.O(P, lambda e: e.memset(b.bones[0:64, 0:64], 1.0), w=["c0"])
        self.O(P, lambda e: e.memset(b.bones[64:128, 64:128], 1.0), w=["c0"])
        self.O(P, sel(b.ident, [[1, 128]], ALU.is_equal, 0, -1), r=["c0"], w=["c1"])
        self.O(P, sel(b.tri, [[1, 128]], ALU.is_ge, 0, -1), r=["c0"], w=["c1"])
        self.O(P, sel(b.trib, [[1, 128]], ALU.is_ge, 0, -1), r=["c0"], w=["c1"])
        self.O(P, sel(b.ustr, [[-1, 128]], ALU.is_ge, -1, 1), r=["c0"], w=["c1"])
        self.O(P, sel(b.maskneg, [[1, 128]], ALU.is_ge, 0, -1, fill=NEG), r=["c0"], w=["c1"])
        self.O(P, lambda e: e.memset(b.halo_m[:], 0.0), w=["halo_m"])
        for l in range(2):
            self.O(P, lambda e, l=l: e.memset(b.halo_f[l][:], 0.0), w=[("halo_f", l)])
        for h in range(8):
            self.O("dve", lambda e, h=h: e.tensor_tensor(out=b.biast[:, h, 0:128], in0=b.biast[:, h, 0:128],
                                                         in1=b.maskneg[:], op=ALU.add),
                   r=["biast", "c1"], w=["biast"])
        self.O(P, lambda e: e.memset(b.ones3[:], 1.0), w=["ones3"])
        for h in range(8):
            self.O("dve", lambda e, h=h: e.tensor_scalar(out=b.bias8[:, h, 0:256], in0=b.biast[:, h, :], scalar1=8.0,
                                                         scalar2=None, op0=ALU.mult), r=["biast"], w=["bias8"])
            self.O("dve", lambda e, h=h: e.tensor_scalar(out=b.bias8[:, h, 256:640], in0=b.ones3[:], scalar1=b.col("b31", h),
                                                         scalar2=8.0, op0=ALU.mult, op1=ALU.mult),
                   r=["ones3", "cv"], w=["bias8"])
        ao = CVOFF["alog"]
        self.O("act", lambda e: e.activation(out=b.Ab[:], in_=b.cv[:, ao:ao + 128], func=AF.Exp), r=["cv"], w=["Ab"])
        self.O("dve", lambda e: e.tensor_scalar(out=b.Ab[:], in0=b.Ab[:], scalar1=-1.0, scalar2=None, op0=ALU.mult),
               r=["Ab"], w=["Ab"])
        lo = CVOFF["lamv"]
        sc = b.smallc
        self.tmp64 = self.sb("tmp64", [128, 128], F32)
        self.O("dve", lambda e: e.tensor_tensor(out=b.tmp64[:, 0:64], in0=b.cv[:, lo:lo + 64], in1=b.cv[:, lo + 64:lo + 128],
                                                op=ALU.mult), r=["cv"], w=["t64a"])
        self.O("dve", lambda e: e.tensor_tensor(out=b.tmp64[:, 64:128], in0=b.cv[:, lo + 128:lo + 192],
                                                in1=b.cv[:, lo + 192:lo + 256], op=ALU.mult), r=["cv"], w=["t64b"])
        self.O("dve", lambda e: e.reduce_sum(out=sc[:, 2:3], in_=b.tmp64[:, 0:64], axis=AX.X), r=["t64a"], w=["sc2"])
        self.O("dve", lambda e: e.reduce_sum(out=sc[:, 3:4], in_=b.tmp64[:, 64:128], axis=AX.X), r=["t64b"], w=["sc3"])
        self.O("act", lambda e: e.activation(out=sc[:, 4:6], in_=sc[:, 2:4], func=AF.Exp), r=["sc2", "sc3"], w=["sc4"])
        self.O("dve", lambda e: e.scalar_tensor_tensor(out=sc[:, 0:1], in0=sc[:, 5:6], scalar=-LAM_INIT, in1=sc[:, 4:5],
                                                       op0=ALU.add, op1=ALU.subtract), r=["sc4"], w=["neglam"])
        self.O("pool", lambda e: e.memset(sc[:, 6:7], EPS), w=["epsc"])
        self.O("dve", lambda e: e.tensor_scalar(out=sc[:, 1:2], in0=b.col("slg"), scalar1=(1.0 - LAM_INIT), scalar2=None,
                                                op0=ALU.mult), r=["cv"], w=["slg2"])

    def casts(self, names):
        s = self.s
        i = getattr(self, "_cast_i", 0)
        for n in names:
            kc, g, nb = WSPEC[n]
            el = _welems(n)
            cb = 2048 if el % 2048 == 0 else (1024 if el % 1024 == 0 else (el if el <= 2048 else 1408))
            assert el % cb == 0, (n, el, cb)
            for blk in range(nb):
                s.dma("pool",
                      lambda e, n=n, blk=blk, cb=cb: e.dma_start(
                          out=self.wbf[n][blk].rearrange("p (a b) -> p a b", b=cb),
                          in_=self.w32[n][blk].rearrange("p (a b) -> p a b", b=cb)),
                      r=[], w=[("wd", n, blk), ("castslot", i % 8)], key=("cast", n, blk))
                i += 1
        self._cast_i = i

    def ws_init(self, seq):
        self.wseq = seq
        self.wi = 0
        self.wissued = 0

    def ws_issue(self, idx):
        n, blk = self.wseq[idx]
        slot = idx % 3
        el = _welems(n)
        self.s.dma("sp", lambda e: e.dma_start(out=self.wsl[slot][:, 0:el], in_=self.wbf[n][blk]),
                   r=[("wd", n, blk)], w=[("ws", slot)], key=("ws", slot))

    def ws_next(self, n, blk):
        idx = self.wi
        assert self.wseq[idx] == (n, blk), (self.wseq[idx], n, blk)
        while self.wissued <= min(idx + 2, len(self.wseq) - 1):
            self.ws_issue(self.wissued)
            self.wissued += 1
        self.wi += 1
        return idx % 3

    def norm(self, gname):
        b = self
        for kc in range(8):
            self.O("act", lambda e, kc=kc: e.activation(out=b.xsq[:, kc, :], in_=b.x[:, kc, :], func=AF.Square),
                   r=[("x", kc)], w=[("xsq", kc)])
        for kc in range(8):
            self.O("pe", lambda e, kc=kc: e.matmul(b.banks[6][:], lhsT=b.ones[:], rhs=b.xsq[:, kc, :],
                                                   start=(kc == 0), stop=(kc == 7)),
                   r=[("xsq", kc), "c0"], w=[("b", 6)])
        self.rstd(b.rs[:], "rs", 6, 1.0 / D)
        for kc in range(8):
            self.O("dve", lambda e, kc=kc: e.scalar_tensor_tensor(out=b.xn[:, kc, :], in0=b.x[:, kc, :],
                                                                  scalar=b.col(gname, kc), in1=b.rs[:],
                                                                  op0=ALU.mult, op1=ALU.mult),
                   r=[("x", kc), "rs", "cv"], w=[("xn", kc)])

    def rstd(self, out, tok, bank, scale):
        b = self
        self.O("act", lambda e: e.activation(out=out, in_=b.banks[bank][:], func=AF.Ln, bias=b.smallc[:, 6:7], scale=scale),
               r=[("b", bank), "epsc"], w=[tok])
        self.O("act", lambda e: e.activation(out=out, in_=out, func=AF.Exp, scale=-0.5), r=[tok], w=[tok])

    def pipeline(self, tasks):
        res = {}
        if tasks:
            res[0] = tasks[0][0]()
        for i in range(len(tasks)):
            if i + 1 < len(tasks):
                res[i + 1] = tasks[i + 1][0]()
            tasks[i][1](res.pop(i))

    def pbank(self):
        self._pb = (getattr(self, "_pb", -1) + 1) % 4
        return self._pb

    def cbank(self):
        self._cb = (getattr(self, "_cb", -1) + 1) % 2
        return 4 + self._cb

    def dg(self):
        self._dg = (getattr(self, "_dg", -1) + 1) % 8
        return self._dg

    def ue(self):
        self._ue = (getattr(self, "_ue", -1) + 1) % 3
        return self._ue

    def proj(self, slot, KC, stride, c0, act, acttok):
        b = self
        pb = self.pbank()
        for kc in range(KC):
            self.O("pe", lambda e, kc=kc: e.matmul(b.banks[pb][:], lhsT=b.wsl[slot][:, kc * stride + c0:kc * stride + c0 + 128],
                                                   rhs=act[:, kc, :], start=(kc == 0), stop=(kc == KC - 1)),
                   r=[("ws", slot), (acttok, kc)], w=[("b", pb)])
        return pb

    def conv(self, pb, K, halo, htok, cj, wname, first):
        b = self
        u = self.ue()
        ut = ("uext", u)
        H = K - 1
        ub = b.uext[u]
        if first:
            self.O("dve", lambda e: e.memset(ub[:, 0:H], 0.0), w=[ut])
        else:
            self.O("dve", lambda e: e.tensor_copy(out=ub[:, 0:H], in_=halo[:, cj, 0:H]), r=[htok], w=[ut])
        self.O("act", lambda e: e.activation(out=ub[:, H:H + T], in_=b.banks[pb][:], func=AF.Identity),
               r=[("b", pb)], w=[ut])
        self.O("dve", lambda e: e.tensor_copy(out=halo[:, cj, 0:H], in_=ub[:, T:T + H]), r=[ut], w=[htok])
        cb = self.cbank()
        for k in range(K):
            d = self.dg()
            self.O("dve", lambda e, d=d, k=k: e.tensor_scalar(out=b.dgs[d][:], in0=b.ident[:], scalar1=b.col(wname, cj * K + k),
                                                              scalar2=None, op0=ALU.mult),
                   r=["c1", "cv"], w=[("dg", d)])
            self.O("pe", lambda e, d=d, k=k: e.matmul(b.banks[cb][:], lhsT=b.dgs[d][:], rhs=ub[:, k:k + T],
                                                      start=(k == 0), stop=(k == K - 1)),
                   r=[("dg", d), ut], w=[("b", cb)])
        return cb

    def ffn(self, L, first):
        b = self
        self.norm("g_ffn%d" % L)
        h = self.carve(0, 22 * T).rearrange("p (a t) -> p a t", t=T)
        up = "up%d" % L
        dn = "dn%d" % L
        tasks = []
        slots = {}
        for blk in range(11):
            for g in range(4):
                is_gate = g < 2
                hj = 2 * blk + (g % 2)
                cj = hj if is_gate else 22 + hj

                def pj(blk=blk, g=g):
                    if g == 0:
                        slots[blk] = self.ws_next(up, blk)
                    return self.proj(slots[blk], 8, 512, g * 128, b.xn, "xn")

                def post(pb, is_gate=is_gate, hj=hj, cj=cj):
                    cb = self.conv(pb, 3, b.halo_f[L], ("halo_f", L, cj), cj, "fcw%d" % L, first)
                    bias = b.col("fcb%d" % L, cj)
                    if is_gate:
                        self.O("act", lambda e: e.activation(out=h[:, hj, :], in_=b.banks[cb][:], func=AF.Silu, bias=bias),
                               r=[("b", cb), "cv"], w=[("h", hj)])
                    else:
                        self.O("dve", lambda e: e.scalar_tensor_tensor(
                            out=h[:, hj, :], in0=b.banks[cb][:], scalar=bias, in1=h[:, hj, :], op0=ALU.add, op1=ALU.mult),
                            r=[("b", cb), "cv", ("h", hj)], w=[("h", hj)])
                tasks.append((pj, post))
        self.pipeline(tasks)
        for blk in range(4):
            slot = self.ws_next(dn, blk)
            for g in range(2):
                j = blk * 2 + g
                pb = self.pbank()
                for fc in range(22):
                    self.O("pe", lambda e, fc=fc, pb=pb, g=g, slot=slot: e.matmul(
                        b.banks[pb][:], lhsT=b.wsl[slot][:, fc * 256 + g * 128:fc * 256 + g * 128 + 128],
                        rhs=h[:, fc, :], start=(fc == 0), stop=(fc == 21)),
                        r=[("ws", slot), ("h", fc)], w=[("b", pb)])
                self.O("dve", lambda e, j=j, pb=pb: e.tensor_tensor(out=b.x[:, j, :], in0=b.banks[pb][:], in1=b.x[:, j, :],
                                                                    op=ALU.add),
                       r=[("b", pb), ("x", j)], w=[("x", j)])

    def load_x(self, sq, ti):
        b = self
        src = b.xT[sq].rearrange("(kc p) t -> p kc t", p=128)[:, :, ti * T:(ti + 1) * T]
        self.s.dma("sp", lambda e: e.dma_start(out=b.x[:], in_=src), w=[("x", kc) for kc in range(8)], key="xl")

    def store_x(self, sq, ti):
        b = self
        dst = b.yT[sq].rearrange("(kc p) t -> p kc t", p=128)[:, :, ti * T:(ti + 1) * T]
        self.s.dma("sp", lambda e: e.dma_start(out=dst, in_=b.x[:]), r=[("x", kc) for kc in range(8)], key="xs")

    def mixer(self, sq, ti):
        b = self
        first = ti == 0
        self.norm("g_ssm")
        v3 = lambda off, n: self.carve(off, n * T).rearrange("p (a t) -> p a t", t=T)
        zs = v3(0, 16)
        xbc = v3(8192, 24)
        yg = zs
        xdt = self.carve(28672, 2048)
        xdtw = self.carve(30720, 2048)
        btok = self.carve(32768, 512).rearrange("p (g n) -> p g n", n=128)
        Ah = [self.carve(20480 + i * 1024, 1024).rearrange("p (h l) -> p h l", l=128) for i in range(4)]
        E = [self.carve(24576 + i * 1024, 1024).rearrange("p (h l) -> p h l", l=128) for i in range(4)]
        M = [self.carve(33280 + i * 1024, 1024).rearrange("p (h l) -> p h l", l=128) for i in range(4)]
        cbm = self.carve(39424, 512).rearrange("p (g l) -> p g l", l=128)
        ytok = self.carve(39936, 2048)
        t1 = [self.carve(41984 + i * 1024, 1024, F32) for i in range(2)]
        tmpf = [self.carve(44032 + i * 256, 256, F32) for i in range(2)]
        sm = [self.carve(44544 + i * 256, 256, F32) for i in range(8)]
        tmpf2 = [self.carve(46592 + i * 2048, 2048, F32) for i in range(2)]
        dtr, dtt, aa, e1 = sm[0], sm[1], sm[2], sm[3]
        acs_sb, eacs, wdec, dec = sm[4], sm[5], sm[6], sm[7]
        bf = lambda i: b.banks[i][:].bitcast(BF16)
        if first:
            self.O("dve", lambda e: e.memset(b.S[:], 0.0), w=["S"])
            self.O("dve", lambda e: e.memset(b.Sb[:], 0.0), w=["Sb"])
        tasks = []
        slots = {}
        for blk in range(10):
            for g in range(4):
                j = blk * 4 + g

                def pj(blk=blk, g=g):
                    if g == 0:
                        slots[blk] = self.ws_next("in", blk)
                    return self.proj(slots[blk], 8, 512, g * 128, b.xn, "xn")

                def post(pb, j=j):
                    if j < 16:
                        self.O("act", lambda e: e.activation(out=zs[:, j, :], in_=b.banks[pb][:], func=AF.Silu),
                               r=[("b", pb)], w=[("zs", j)])
                    else:
                        cj = j - 16
                        cb = self.conv(pb, 4, b.halo_m, ("halo_m", cj), cj, "mcw", first)
                        self.O("act", lambda e: e.activation(out=xbc[:, cj, :], in_=b.banks[cb][:], func=AF.Silu,
                                                             bias=b.col("mcb", cj)),
                               r=[("b", cb), "cv"], w=[("xbc", cj)])
                tasks.append((pj, post))
        self.pipeline(tasks)
        slot = self.ws_next("dt", 0)
        for c in range(4):
            for kc in range(8):
                self.O("pe", lambda e, c=c, kc=kc, slot=slot: e.matmul(b.banks[7][:, c * 32:(c + 1) * 32],
                                                            lhsT=b.xn[:, kc, c * 128:(c + 1) * 128],
                                                            rhs=b.wsl[slot][:, kc * 32:(kc + 1) * 32],
                                                            start=(kc == 0), stop=(kc == 7)),
                       r=[("ws", slot), ("xn", kc)], w=[("b", 7)])
        do = CVOFF["dtb"]
        self.O("dve", lambda e: e.tensor_tensor(out=dtr[:], in0=b.banks[7][:, 0:128], in1=b.cv[:, do:do + 128], op=ALU.add),
               r=[("b", 7), "cv"], w=["dtr"])
        self.O("act", lambda e: e.activation(out=e1[:], in_=dtr[:], func=AF.Exp), r=["dtr"], w=["e1"])
        self.O("act", lambda e: e.activation(out=dtt[:], in_=e1[:], func=AF.Ln, bias=1.0), r=["e1"], w=["dt"])
        self.O("dve", lambda e: e.tensor_tensor(out=aa[:], in0=dtt[:], in1=b.Ab[:], op=ALU.mult), r=["dt", "Ab"], w=["aa"])
        self.dump("dtt", dtt[:], ["dt"], [128, 128])
        self.dump("dtr", dtr[:], ["dtr"], [128, 128])
        self.dump("x0", xbc[:, 0, :], [("xbc", 0)], [128, 512], BF16)
        self.dump("B0", xbc[:, 16, :], [("xbc", 16)], [128, 512], BF16)
        self.dump("z0", zs[:, 0, :], [("zs", 0)], [128, 512], BF16)
        def chunk(c):
            cs = slice(c * 128, (c + 1) * 128)
            a_c = aa[:, c * 32:(c + 1) * 32]
            dt_c = dtt[:, c * 32:(c + 1) * 32]
            self.O("pe", lambda e, a_c=a_c: e.matmul(b.banks[6][:, 0:32], lhsT=b.tri[:], rhs=a_c, start=True, stop=True),
                   r=["aa", "c1"], w=[("b", 6)])
            self.O("pe", lambda e, a_c=a_c: e.matmul(b.banks[6][:, 32:64], lhsT=b.onesf[:], rhs=a_c, start=True, stop=True),
                   r=["aa", "c0"], w=[("b", 6)])
            self.O("act", lambda e: e.activation(out=eacs[:, 0:32], in_=b.banks[6][:, 0:32], func=AF.Exp),
                   r=[("b", 6)], w=["eacs"])
            self.O("dve", lambda e: e.tensor_copy(out=acs_sb[:, 0:32], in_=b.banks[6][:, 0:32]), r=[("b", 6)], w=["acs"])
            self.O("dve", lambda e: e.tensor_tensor(out=wdec[:, 0:32], in0=b.banks[6][:, 32:64], in1=acs_sb[:, 0:32],
                                                    op=ALU.subtract), r=[("b", 6), "acs"], w=["wd"])
            self.O("act", lambda e: e.activation(out=wdec[:, 0:32], in_=wdec[:, 0:32], func=AF.Exp), r=["wd"], w=["wd"])
            self.O("act", lambda e: e.activation(out=dec[:, 0:32], in_=b.banks[6][:, 32:64], func=AF.Exp),
                   r=[("b", 6)], w=["dec"])
            for j in range(16):
                bk = 4 + j // 8
                self.O("pe", lambda e, j=j, bk=bk: e.transpose(out=bf(bk)[:, (j % 8) * 128:(j % 8 + 1) * 128],
                                                               in_=xbc[:, j, cs], identity=b.ident[:]),
                       r=[("xbc", j), "c1"], w=[("b", bk)])
            for half in range(2):
                self.O("dve", lambda e, half=half: e.tensor_tensor(
                    out=xdt[:, half * 1024:(half + 1) * 1024].rearrange("p (h d) -> p h d", d=64),
                    in0=bf(4 + half)[:, 0:1024].rearrange("p (h d) -> p h d", d=64),
                    in1=self.bc_last(dt_c[:, half * 16:(half + 1) * 16], 64), op=ALU.mult),
                    r=[("b", 4 + half), "dt"], w=[("xdt", 2 * half), ("xdt", 2 * half + 1)])
            self.O("pool", lambda e: e.tensor_tensor(out=xdtw[:].rearrange("p (h d) -> p h d", d=64),
                                                     in0=xdt[:].rearrange("p (h d) -> p h d", d=64),
                                                     in1=self.bc_last(wdec[:, 0:32], 64), op=ALU.mult),
                   r=[("xdt", q) for q in range(4)] + ["wd"], w=[("xdtw", q) for q in range(4)])
            for g in range(4):
                self.O("pe", lambda e, g=g: e.transpose(out=bf(4)[:, g * 128:(g + 1) * 128], in_=xbc[:, 16 + g, cs],
                                                        identity=b.ident[:]),
                       r=[("xbc", 16 + g), "c1"], w=[("b", 4)])
            self.O("act", lambda e: e.activation(out=btok[:].rearrange("p g n -> p (g n)"), in_=bf(4)[:, 0:512],
                                                 func=AF.Identity), r=[("b", 4)], w=["btok"])
            for g in range(4):
                self.O("pe", lambda e, g=g: e.matmul(b.banks[0][:, g * 128:(g + 1) * 128], lhsT=xbc[:, 16 + g, cs],
                                                     rhs=xbc[:, 20 + g, cs], start=True, stop=True),
                       r=[("xbc", 16 + g), ("xbc", 20 + g)], w=[("b", 0)])
            for g in range(4):
                self.O("dve", lambda e, g=g: e.tensor_tensor(out=cbm[:, g, :], in0=b.banks[0][:, g * 128:(g + 1) * 128],
                                                             in1=b.tri[:], op=ALU.mult),
                       r=[("b", 0), "c1"], w=[("cbm", g)])
            for g in range(4):
                self.O("pool", lambda e, g=g: e.tensor_tensor(out=Ah[g][:], in0=self.bc_mid(b.trib[:], 8),
                                                              in1=self.bc_last(a_c[:, g * 8:(g + 1) * 8], 128), op=ALU.mult),
                       r=["c1", "aa"], w=[("Ah", g)])
            for g in range(4):
                self.O("pool", lambda e, g=g: e.tensor_tensor(
                    out=b.S[:, g * 512:(g + 1) * 512].rearrange("p (h d) -> p h d", d=64),
                    in0=b.S[:, g * 512:(g + 1) * 512].rearrange("p (h d) -> p h d", d=64),
                    in1=self.bc_last(dec[:, g * 8:(g + 1) * 8], 64), op=ALU.mult),
                    r=["dec", ("S", g)], w=[("S", g)])
            for g in range(4):
                for i in range(2):
                    self.O("pe", lambda e, i=i, g=g: e.matmul(b.banks[1 + i][:], lhsT=b.ustr[:],
                                                              rhs=Ah[g][:, i * 4:(i + 1) * 4, :].rearrange("p h l -> p (h l)"),
                                                              start=True, stop=True),
                           r=[("Ah", g), "c1"], w=[("b", 1 + i)])
                    self.O("act", lambda e, i=i, g=g: e.activation(
                        out=E[g][:, i * 4:(i + 1) * 4, :].rearrange("p h l -> p (h l)"), in_=b.banks[1 + i][:], func=AF.Exp),
                        r=[("b", 1 + i)], w=[("E", g, i)])
            for g in range(4):
                for hh in range(8):
                    self.O("dve", lambda e, hh=hh, g=g: e.tensor_tensor(out=M[g][:, hh, :], in0=E[g][:, hh, :],
                                                                        in1=cbm[:, g, :], op=ALU.mult),
                           r=[("E", g, hh // 4), ("cbm", g)], w=[("M", g)])
            for g in range(4):
                ob = (3, 1)[g % 2]
                db = (7, 2)[g % 2]
                self.O("pe", lambda e, g=g, ob=ob: e.matmul(b.banks[ob][:], lhsT=xbc[:, 20 + g, cs],
                                                            rhs=b.Sb[:, g * 512:(g + 1) * 512], start=True, stop=True),
                       r=[("xbc", 20 + g), ("Sb", g)], w=[("b", ob)])
                for hh in range(8):
                    h = g * 8 + hh
                    self.O("pe", lambda e, hh=hh, h=h, g=g, db=db: e.matmul(b.banks[db][:, hh * 64:(hh + 1) * 64],
                                                                            lhsT=M[g][:, hh, :],
                                                                            rhs=xdt[:, h * 64:(h + 1) * 64], start=True, stop=True),
                           r=[("M", g), ("xdt", g)], w=[("b", db)])
                tt = t1[g % 2]
                self.O("dve", lambda e, g=g, tt=tt, ob=ob: e.tensor_tensor(
                    out=tt[:].rearrange("p (h d) -> p h d", d=64), in0=b.banks[ob][:].rearrange("p (h d) -> p h d", d=64),
                    in1=self.bc_last(eacs[:, g * 8:(g + 1) * 8], 64), op=ALU.mult),
                    r=[("b", ob), "eacs"], w=[("t1", g % 2)])
                self.O("dve", lambda e, g=g, tt=tt, db=db: e.tensor_tensor(out=ytok[:, g * 512:(g + 1) * 512], in0=b.banks[db][:],
                                                                           in1=tt[:], op=ALU.add),
                       r=[("b", db), ("t1", g % 2)], w=[("ytok", g)])
            for g in range(4):
                sbk = (0, 6)[g % 2]
                self.O("pe", lambda e, g=g, sbk=sbk: e.matmul(b.banks[sbk][:], lhsT=btok[:, g, :], rhs=xdtw[:, g * 512:(g + 1) * 512],
                                                              start=True, stop=True),
                       r=["btok", ("xdtw", g)] + [("cbm", q) for q in range(4)], w=[("b", sbk)])
                self.O("dve", lambda e, g=g, sbk=sbk: e.tensor_tensor(out=b.S[:, g * 512:(g + 1) * 512], in0=b.banks[sbk][:],
                                                                      in1=b.S[:, g * 512:(g + 1) * 512], op=ALU.add),
                       r=[("b", sbk), ("S", g)], w=[("S", g)])
                self.O("act", lambda e, g=g: e.activation(out=b.Sb[:, g * 512:(g + 1) * 512], in_=b.S[:, g * 512:(g + 1) * 512],
                                                          func=AF.Identity), r=[("S", g)], w=[("Sb", g)])
            if c == 0:
                self.dump("ytok0", ytok[:], [("ytok", q) for q in range(4)], [128, 2048], BF16)
                self.dump("eacs0", eacs[:, 0:32], ["eacs"], [128, 32])
                self.dump("acs0", acs_sb[:, 0:32], ["acs"], [128, 32])
                self.dump("S0", b.S[:], [("S", q) for q in range(4)], [128, 2048])
                self.dump("xdt0", xdt[:], [("xdt", q) for q in range(4)], [128, 2048], BF16)
            for j in range(16):
                bk = 4 + j // 8
                self.O("pe", lambda e, j=j, bk=bk: e.transpose(out=bf(bk)[:, (j % 8) * 128:(j % 8 + 1) * 128],
                                                               in_=ytok[:, j * 128:(j + 1) * 128], identity=b.ident[:]),
                       r=[("ytok", j // 4), "c1"], w=[("b", bk)])
            for half in range(2):
                j0 = half * 8
                tf = tmpf2[half]
                self.O("pool", lambda e, j0=j0, tf=tf: e.tensor_tensor(
                    out=tf[:].rearrange("p (j l) -> p j l", l=128), in0=xbc[:, j0:j0 + 8, cs],
                    in1=self.bc_last(b.cv[:, CVOFF["dsk"] + j0:CVOFF["dsk"] + j0 + 8], 128), op=ALU.mult),
                    r=[("xbc", j0 + q) for q in range(8)] + ["cv"], w=[("tmpf", half)])
                self.O("dve", lambda e, half=half, tf=tf: e.tensor_tensor(
                    out=tf[:].rearrange("p (j l) -> p j l", l=128), in0=bf(4 + half)[:, 0:1024].rearrange("p (j l) -> p j l", l=128),
                    in1=tf[:].rearrange("p (j l) -> p j l", l=128), op=ALU.add),
                    r=[("b", 4 + half), ("tmpf", half)], w=[("tmpf", half)])
                self.O("pool", lambda e, j0=j0, tf=tf: e.tensor_tensor(
                    out=yg[:, j0:j0 + 8, cs], in0=tf[:].rearrange("p (j l) -> p j l", l=128), in1=zs[:, j0:j0 + 8, cs],
                    op=ALU.mult),
                    r=[("tmpf", half)] + [("zs", j0 + q) for q in range(8)], w=[("zs", j0 + q) for q in range(8)])
        for c in range(4):
            chunk(c)
        for g in range(4):
            for q in range(4):
                j = g * 4 + q
                self.O("act", lambda e, j=j, q=q: e.activation(out=b.xsq[:, q, :], in_=yg[:, j, :], func=AF.Square),
                       r=[("zs", j)], w=[("xsq", q)])
            for q in range(4):
                self.O("pe", lambda e, q=q: e.matmul(b.banks[6][:], lhsT=b.ones[:], rhs=b.xsq[:, q, :], start=(q == 0),
                                                     stop=(q == 3)), r=[("xsq", q), "c0"], w=[("b", 6)])
            self.rstd(b.rs[:], "rs", 6, 1.0 / 512)
            for q in range(4):
                j = g * 4 + q
                self.O("dve", lambda e, j=j: e.scalar_tensor_tensor(out=yg[:, j, :], in0=yg[:, j, :], scalar=b.col("ng", j),
                                                                    in1=b.rs[:], op0=ALU.mult, op1=ALU.mult),
                       r=[("zs", j), "rs", "cv"], w=[("zs", j)])
        for blk in range(4):
            slot = self.ws_next("out", blk)
            for g2 in range(2):
                j = blk * 2 + g2
                pb = self.proj(slot, 16, 256, g2 * 128, yg, "zs")
                self.O("dve", lambda e, j=j, pb=pb: e.tensor_tensor(out=b.x[:, j, :], in0=b.banks[pb][:], in1=b.x[:, j, :],
                                                                    op=ALU.add),
                       r=[("b", pb), ("x", j)], w=[("x", j)])

    def headnorm(self, pb, gcol, outap, outtok, bufs, i):
        b = self
        ksq, rk = bufs
        self.O("act", lambda e: e.activation(out=ksq[i][:], in_=b.banks[pb][:], func=AF.Square), r=[("b", pb)], w=[("ksq", i)])
        self.O("pe", lambda e: e.matmul(b.banks[6][:], lhsT=b.bones[:], rhs=ksq[i][:], start=True, stop=True),
               r=[("ksq", i), "c0"], w=[("b", 6)])
        self.rstd(rk[i][:], ("rk", i), 6, 1.0 / 64)
        self.O("dve", lambda e: e.scalar_tensor_tensor(out=outap, in0=b.banks[pb][:], scalar=gcol, in1=rk[i][:],
                                                       op0=ALU.mult, op1=ALU.mult),
               r=[("b", pb), ("rk", i), "cv"], w=[outtok])

    def kvproj(self, sq, ti):
        b = self
        s = self.s
        self.norm("g_kv")
        kt = [self.carve(i * 512, 512) for i in range(2)]
        ksq = [self.carve(1024 + i * 512, 512) for i in range(2)]
        rk = [self.carve(2048 + i * 1024, 1024, F32) for i in range(2)]
        vt = [self.carve(4096 + i * 512, 512) for i in range(2)]
        tasks = []
        slots = {}
        for blk in range(2):
            for g in range(4):
                h = blk * 4 + g

                def pj(blk=blk, g=g):
                    if g == 0:
                        slots[blk] = self.ws_next("kv", blk)
                    return self.proj(slots[blk], 8, 512, g * 128, b.xn, "xn")

                def post(pb, h=h):
                    i = h % 2
                    self.headnorm(pb, b.col("kng"), kt[i][:], ("kt", i), (ksq, rk), i)
                    s.dma("sp", lambda e: e.dma_start(out=b.kT_d[sq, h, :, ti * T:(ti + 1) * T], in_=kt[i][:]),
                          r=[("kt", i)], w=[("kTd", sq, h, ti)], key=("kst", i), fenced=True)
                tasks.append((pj, post))
        self.pipeline(tasks)
        for blk in range(2, 4):
            slot = self.ws_next("kv", blk)
            nh = blk - 2
            for c in range(4):
                pb = self.pbank()
                i = c % 2
                for kc in range(8):
                    self.O("pe", lambda e, kc=kc, c=c, pb=pb, slot=slot: e.matmul(b.banks[pb][:], lhsT=b.xn[:, kc, c * 128:(c + 1) * 128],
                                                                       rhs=b.wsl[slot][:, kc * 512:(kc + 1) * 512],
                                                                       start=(kc == 0), stop=(kc == 7)),
                           r=[("ws", slot), ("xn", kc)], w=[("b", pb)])
                self.O("act", lambda e, i=i, pb=pb: e.activation(out=vt[i][:], in_=b.banks[pb][:], func=AF.Identity),
                       r=[("b", pb)], w=[("vt", i)])
                s.dma("sp", lambda e, c=c, i=i, nh=nh: e.dma_start(
                    out=b.v_d[sq, ti * T + c * 128:ti * T + (c + 1) * 128, nh * 512:(nh + 1) * 512], in_=vt[i][:]),
                    r=[("vt", i)], w=[("vd", sq, ti, c, nh)], key=("vst", i), fenced=True)

    def attn(self, sq, ti):
        b = self
        s = self.s
        self.norm("g_attn")
        v3 = lambda off: self.carve(off, 8 * T).rearrange("p (a t) -> p a t", t=T)
        qT = [v3(0), v3(4096)]
        oT = v3(8192)
        KT = [self.carve(12288 + i * 2048, 2048) for i in range(2)]
        VV = [self.carve(16384 + i * 2048, 2048).rearrange("p (k d) -> p k d", d=128) for i in range(2)]
        PT = [self.carve(20480 + i * 512, 512) for i in range(4)]
        tmp = [self.carve(22528 + i * 1024, 1024, F32) for i in range(10)]
        ksq = [self.carve(32768 + i * 512, 512) for i in range(2)]
        rk = [tmp[8], tmp[9]]
        self.O("pool", lambda e: e.memset(qT[0][64:128, :, :], 0.0), w=[("qT0z",)])
        self.O("pool", lambda e: e.memset(qT[1][0:64, :, :], 0.0), w=[("qT1z",)])
        tasks = []
        slots = {}
        for blk in range(2):
            for g in range(4):
                h = blk * 4 + g

                def pj(blk=blk, g=g):
                    if g == 0:
                        slots[blk] = self.ws_next("q", blk)
                    return self.proj(slots[blk], 8, 512, g * 128, b.xn, "xn")

                def post(pb, h=h):
                    i = h % 2
                    self.O("act", lambda e: e.activation(out=ksq[i][:], in_=b.banks[pb][:], func=AF.Square),
                           r=[("b", pb)], w=[("ksq", i)])
                    self.O("pe", lambda e: e.matmul(b.banks[6][:], lhsT=b.bones[:], rhs=ksq[i][:], start=True, stop=True),
                           r=[("ksq", i), "c0"], w=[("b", 6)])
                    self.rstd(rk[i][:], ("rk", i), 6, 1.0 / 64)
                    for c in range(2):
                        ps_ = slice(c * 64, (c + 1) * 64)
                        self.O("dve", lambda e, c=c, ps_=ps_: e.scalar_tensor_tensor(
                            out=qT[c][ps_, h, :], in0=b.banks[pb][ps_, :], scalar=b.cv[ps_, CVOFF["qng"]:CVOFF["qng"] + 1],
                            in1=rk[i][ps_, :], op0=ALU.mult, op1=ALU.mult),
                            r=[("b", pb), ("rk", i), "cv", ("qT0z",), ("qT1z",)], w=[("qT", h, c)])
                tasks.append((pj, post))
        self.pipeline(tasks)
        nk = (ti + 1) * T
        nkb = nk // 128
        tR = [tmp[2], tmp[3]]
        tAB = [tmp[4], tmp[5]]
        tC, tD = tmp[6], tmp[7]
        osq = ksq[0]
        its = [(h, c, kb) for h in range(8) for c in range(2) for kb in range(nkb)]
        N = len(its)
        ptt_of = {}
        deferred = []

        def loads(h):
            i = h % 2
            s.dma("sp", lambda e: e.dma_start(out=KT[i][:, 0:nk], in_=b.kT_d[sq, h, :, 0:nk]),
                  r=[("kTd", sq, h, t2) for t2 in range(ti + 1)], w=[("KT", i)], key=("kl", i), fenced=True)
            s.dma("sp", lambda e: e.dma_start(
                out=VV[i][:, 0:nkb, :],
                in_=b.v_d[sq, 0:nk, h * 128:(h + 1) * 128].rearrange("(k p) d -> p k d", p=128)),
                r=[("vd", sq, t2, c2, h // 4) for t2 in range(ti + 1) for c2 in range(4)], w=[("VV", i)],
                key=("vl", i), fenced=True)

        def qk(n):
            h, c, kb = its[n]
            i = h % 2
            c0 = max(0, kb - ti * 4) * 128
            sb = n % 3
            far = kb <= ti * 4 - 2
            self.O("pe", lambda e: e.matmul(b.banks[sb][:, c0:T], lhsT=KT[i][:, kb * 128:(kb + 1) * 128],
                                            rhs=qT[c][:, h, c0:T], start=True, stop=far),
                   r=[("KT", i), ("qT", h, c)], w=[("b", sb)])
            if not far:
                d0 = ti * 4 + c0 // 128 - kb
                w0 = d0 * 128
                self.O("pe", lambda e: e.matmul(b.banks[sb][:, c0:T], lhsT=b.ident[:], rhs=b.bias8[:, h, w0:w0 + (T - c0)],
                                                start=False, stop=True),
                       r=["c1", "bias8"], w=[("b", sb)])

        def post(n):
            h, c, kb = its[n]
            c0 = max(0, kb - ti * 4) * 128
            sb = n % 3
            pt = PT[n % 4]
            ptt = ("PT", n % 4)
            far = kb <= ti * 4 - 2
            if far:
                self.O("act", lambda e: e.activation(out=pt[:], in_=b.banks[sb][:], func=AF.Exp, bias=b.col("b31", h),
                                                     scale=0.125), r=[("b", sb), "cv"], w=[ptt])
            else:
                self.O("act", lambda e: e.activation(out=pt[:, c0:T], in_=b.banks[sb][:, c0:T], func=AF.Exp, scale=0.125),
                       r=[("b", sb)], w=[ptt])

        def pv(n):
            h, c, kb = its[n]
            i = h % 2
            c0 = max(0, kb - ti * 4) * 128
            pt = PT[n % 4]
            ptt = ("PT", n % 4)
            self.O("pe", lambda e: e.matmul(b.banks[3 + c][:, c0:T], lhsT=VV[i][:, kb, :], rhs=pt[:, c0:T],
                                            start=(kb == 0), stop=(kb == nkb - 1)),
                   r=[("VV", i), ptt], w=[("b", 3 + c)])
            self.O("pe", lambda e: e.matmul(b.banks[5 + c][:, c0:T], lhsT=b.ones[:], rhs=pt[:, c0:T],
                                            start=(kb == 0), stop=(kb == nkb - 1)),
                   r=["c0", ptt], w=[("b", 5 + c)])

        def epi1(h, c):
            self.O("act", lambda e: e.activation(out=tR[c][:], in_=b.banks[5 + c][:], func=AF.Ln), r=[("b", 5 + c)],
                   w=[("tR", c)])
            self.O("act", lambda e: e.activation(out=tR[c][:], in_=tR[c][:], func=AF.Exp, scale=-1.0), r=[("tR", c)],
                   w=[("tR", c)])
            self.O("dve", lambda e: e.tensor_tensor(out=tAB[c][:], in0=b.banks[3 + c][:], in1=tR[c][:], op=ALU.mult),
                   r=[("b", 3 + c), ("tR", c)], w=[("tAB", c)])
            if c == 1:
                self.O("dve", lambda e: e.scalar_tensor_tensor(out=tC[:], in0=tAB[1][:], scalar=b.smallc[:, 0:1],
                                                               in1=tAB[0][:], op0=ALU.mult, op1=ALU.add),
                       r=[("tAB", 0), ("tAB", 1), "neglam"], w=["tC"])
                self.O("act", lambda e: e.activation(out=osq[:], in_=tC[:], func=AF.Square), r=["tC"], w=[("ksq", 0)])

        def epi2(h):
            self.O("pe", lambda e: e.matmul(b.banks[7][:], lhsT=b.ones[:], rhs=osq[:], start=True, stop=True),
                   r=[("ksq", 0), "c0"], w=[("b", 7)])
            self.rstd(tD[:], "tD", 7, 1.0 / 128)
            self.O("dve", lambda e: e.scalar_tensor_tensor(out=oT[:, h, :], in0=tC[:], scalar=b.smallc[:, 1:2], in1=tD[:],
                                                           op0=ALU.mult, op1=ALU.mult),
                   r=["tC", "tD", "slg2"], w=[("oT", h)])

        def run_deferred(n):
            keep = []
            for at, fn in deferred:
                if at <= n:
                    fn()
                else:
                    keep.append((at, fn))
            deferred[:] = keep

        loads(0)
        if 1 < 8:
            loads(1)
        qk(0)
        if N > 1:
            qk(1)
        for n in range(N):
            h, c, kb = its[n]
            post(n)
            run_deferred(n)
            pv(n)
            if n + 2 < N:
                h2, c2, kb2 = its[n + 2]
                qk(n + 2)
            if kb == nkb - 1:
                deferred.append((n + 1, lambda h=h, c=c: epi1(h, c)))
                if c == 1:
                    deferred.append((n + 5, lambda h=h: epi2(h)))
                    if h + 2 < 8:
                        deferred.append((n, lambda h=h: loads(h + 2)))
        run_deferred(10 ** 9)
        for blk in range(2):
            slot = self.ws_next("ao", blk)
            for g in range(4):
                j = blk * 4 + g
                pb = self.proj(slot, 8, 512, g * 128, oT, "oT")
                self.O("dve", lambda e, j=j, pb=pb: e.tensor_tensor(out=b.x[:, j, :], in0=b.banks[pb][:], in1=b.x[:, j, :],
                                                                    op=ALU.add),
                       r=[("b", pb), ("x", j)], w=[("x", j)])

    def build(self):
        s = self.s
        ph = self.phases
        self.setup()
        s.barrier()
        cast_plan = {"M": [], "F": [], "K": [], "A": [], "G": []}
        order = [p for p in "MFKAG" if p in ph]
        wof = {"M": ["in", "dt", "out"], "F": ["up0", "dn0"], "K": ["kv"], "A": ["q", "ao"], "G": ["up1", "dn1"]}
        if order:
            self.casts(wof[order[0]])
            for a_, b_ in zip(order, order[1:]):
                cast_plan[a_] = wof[b_]
        per_tile = []
        if "M" in ph:
            per_tile += [("in", i) for i in range(10)] + [("dt", 0)] + [("out", i) for i in range(4)]
        if "F" in ph:
            per_tile += [("up0", i) for i in range(11)] + [("dn0", i) for i in range(4)]
        if "K" in ph:
            per_tile += [("kv", i) for i in range(4)]
        if "A" in ph:
            per_tile += [("q", i) for i in range(2)] + [("ao", i) for i in range(2)]
        if "G" in ph:
            per_tile += [("up1", i) for i in range(11)] + [("dn1", i) for i in range(4)]
        self.ws_init(per_tile * (self.nseq * self.ntiles))
        for sq in range(self.nseq):
            for ti in range(self.ntiles):
                first = ti == 0
                self.load_x(sq, ti)
                if "M" in ph:
                    if sq == 0 and ti == 0:
                        self.casts(cast_plan["M"])
                    self.mixer(sq, ti)
                    s.barrier()
                if "F" in ph:
                    if sq == 0 and ti == 0:
                        self.casts(cast_plan["F"])
                    self.ffn(0, first)
                    s.barrier()
                if "K" in ph:
                    if sq == 0 and ti == 0:
                        self.casts(cast_plan["K"])
                    self.kvproj(sq, ti)
                    s.barrier()
                if "A" in ph:
                    if sq == 0 and ti == 0:
                        self.casts(cast_plan["A"])
                    self.attn(sq, ti)
                    s.barrier()
                if "G" in ph:
                    if sq == 0 and ti == 0:
                        self.casts(cast_plan["G"])
                    self.ffn(1, first)
                    s.barrier()
                self.store_x(sq, ti)
        s.emit()


def _t5_bucket_np(n):
    n = np.maximum(n, 0)
    max_exact = 16
    nf = np.maximum(n, 1).astype(np.float32)
    large = max_exact + (np.log(nf / max_exact) / math.log(128 / max_exact) * (32 - max_exact)).astype(np.int32)
    large = np.minimum(large, 31)
    return np.where(n < max_exact, n, large)


def _cols(v):
    return np.ascontiguousarray(v.reshape(-1, 128).T)


def prepare_shared(inp):
    f = np.float32
    sh = {}
    in_w = inp["ssm_in_w"][0]
    sh["w_in"] = _blockify(in_w[:, :5120], 4)
    sh["w_dt"] = np.ascontiguousarray(in_w[:, 5120:5152].reshape(8, 128, 32).transpose(1, 0, 2)).reshape(1, 128, 256)
    sh["w_out"] = _blockify(inp["ssm_out_w"][0], 2)
    for L in range(2):
        up = inp["ffn_up_w"][L]
        order = []
        for blk in range(11):
            order += [2 * blk, 2 * blk + 1, 22 + 2 * blk, 22 + 2 * blk + 1]
        idx = np.concatenate([np.arange(c * 128, (c + 1) * 128) for c in order])
        sh["w_up%d" % L] = _blockify(up[:, idx], 4)
        sh["w_dn%d" % L] = _blockify(inp["ffn_down_w"][L], 2)
    sh["w_kv"] = _blockify(inp["kv_w"], 4)
    sh["w_q"] = _blockify(inp["q_w"][0], 4)
    sh["w_ao"] = _blockify(inp["attn_out_w"][0], 4)
    cv = np.zeros((128, NCV), f)

    def put(name, arr):
        arr = np.asarray(arr, f)
        cv[:, CVOFF[name]:CVOFF[name] + arr.shape[1]] = arr

    put("g_ssm", _cols(inp["ssm_ln_g"][0]))
    put("g_ffn0", _cols(inp["ffn_ln_g"][0]))
    put("g_ffn1", _cols(inp["ffn_ln_g"][1]))
    put("g_kv", _cols(inp["kv_ln_g"]))
    put("g_attn", _cols(inp["attn_ln_g"][0]))
    mcw = inp["ssm_conv_w"][0]
    put("mcw", mcw.reshape(4, 24, 128).transpose(2, 1, 0).reshape(128, 96))
    put("mcb", _cols(inp["ssm_conv_b"][0]))
    for L in range(2):
        fw = inp["ffn_conv_w"][L]
        put("fcw%d" % L, fw.reshape(3, 44, 128).transpose(2, 1, 0).reshape(128, 132))
        put("fcb%d" % L, _cols(inp["ffn_conv_b"][L]))
    put("ng", _cols(inp["ssm_norm_g"][0]))
    put("dsk", _cols(np.repeat(inp["ssm_d"][0], 64)))
    put("kng", np.tile(inp["k_norm_g"], 2).reshape(128, 1))
    put("qng", np.tile(inp["q_norm_g"][0], 2).reshape(128, 1))
    put("slg", inp["subln_g"][0].reshape(128, 1))
    put("b31", np.tile(inp["rel_bias"][31][None, :], (128, 1)))
    put("dtb", np.tile(inp["ssm_dt_bias"][0][None, :], (128, 4)))
    put("alog", np.tile(inp["ssm_a_log"][0][None, :], (128, 4)))
    put("lamv", np.tile(inp["lam_vecs"][0].reshape(1, 256), (128, 1)))
    sh["cvec"] = cv
    k = np.arange(128)[:, None]
    j = np.arange(256)[None, :]
    bucket = _t5_bucket_np(j - k)
    rb = inp["rel_bias"]
    sh["biast"] = np.ascontiguousarray(rb[bucket].transpose(0, 2, 1)).astype(f)
    return sh


_NC_CACHE = {}


def kernel(**inputs):
    inp = {k: np.asarray(v) for k, v in inputs.items()}
    sh = prepare_shared(inp)
    x = inp["x"]
    if "nc" not in _NC_CACHE:
        nc = bass.Bass("TRN2", target_bir_lowering=False)
        Builder(nc).build()
        _NC_CACHE["nc"] = nc
    nc = _NC_CACHE["nc"]
    in_maps = []
    for c in range(NCORES):
        m = dict(sh)
        m["xT"] = np.ascontiguousarray(x[2 * c:2 * c + 2].transpose(0, 2, 1))
        in_maps.append(m)
    res = run_bass_kernel_spmd(nc, in_maps, core_ids=list(range(NCORES)))
    out = np.empty_like(x)
    for c in range(NCORES):
        out[2 * c:2 * c + 2] = res.results[c]["yT"].transpose(0, 2, 1)
    return out
```

```python
import math
from contextlib import ExitStack
import numpy as np
import concourse.bass as bass
import concourse.mybir as mybir
from concourse.bass_utils import run_bass_kernel_spmd

F32 = mybir.dt.float32
BF16 = mybir.dt.bfloat16
AF = mybir.ActivationFunctionType
ALU = mybir.AluOpType
AX = mybir.AxisListType

NCORES = 8
D = 1024
SEQ = 2048
T = 512
NT = SEQ // T
FFN = 2816
DIN = 2048
EPS = 1e-6
LAM_INIT = 0.8 - 0.6 * math.exp(-0.3 * 1)
WSLOT = 5632
NEG = -30000.0

ENGS = ("pe", "act", "dve", "pool", "sp")


class _Op:
    __slots__ = ("id", "eng", "fn", "deps", "dma", "key", "signaled", "count", "pos", "rawdeps")


class Sched:
    def __init__(self, nc):
        self.nc = nc
        self.ops = []
        self.eng_ops = {e: [] for e in ENGS}
        self.last_w = {}
        self.readers = {}
        self.key_count = {}
        self.bar_deps = set()
        self.bar_pending = set()
        self.fence_dmas = []

    def _add(self, eng, fn, r, w, dma, key, fenced):
        op = _Op()
        op.id = len(self.ops)
        op.eng = eng
        op.fn = fn
        op.dma = dma
        op.key = key
        op.signaled = False
        op.count = 0
        op.pos = len(self.eng_ops[eng])
        deps = set()
        raw = set()
        for t in r:
            lw = self.last_w.get(t)
            if lw is not None:
                deps.add(lw)
                raw.add(lw)
        for t in w:
            lw = self.last_w.get(t)
            if lw is not None:
                deps.add(lw)
            for rd in self.readers.get(t, ()):
                deps.add(rd)
        for t in r:
            self.readers.setdefault(t, []).append(op.id)
        for t in w:
            self.last_w[t] = op.id
            self.readers[t] = []
        if dma:
            if fenced:
                deps |= self.bar_deps
                self.fence_dmas.append(op.id)
        elif eng in self.bar_pending:
            deps |= self.bar_deps
            self.bar_pending.discard(eng)
        deps.discard(op.id)
        op.deps = deps
        op.rawdeps = raw
        if dma:
            self.key_count[key] = self.key_count.get(key, 0) + 1
            op.count = self.key_count[key]
        self.ops.append(op)
        self.eng_ops[eng].append(op)
        return op

    def op(self, eng, fn, r=(), w=()):
        return self._add(eng, fn, tuple(r), tuple(w), False, None, False)

    def dma(self, q, fn, r=(), w=(), key=None, fenced=False):
        assert key is not None
        return self._add(q, fn, tuple(r), tuple(w), True, key, fenced)

    def barrier(self):
        deps = set()
        for e in ("pe", "act", "dve", "pool"):
            for op in reversed(self.eng_ops[e]):
                if not op.dma:
                    deps.add(op.id)
                    break
        deps |= set(self.fence_dmas)
        self.fence_dmas = []
        self.bar_deps = deps
        self.bar_pending = {"pe", "act", "dve", "pool"}

    def emit(self):
        nc = self.nc
        ops = self.ops
        need = {}
        for op in ops:
            lst = []
            for d in op.deps:
                p = ops[d]
                if p.dma:
                    lst.append(p)
                elif p.eng != op.eng or op.dma:
                    lst.append(p)
                    p.signaled = True
                else:
                    if op.eng != "pe" and d in op.rawdeps and op.pos - p.pos <= 2:
                        lst.append(p)
                        p.signaled = True
            need[op.id] = lst
        cnt = {e: 0 for e in ENGS}
        for op in ops:
            if not op.dma and op.signaled:
                cnt[op.eng] += 1
                op.count = cnt[op.eng]
        keys = sorted(self.key_count.keys(), key=str)
        with ExitStack() as st:
            esem = {e: st.enter_context(nc.semaphore("s_" + e)) for e in ENGS}
            ksem = {k: st.enter_context(nc.semaphore("d_%d" % i)) for i, k in enumerate(keys)}
            block = st.enter_context(nc.Block())

            def body(ename):
                def _f(e):
                    waited = {}
                    for op in self.eng_ops[ename]:
                        for p in need[op.id]:
                            if p.dma:
                                sem, val, sk = ksem[p.key], 16 * p.count, ("k", p.key)
                            else:
                                sem, val, sk = esem[p.eng], p.count, ("e", p.eng)
                            if waited.get(sk, 0) >= val:
                                continue
                            waited[sk] = val
                            e.wait_ge(sem, val)
                        ins = op.fn(e)
                        if op.dma:
                            ins.then_inc(ksem[op.key], 16)
                        elif op.signaled:
                            ins.then_inc(esem[op.eng], 1)
                    if ename == "sp":
                        for k in keys:
                            e.wait_ge(ksem[k], 16 * self.key_count[k])
                return _f

            block.tensor(body("pe"))
            block.scalar(body("act"))
            block.vector(body("dve"))
            block.gpsimd(body("pool"))
            block.sync(body("sp"))


WSPEC = {
    "in": (8, 4, 10),
    "dt": (8, None, 1),
    "out": (16, 2, 4),
    "up0": (8, 4, 11),
    "dn0": (22, 2, 4),
    "kv": (8, 4, 4),
    "q": (8, 4, 2),
    "ao": (8, 4, 2),
    "up1": (8, 4, 11),
    "dn1": (22, 2, 4),
}
WORDER = ["in", "dt", "out", "up0", "dn0", "kv", "q", "ao", "up1", "dn1"]


def _welems(name):
    kc, g, nb = WSPEC[name]
    return kc * (32 if g is None else g * 128)


def _blockify(w, g):
    K, N = w.shape
    kc = K // 128
    nb = N // (g * 128)
    a = w.reshape(kc, 128, nb, g * 128).transpose(2, 1, 0, 3)
    return np.ascontiguousarray(a).reshape(nb, 128, kc * g * 128)


def _cv_layout():
    off = {}
    n = 0

    def add(name, cnt):
        nonlocal n
        off[name] = n
        n += cnt

    add("g_ssm", 8)
    add("g_ffn0", 8)
    add("g_ffn1", 8)
    add("g_kv", 8)
    add("g_attn", 8)
    add("mcw", 24 * 4)
    add("mcb", 24)
    add("fcw0", 44 * 3)
    add("fcb0", 44)
    add("fcw1", 44 * 3)
    add("fcb1", 44)
    add("ng", 16)
    add("dsk", 16)
    add("kng", 1)
    add("qng", 1)
    add("slg", 1)
    add("b31", 8)
    add("dtb", 128)
    add("alog", 128)
    add("lamv", 256)
    return off, n


CVOFF, NCV = _cv_layout()


class Builder:
    def __init__(self, nc, ntiles=NT, nseq=2, phases="MFKAG", dbg=None):
        self.nc = nc
        self.s = Sched(nc)
        self.ntiles = ntiles
        self.nseq = nseq
        self.phases = phases
        self.dbg = dbg
        self._uid = 0
        self.dram()
        self.sbuf()

    def dram(self):
        nc = self.nc
        self.xT = nc.dram_tensor("xT", [2, D, SEQ], F32, kind="ExternalInput").ap()
        self.yT = nc.dram_tensor("yT", [2, D, SEQ], F32, kind="ExternalOutput").ap()
        self.cv_d = nc.dram_tensor("cvec", [128, NCV], F32, kind="ExternalInput").ap()
        self.bt_d = nc.dram_tensor("biast", [128, 8, 256], F32, kind="ExternalInput").ap()
        self.w32 = {}
        self.wbf = {}
        for n in WORDER:
            kc, g, nb = WSPEC[n]
            el = _welems(n)
            self.w32[n] = nc.dram_tensor("w_" + n, [nb, 128, el], F32, kind="ExternalInput").ap()
            self.wbf[n] = nc.dram_tensor("wb_" + n, [nb, 128, el], BF16).ap()
        self.kT_d = nc.dram_tensor("kT_s", [2, 8, 128, SEQ], BF16).ap()
        self.v_d = nc.dram_tensor("v_s", [2, 8, 128, SEQ // 128, 128], BF16).ap()

    def sb(self, name, shape, dt):
        return self.nc.alloc_sbuf_tensor(name, list(shape), dt)

    def sbuf(self):
        nc = self.nc
        self.x = self.sb("x", [128, 8, T], F32)
        self.xn = self.sb("xn", [128, 8, T], BF16)
        self.xsq = self.sb("xsq", [128, 8, T], BF16)
        self.rs = self.sb("rs", [128, T], F32)
        self.wsl = [self.sb("wsl%d" % i, [128, WSLOT], BF16) for i in range(3)]
        self.cv = self.sb("cv", [128, NCV], F32)
        self.ident = self.sb("ident", [128, 128], BF16)
        self.ones = self.sb("ones", [128, 128], BF16)
        self.bones = self.sb("bones", [128, 128], BF16)
        self.onesf = self.sb("onesf", [128, 128], F32)
        self.tri = self.sb("tri", [128, 128], F32)
        self.trib = self.sb("trib", [128, 128], BF16)
        self.ustr = self.sb("ustr", [128, 128], BF16)
        self.maskneg = self.sb("maskneg", [128, 128], F32)
        self.bias8 = self.sb("bias8", [128, 8, 640], BF16)
        self.ones3 = self.sb("ones3", [128, 384], BF16)
        self.smallc = self.sb("smallc", [128, 64], F32)
        self.Ab = self.sb("Ab", [128, 128], F32)
        self.dgs = [self.sb("dg%d" % i, [128, 128], BF16) for i in range(8)]
        self.uext = [self.sb("uext%d" % i, [128, T + 4], BF16) for i in range(3)]
        self.halo_m = self.sb("halo_m", [128, 24, 4], BF16)
        self.halo_f = [self.sb("halo_f%d" % l, [128, 44, 2], BF16) for l in range(2)]
        ARENA = 52000
        self.arena = self.sb("arena", [128, ARENA], BF16)
        self.S = self.sb("S", [128, DIN], F32)
        self.Sb = self.sb("Sb", [128, DIN], BF16)
        self.banks = [nc.alloc_psum_tensor("bank%d" % i, [128, 512], F32) for i in range(8)]

    def carve(self, off, nelem_bf16, dt=BF16):
        ap = self.arena[:, off:off + nelem_bf16]
        if dt == F32:
            ap = ap.bitcast(F32)
        return ap

    def uid(self):
        self._uid += 1
        return self._uid

    def dump(self, name, ap, rtoks, shape, dt=F32):
        if not self.dbg:
            return
        d = self.nc.dram_tensor("dbg_" + name, list(shape), dt, kind="ExternalOutput").ap()
        self.s.dma("sp", lambda e: e.dma_start(out=d, in_=ap), r=rtoks, key=("dbg", name))

    def O(self, eng, fn, r=(), w=()):
        return self.s.op(eng, fn, r, w)

    @staticmethod
    def bc_last(ap, n):
        return bass.AP(ap.tensor, ap.offset, [list(ap.ap[0]), list(ap.ap[1]), [0, n]])

    @staticmethod
    def bc_mid(ap, n):
        return bass.AP(ap.tensor, ap.offset, [list(ap.ap[0]), [0, n], list(ap.ap[1])])

    def col(self, name, i=0):
        o = CVOFF[name] + i
        return self.cv[:, o:o + 1]

    def setup(self):
        s = self.s
        b = self
        s.dma("sp", lambda e: e.dma_start(out=b.cv[:], in_=b.cv_d), w=["cv"], key="cvl")
        b.biast = self.carve(0, 8 * 256 * 2, F32).rearrange("p (h j) -> p h j", j=256)
        s.dma("sp", lambda e: e.dma_start(out=b.biast, in_=b.bt_d), w=["biast"], key="btl")
        P = "pool"

        def sel(t, pattern, op, base, cm, fill=0.0):
            return lambda e: e.affine_select(out=t[:], in_=t[:], pattern=pattern, compare_op=op, fill=fill,
                                             base=base, channel_multiplier=cm)
        for t in (b.ident, b.ones, b.onesf, b.tri, b.trib, b.ustr):
            self.O(P, lambda e, t=t: e.memset(t[:], 1.0), w=["c0"])
        self.O(P, lambda e: e.memset(b.maskneg[:], 0.0), w=["c0"])
        self.O(P, lambda e: e.memset(b.bones[:], 0.0), w=["c0"])
        self.O(P, lambda e: e.memset(b.bones[0:64, 0:64], 1.0), w=["c0"])
        self.O(P, lambda e: e.memset(b.bones[64:128, 64:128], 1.0), w=["c0"])
        self.O(P, sel(b.ident, [[1, 128]], ALU.is_equal, 0, -1), r=["c0"], w=["c1"])
        self.O(P, sel(b.tri, [[1, 128]], ALU.is_ge, 0, -1), r=["c0"], w=["c1"])
        self.O(P, sel(b.trib, [[1, 128]], ALU.is_ge, 0, -1), r=["c0"], w=["c1"])
        self.O(P, sel(b.ustr, [[-1, 128]], ALU.is_ge, -1, 1), r=["c0"], w=["c1"])
        self.O(P, sel(b.maskneg, [[1, 128]], ALU.is_ge, 0, -1, fill=NEG), r=["c0"], w=["c1"])
        self.O(P, lambda e: e.memset(b.halo_m[:], 0.0), w=["halo_m"])
        for l in range(2):
            self.O(P, lambda e, l=l: e.memset(b.halo_f[l][:], 0.0), w=[("halo_f", l)])
        for h in range(8):
            self.O("dve", lambda e, h=h: e.tensor_tensor(out=b.biast[:, h, 0:128], in0=b.biast[:, h, 0:128],
                                                         in1=b.maskneg[:], op=ALU.add),
                   r=["biast", "c1"], w=["biast"])
        self.O(P, lambda e: e.memset(b.ones3[:], 1.0), w=["ones3"])
        for h in range(8):
            self.O("dve", lambda e, h=h: e.tensor_scalar(out=b.bias8[:, h, 0:256], in0=b.biast[:, h, :], scalar1=8.0,
                                                         scalar2=None, op0=ALU.mult), r=["biast"], w=["bias8"])
            self.O("dve", lambda e, h=h: e.tensor_scalar(out=b.bias8[:, h, 256:640], in0=b.ones3[:], scalar1=b.col("b31", h),
                                                         scalar2=8.0, op0=ALU.mult, op1=ALU.mult),
                   r=["ones3", "cv"], w=["bias8"])
        ao = CVOFF["alog"]
        self.O("act", lambda e: e.activation(out=b.Ab[:], in_=b.cv[:, ao:ao + 128], func=AF.Exp), r=["cv"], w=["Ab"])
        self.O("dve", lambda e: e.tensor_scalar(out=b.Ab[:], in0=b.Ab[:], scalar1=-1.0, scalar2=None, op0=ALU.mult),
               r=["Ab"], w=["Ab"])
        lo = CVOFF["lamv"]
        sc = b.smallc
        self.tmp64 = self.sb("tmp64", [128, 128], F32)
        self.O("dve", lambda e: e.tensor_tensor(out=b.tmp64[:, 0:64], in0=b.cv[:, lo:lo + 64], in1=b.cv[:, lo + 64:lo + 128],
                                                op=ALU.mult), r=["cv"], w=["t64a"])
        self.O("dve", lambda e: e.tensor_tensor(out=b.tmp64[:, 64:128], in0=b.cv[:, lo + 128:lo + 192],
                                                in1=b.cv[:, lo + 192:lo + 256], op=ALU.mult), r=["cv"], w=["t64b"])
        self.O("dve", lambda e: e.reduce_sum(out=sc[:, 2:3], in_=b.tmp64[:, 0:64], axis=AX.X), r=["t64a"], w=["sc2"])
        self.O("dve", lambda e: e.reduce_sum(out=sc[:, 3:4], in_=b.tmp64[:, 64:128], axis=AX.X), r=["t64b"], w=["sc3"])
        self.O("act", lambda e: e.activation(out=sc[:, 4:6], in_=sc[:, 2:4], func=AF.Exp), r=["sc2", "sc3"], w=["sc4"])
        self.O("dve", lambda e: e.scalar_tensor_tensor(out=sc[:, 0:1], in0=sc[:, 5:6], scalar=-LAM_INIT, in1=sc[:, 4:5],
                                                       op0=ALU.add, op1=ALU.subtract), r=["sc4"], w=["neglam"])
        self.O("pool", lambda e: e.memset(sc[:, 6:7], EPS), w=["epsc"])
        self.O("dve", lambda e: e.tensor_scalar(out=sc[:, 1:2], in0=b.col("slg"), scalar1=(1.0 - LAM_INIT), scalar2=None,
                                                op0=ALU.mult), r=["cv"], w=["slg2"])

    def casts(self, names):
        s = self.s
        i = getattr(self, "_cast_i", 0)
        for n in names:
            kc, g, nb = WSPEC[n]
            el = _welems(n)
            cb = 2048 if el % 2048 == 0 else (1024 if el % 1024 == 0 else (el if el <= 2048 else 1408))
            assert el % cb == 0, (n, el, cb)
            for blk in range(nb):
                s.dma("pool",
                      lambda e, n=n, blk=blk, cb=cb: e.dma_start(
                          out=self.wbf[n][blk].rearrange("p (a b) -> p a b", b=cb),
                          in_=self.w32[n][blk].rearrange("p (a b) -> p a b", b=cb)),
                      r=[], w=[("wd", n, blk), ("castslot", i % 8)], key=("cast", n, blk))
                i += 1
        self._cast_i = i

    def ws_init(self, seq):
        self.wseq = seq
        self.wi = 0
        self.wissued = 0

    def ws_issue(self, idx):
        n, blk = self.wseq[idx]
        slot = idx % 3
        el = _welems(n)
        self.s.dma("sp", lambda e: e.dma_start(out=self.wsl[slot][:, 0:el], in_=self.wbf[n][blk]),
                   r=[("wd", n, blk)], w=[("ws", slot)], key=("ws", slot))

    def ws_next(self, n, blk):
        idx = self.wi
        assert self.wseq[idx] == (n, blk), (self.wseq[idx], n, blk)
        while self.wissued <= min(idx + 2, len(self.wseq) - 1):
            self.ws_issue(self.wissued)
            self.wissued += 1
        self.wi += 1
        return idx % 3

    def norm(self, gname):
        b = self
        for kc in range(8):
            self.O("act", lambda e, kc=kc: e.activation(out=b.xsq[:, kc, :], in_=b.x[:, kc, :], func=AF.Square),
                   r=[("x", kc)], w=[("xsq", kc)])
        for kc in range(8):
            self.O("pe", lambda e, kc=kc: e.matmul(b.banks[6][:], lhsT=b.ones[:], rhs=b.xsq[:, kc, :],
                                                   start=(kc == 0), stop=(kc == 7)),
                   r=[("xsq", kc), "c0"], w=[("b", 6)])
        self.rstd(b.rs[:], "rs", 6, 1.0 / D)
        for kc in range(8):
            self.O("dve", lambda e, kc=kc: e.scalar_tensor_tensor(out=b.xn[:, kc, :], in0=b.x[:, kc, :],
                                                                  scalar=b.col(gname, kc), in1=b.rs[:],
                                                                  op0=ALU.mult, op1=ALU.mult),
                   r=[("x", kc), "rs", "cv"], w=[("xn", kc)])

    def rstd(self, out, tok, bank, scale):
        b = self
        self.O("act", lambda e: e.activation(out=out, in_=b.banks[bank][:], func=AF.Ln, bias=b.smallc[:, 6:7], scale=scale),
               r=[("b", bank), "epsc"], w=[tok])
        self.O("act", lambda e: e.activation(out=out, in_=out, func=AF.Exp, scale=-0.5), r=[tok], w=[tok])

    def pipeline(self, tasks):
        res = {}
        if tasks:
            res[0] = tasks[0][0]()
        for i in range(len(tasks)):
            if i + 1 < len(tasks):
                res[i + 1] = tasks[i + 1][0]()
            tasks[i][1](res.pop(i))

    def pbank(self):
        self._pb = (getattr(self, "_pb", -1) + 1) % 4
        return self._pb

    def cbank(self):
        self._cb = (getattr(self, "_cb", -1) + 1) % 2
        return 4 + self._cb

    def dg(self):
        self._dg = (getattr(self, "_dg", -1) + 1) % 8
        return self._dg

    def ue(self):
        self._ue = (getattr(self, "_ue", -1) + 1) % 3
        return self._ue

    def proj(self, slot, KC, stride, c0, act, acttok):
        b = self
        pb = self.pbank()
        for kc in range(KC):
            self.O("pe", lambda e, kc=kc: e.matmul(b.banks[pb][:], lhsT=b.wsl[slot][:, kc * stride + c0:kc * stride + c0 + 128],
                                                   rhs=act[:, kc, :], start=(kc == 0), stop=(kc == KC - 1)),
                   r=[("ws", slot), (acttok, kc)], w=[("b", pb)])
        return pb

    def conv(self, pb, K, halo, htok, cj, wname, first):
        b = self
        u = self.ue()
        ut = ("uext", u)
        H = K - 1
        ub = b.uext[u]
        if first:
            self.O("dve", lambda e: e.memset(ub[:, 0:H], 0.0), w=[ut])
        else:
            self.O("dve", lambda e: e.tensor_copy(out=ub[:, 0:H], in_=halo[:, cj, 0:H]), r=[htok], w=[ut])
        self.O("act", lambda e: e.activation(out=ub[:, H:H + T], in_=b.banks[pb][:], func=AF.Identity),
               r=[("b", pb)], w=[ut])
        self.O("dve", lambda e: e.tensor_copy(out=halo[:, cj, 0:H], in_=ub[:, T:T + H]), r=[ut], w=[htok])
        cb = self.cbank()
        for k in range(K):
            d = self.dg()
            self.O("dve", lambda e, d=d, k=k: e.tensor_scalar(out=b.dgs[d][:], in0=b.ident[:], scalar1=b.col(wname, cj * K + k),
                                                              scalar2=None, op0=ALU.mult),
                   r=["c1", "cv"], w=[("dg", d)])
            self.O("pe", lambda e, d=d, k=k: e.matmul(b.banks[cb][:], lhsT=b.dgs[d][:], rhs=ub[:, k:k + T],
                                                      start=(k == 0), stop=(k == K - 1)),
                   r=[("dg", d), ut], w=[("b", cb)])
        return cb

    def ffn(self, L, first):
        b = self
        self.norm("g_ffn%d" % L)
        self.s.barrier()
        h = self.carve(0, 22 * T).rearrange("p (a t) -> p a t", t=T)
        up = "up%d" % L
        dn = "dn%d" % L
        tasks = []
        slots = {}
        for blk in range(11):
            for g in range(4):
                is_gate = g < 2
                hj = 2 * blk + (g % 2)
                cj = hj if is_gate else 22 + hj

                def pj(blk=blk, g=g):
                    if g == 0:
                        slots[blk] = self.ws_next(up, blk)
                    return self.proj(slots[blk], 8, 512, g * 128, b.xn, "xn")

                def post(pb, is_gate=is_gate, hj=hj, cj=cj):
                    cb = self.conv(pb, 3, b.halo_f[L], ("halo_f", L, cj), cj, "fcw%d" % L, first)
                    bias = b.col("fcb%d" % L, cj)
                    if is_gate:
                        self.O("act", lambda e: e.activation(out=h[:, hj, :], in_=b.banks[cb][:], func=AF.Silu, bias=bias),
                               r=[("b", cb), "cv"], w=[("h", hj)])
                    else:
                        self.O("dve", lambda e: e.scalar_tensor_tensor(
                            out=h[:, hj, :], in0=b.banks[cb][:], scalar=bias, in1=h[:, hj, :], op0=ALU.add, op1=ALU.mult),
                            r=[("b", cb), "cv", ("h", hj)], w=[("h", hj)])
                tasks.append((pj, post))
        self.pipeline(tasks)
        for blk in range(4):
            slot = self.ws_next(dn, blk)
            for g in range(2):
                j = blk * 2 + g
                pb = self.pbank()
                for fc in range(22):
                    self.O("pe", lambda e, fc=fc, pb=pb, g=g, slot=slot: e.matmul(
                        b.banks[pb][:], lhsT=b.wsl[slot][:, fc * 256 + g * 128:fc * 256 + g * 128 + 128],
                        rhs=h[:, fc, :], start=(fc == 0), stop=(fc == 21)),
                        r=[("ws", slot), ("h", fc)], w=[("b", pb)])
                self.O("dve", lambda e, j=j, pb=pb: e.tensor_tensor(out=b.x[:, j, :], in0=b.banks[pb][:], in1=b.x[:, j, :],
                                                                    op=ALU.add),
                       r=[("b", pb), ("x", j)], w=[("x", j)])

    def load_x(self, sq, ti):
        b = self
        src = b.xT[sq].rearrange("(kc p) t -> p kc t", p=128)[:, :, ti * T:(ti + 1) * T]
        self.s.dma("sp", lambda e: e.dma_start(out=b.x[:], in_=src), w=[("x", kc) for kc in range(8)], key="xl")

    def store_x(self, sq, ti):
        b = self
        dst = b.yT[sq].rearrange("(kc p) t -> p kc t", p=128)[:, :, ti * T:(ti + 1) * T]
        self.s.dma("sp", lambda e: e.dma_start(out=dst, in_=b.x[:]), r=[("x", kc) for kc in range(8)], key="xs")

    def mixer(self, sq, ti):
        b = self
        first = ti == 0
        self.norm("g_ssm")
        self.s.barrier()
        v3 = lambda off, n: self.carve(off, n * T).rearrange("p (a t) -> p a t", t=T)
        zs = v3(0, 16)
        xbc = v3(8192, 24)
        yg = zs
        xdt = self.carve(28672, 2048)
        xdtw = self.carve(30720, 2048)
        btok = self.carve(32768, 512).rearrange("p (g n) -> p g n", n=128)
        Ah = [self.carve(20480 + i * 1024, 1024).rearrange("p (h l) -> p h l", l=128) for i in range(4)]
        E = [self.carve(24576 + i * 1024, 1024).rearrange("p (h l) -> p h l", l=128) for i in range(4)]
        M = [self.carve(33280 + i * 1024, 1024).rearrange("p (h l) -> p h l", l=128) for i in range(4)]
        cbm = self.carve(39424, 512).rearrange("p (g l) -> p g l", l=128)
        ytok = self.carve(39936, 2048)
        t1 = [self.carve(41984 + i * 1024, 1024, F32) for i in range(2)]
        tmpf = [self.carve(44032 + i * 256, 256, F32) for i in range(2)]
        sm = [self.carve(44544 + i * 256, 256, F32) for i in range(8)]
        tmpf2 = [self.carve(46592 + i * 2048, 2048, F32) for i in range(2)]
        dtr, dtt, aa, e1 = sm[0], sm[1], sm[2], sm[3]
        acs_sb, eacs, wdec, dec = sm[4], sm[5], sm[6], sm[7]
        bf = lambda i: b.banks[i][:].bitcast(BF16)
        if first:
            self.O("dve", lambda e: e.memset(b.S[:], 0.0), w=["S"])
            self.O("dve", lambda e: e.memset(b.Sb[:], 0.0), w=["Sb"])
        tasks = []
        slots = {}
        for blk in range(10):
            for g in range(4):
                j = blk * 4 + g

                def pj(blk=blk, g=g):
                    if g == 0:
                        slots[blk] = self.ws_next("in", blk)
                    return self.proj(slots[blk], 8, 512, g * 128, b.xn, "xn")

                def post(pb, j=j):
                    if j < 16:
                        self.O("act", lambda e: e.activation(out=zs[:, j, :], in_=b.banks[pb][:], func=AF.Silu),
                               r=[("b", pb)], w=[("zs", j)])
                    else:
                        cj = j - 16
                        cb = self.conv(pb, 4, b.halo_m, ("halo_m", cj), cj, "mcw", first)
                        self.O("act", lambda e: e.activation(out=xbc[:, cj, :], in_=b.banks[cb][:], func=AF.Silu,
                                                             bias=b.col("mcb", cj)),
                               r=[("b", cb), "cv"], w=[("xbc", cj)])
                tasks.append((pj, post))
        self.pipeline(tasks)
        slot = self.ws_next("dt", 0)
        for c in range(4):
            for kc in range(8):
                self.O("pe", lambda e, c=c, kc=kc, slot=slot: e.matmul(b.banks[7][:, c * 32:(c + 1) * 32],
                                                            lhsT=b.xn[:, kc, c * 128:(c + 1) * 128],
                                                            rhs=b.wsl[slot][:, kc * 32:(kc + 1) * 32],
                                                            start=(kc == 0), stop=(kc == 7)),
                       r=[("ws", slot), ("xn", kc)], w=[("b", 7)])
        do = CVOFF["dtb"]
        self.O("dve", lambda e: e.tensor_tensor(out=dtr[:], in0=b.banks[7][:, 0:128], in1=b.cv[:, do:do + 128], op=ALU.add),
               r=[("b", 7), "cv"], w=["dtr"])
        self.O("act", lambda e: e.activation(out=e1[:], in_=dtr[:], func=AF.Exp), r=["dtr"], w=["e1"])
        self.O("act", lambda e: e.activation(out=dtt[:], in_=e1[:], func=AF.Ln, bias=1.0), r=["e1"], w=["dt"])
        self.O("dve", lambda e: e.tensor_tensor(out=aa[:], in0=dtt[:], in1=b.Ab[:], op=ALU.mult), r=["dt", "Ab"], w=["aa"])
        self.dump("dtt", dtt[:], ["dt"], [128, 128])
        self.dump("dtr", dtr[:], ["dtr"], [128, 128])
        self.dump("x0", xbc[:, 0, :], [("xbc", 0)], [128, 512], BF16)
        self.dump("B0", xbc[:, 16, :], [("xbc", 16)], [128, 512], BF16)
        self.dump("z0", zs[:, 0, :], [("zs", 0)], [128, 512], BF16)
        def chunk(c):
            cs = slice(c * 128, (c + 1) * 128)
            a_c = aa[:, c * 32:(c + 1) * 32]
            dt_c = dtt[:, c * 32:(c + 1) * 32]
            self.O("pe", lambda e, a_c=a_c: e.matmul(b.banks[6][:, 0:32], lhsT=b.tri[:], rhs=a_c, start=True, stop=True),
                   r=["aa", "c1"], w=[("b", 6)])
            self.O("pe", lambda e, a_c=a_c: e.matmul(b.banks[6][:, 32:64], lhsT=b.onesf[:], rhs=a_c, start=True, stop=True),
                   r=["aa", "c0"], w=[("b", 6)])
            self.O("act", lambda e: e.activation(out=eacs[:, 0:32], in_=b.banks[6][:, 0:32], func=AF.Exp),
                   r=[("b", 6)], w=["eacs"])
            self.O("dve", lambda e: e.tensor_copy(out=acs_sb[:, 0:32], in_=b.banks[6][:, 0:32]), r=[("b", 6)], w=["acs"])
            self.O("dve", lambda e: e.tensor_tensor(out=wdec[:, 0:32], in0=b.banks[6][:, 32:64], in1=acs_sb[:, 0:32],
                                                    op=ALU.subtract), r=[("b", 6), "acs"], w=["wd"])
            self.O("act", lambda e: e.activation(out=wdec[:, 0:32], in_=wdec[:, 0:32], func=AF.Exp), r=["wd"], w=["wd"])
            self.O("act", lambda e: e.activation(out=dec[:, 0:32], in_=b.banks[6][:, 32:64], func=AF.Exp),
                   r=[("b", 6)], w=["dec"])
            for j in range(16):
                bk = 4 + j // 8
                self.O("pe", lambda e, j=j, bk=bk: e.transpose(out=bf(bk)[:, (j % 8) * 128:(j % 8 + 1) * 128],
                                                               in_=xbc[:, j, cs], identity=b.ident[:]),
                       r=[("xbc", j), "c1"], w=[("b", bk)])
            for half in range(2):
                self.O("dve", lambda e, half=half: e.tensor_tensor(
                    out=xdt[:, half * 1024:(half + 1) * 1024].rearrange("p (h d) -> p h d", d=64),
                    in0=bf(4 + half)[:, 0:1024].rearrange("p (h d) -> p h d", d=64),
                    in1=self.bc_last(dt_c[:, half * 16:(half + 1) * 16], 64), op=ALU.mult),
                    r=[("b", 4 + half), "dt"], w=[("xdt", 2 * half), ("xdt", 2 * half + 1)])
            self.O("pool", lambda e: e.tensor_tensor(out=xdtw[:].rearrange("p (h d) -> p h d", d=64),
                                                     in0=xdt[:].rearrange("p (h d) -> p h d", d=64),
                                                     in1=self.bc_last(wdec[:, 0:32], 64), op=ALU.mult),
                   r=[("xdt", q) for q in range(4)] + ["wd"], w=[("xdtw", q) for q in range(4)])
            for g in range(4):
                self.O("pe", lambda e, g=g: e.transpose(out=bf(4)[:, g * 128:(g + 1) * 128], in_=xbc[:, 16 + g, cs],
                                                        identity=b.ident[:]),
                       r=[("xbc", 16 + g), "c1"], w=[("b", 4)])
            self.O("act", lambda e: e.activation(out=btok[:].rearrange("p g n -> p (g n)"), in_=bf(4)[:, 0:512],
                                                 func=AF.Identity), r=[("b", 4)], w=["btok"])
            for g in range(4):
                self.O("pe", lambda e, g=g: e.matmul(b.banks[0][:, g * 128:(g + 1) * 128], lhsT=xbc[:, 16 + g, cs],
                                                     rhs=xbc[:, 20 + g, cs], start=True, stop=True),
                       r=[("xbc", 16 + g), ("xbc", 20 + g)], w=[("b", 0)])
            for g in range(4):
                self.O("dve", lambda e, g=g: e.tensor_tensor(out=cbm[:, g, :], in0=b.banks[0][:, g * 128:(g + 1) * 128],
                                                             in1=b.tri[:], op=ALU.mult),
                       r=[("b", 0), "c1"], w=[("cbm", g)])
            for g in range(4):
                self.O("pool", lambda e, g=g: e.tensor_tensor(out=Ah[g][:], in0=self.bc_mid(b.trib[:], 8),
                                                              in1=self.bc_last(a_c[:, g * 8:(g + 1) * 8], 128), op=ALU.mult),
                       r=["c1", "aa"], w=[("Ah", g)])
            for g in range(4):
                self.O("pool", lambda e, g=g: e.tensor_tensor(
                    out=b.S[:, g * 512:(g + 1) * 512].rearrange("p (h d) -> p h d", d=64),
                    in0=b.S[:, g * 512:(g + 1) * 512].rearrange("p (h d) -> p h d", d=64),
                    in1=self.bc_last(dec[:, g * 8:(g + 1) * 8], 64), op=ALU.mult),
                    r=["dec", ("S", g)], w=[("S", g)])
            for g in range(4):
                for i in range(2):
                    self.O("pe", lambda e, i=i, g=g: e.matmul(b.banks[1 + i][:], lhsT=b.ustr[:],
                                                              rhs=Ah[g][:, i * 4:(i + 1) * 4, :].rearrange("p h l -> p (h l)"),
                                                              start=True, stop=True),
                           r=[("Ah", g), "c1"], w=[("b", 1 + i)])
                    self.O("act", lambda e, i=i, g=g: e.activation(
                        out=E[g][:, i * 4:(i + 1) * 4, :].rearrange("p h l -> p (h l)"), in_=b.banks[1 + i][:], func=AF.Exp),
                        r=[("b", 1 + i)], w=[("E", g, i)])
            for g in range(4):
                for hh in range(8):
                    self.O("dve", lambda e, hh=hh, g=g: e.tensor_tensor(out=M[g][:, hh, :], in0=E[g][:, hh, :],
                                                                        in1=cbm[:, g, :], op=ALU.mult),
                           r=[("E", g, hh // 4), ("cbm", g)], w=[("M", g)])
            for g in range(4):
                ob = (3, 1)[g % 2]
                db = (7, 2)[g % 2]
                self.O("pe", lambda e, g=g, ob=ob: e.matmul(b.banks[ob][:], lhsT=xbc[:, 20 + g, cs],
                                                            rhs=b.Sb[:, g * 512:(g + 1) * 512], start=True, stop=True),
                       r=[("xbc", 20 + g), ("Sb", g)], w=[("b", ob)])
                for hh in range(8):
                    h = g * 8 + hh
                    self.O("pe", lambda e, hh=hh, h=h, g=g, db=db: e.matmul(b.banks[db][:, hh * 64:(hh + 1) * 64],
                                                                            lhsT=M[g][:, hh, :],
                                                                            rhs=xdt[:, h * 64:(h + 1) * 64], start=True, stop=True),
                           r=[("M", g), ("xdt", g)], w=[("b", db)])
                tt = t1[g % 2]
                self.O("dve", lambda e, g=g, tt=tt, ob=ob: e.tensor_tensor(
                    out=tt[:].rearrange("p (h d) -> p h d", d=64), in0=b.banks[ob][:].rearrange("p (h d) -> p h d", d=64),
                    in1=self.bc_last(eacs[:, g * 8:(g + 1) * 8], 64), op=ALU.mult),
                    r=[("b", ob), "eacs"], w=[("t1", g % 2)])
                self.O("dve", lambda e, g=g, tt=tt, db=db: e.tensor_tensor(out=ytok[:, g * 512:(g + 1) * 512], in0=b.banks[db][:],
                                                                           in1=tt[:], op=ALU.add),
                       r=[("b", db), ("t1", g % 2)], w=[("ytok", g)])
            for g in range(4):
                sbk = (0, 6)[g % 2]
                self.O("pe", lambda e, g=g, sbk=sbk: e.matmul(b.banks[sbk][:], lhsT=btok[:, g, :], rhs=xdtw[:, g * 512:(g + 1) * 512],
                                                              start=True, stop=True),
                       r=["btok", ("xdtw", g)] + [("cbm", q) for q in range(4)], w=[("b", sbk)])
                self.O("dve", lambda e, g=g, sbk=sbk: e.tensor_tensor(out=b.S[:, g * 512:(g + 1) * 512], in0=b.banks[sbk][:],
                                                                      in1=b.S[:, g * 512:(g + 1) * 512], op=ALU.add),
                       r=[("b", sbk), ("S", g)], w=[("S", g)])
                self.O("act", lambda e, g=g: e.activation(out=b.Sb[:, g * 512:(g + 1) * 512], in_=b.S[:, g * 512:(g + 1) * 512],
                                                          func=AF.Identity), r=[("S", g)], w=[("Sb", g)])
            if c == 0:
                self.dump("ytok0", ytok[:], [("ytok", q) for q in range(4)], [128, 2048], BF16)
                self.dump("eacs0", eacs[:, 0:32], ["eacs"], [128, 32])
                self.dump("acs0", acs_sb[:, 0:32], ["acs"], [128, 32])
                self.dump("S0", b.S[:], [("S", q) for q in range(4)], [128, 2048])
                self.dump("xdt0", xdt[:], [("xdt", q) for q in range(4)], [128, 2048], BF16)
            for j in range(16):
                bk = 4 + j // 8
                self.O("pe", lambda e, j=j, bk=bk: e.transpose(out=bf(bk)[:, (j % 8) * 128:(j % 8 + 1) * 128],
                                                               in_=ytok[:, j * 128:(j + 1) * 128], identity=b.ident[:]),
                       r=[("ytok", j // 4), "c1"], w=[("b", bk)])
            for half in range(2):
                j0 = half * 8
                tf = tmpf2[half]
                self.O("pool", lambda e, j0=j0, tf=tf: e.tensor_tensor(
                    out=tf[:].rearrange("p (j l) -> p j l", l=128), in0=xbc[:, j0:j0 + 8, cs],
                    in1=self.bc_last(b.cv[:, CVOFF["dsk"] + j0:CVOFF["dsk"] + j0 + 8], 128), op=ALU.mult),
                    r=[("xbc", j0 + q) for q in range(8)] + ["cv"], w=[("tmpf", half)])
                self.O("dve", lambda e, half=half, tf=tf: e.tensor_tensor(
                    out=tf[:].rearrange("p (j l) -> p j l", l=128), in0=bf(4 + half)[:, 0:1024].rearrange("p (j l) -> p j l", l=128),
                    in1=tf[:].rearrange("p (j l) -> p j l", l=128), op=ALU.add),
                    r=[("b", 4 + half), ("tmpf", half)], w=[("tmpf", half)])
                self.O("pool", lambda e, j0=j0, tf=tf: e.tensor_tensor(
                    out=yg[:, j0:j0 + 8, cs], in0=tf[:].rearrange("p (j l) -> p j l", l=128), in1=zs[:, j0:j0 + 8, cs],
                    op=ALU.mult),
                    r=[("tmpf", half)] + [("zs", j0 + q) for q in range(8)], w=[("zs", j0 + q) for q in range(8)])
        for c in range(4):
            chunk(c)
        for g in range(4):
            for q in range(4):
                j = g * 4 + q
                self.O("act", lambda e, j=j, q=q: e.activation(out=b.xsq[:, q, :], in_=yg[:, j, :], func=AF.Square),
                       r=[("zs", j)], w=[("xsq", q)])
            for q in range(4):
                self.O("pe", lambda e, q=q: e.matmul(b.banks[6][:], lhsT=b.ones[:], rhs=b.xsq[:, q, :], start=(q == 0),
                                                     stop=(q == 3)), r=[("xsq", q), "c0"], w=[("b", 6)])
            self.rstd(b.rs[:], "rs", 6, 1.0 / 512)
            for q in range(4):
                j = g * 4 + q
                self.O("dve", lambda e, j=j: e.scalar_tensor_tensor(out=yg[:, j, :], in0=yg[:, j, :], scalar=b.col("ng", j),
                                                                    in1=b.rs[:], op0=ALU.mult, op1=ALU.mult),
                       r=[("zs", j), "rs", "cv"], w=[("zs", j)])
        for blk in range(4):
            slot = self.ws_next("out", blk)
            for g2 in range(2):
                j = blk * 2 + g2
                pb = self.proj(slot, 16, 256, g2 * 128, yg, "zs")
                self.O("dve", lambda e, j=j, pb=pb: e.tensor_tensor(out=b.x[:, j, :], in0=b.banks[pb][:], in1=b.x[:, j, :],
                                                                    op=ALU.add),
                       r=[("b", pb), ("x", j)], w=[("x", j)])

    def headnorm(self, pb, gcol, outap, outtok, bufs, i):
        b = self
        ksq, rk = bufs
        self.O("act", lambda e: e.activation(out=ksq[i][:], in_=b.banks[pb][:], func=AF.Square), r=[("b", pb)], w=[("ksq", i)])
        self.O("pe", lambda e: e.matmul(b.banks[6][:], lhsT=b.bones[:], rhs=ksq[i][:], start=True, stop=True),
               r=[("ksq", i), "c0"], w=[("b", 6)])
        self.rstd(rk[i][:], ("rk", i), 6, 1.0 / 64)
        self.O("dve", lambda e: e.scalar_tensor_tensor(out=outap, in0=b.banks[pb][:], scalar=gcol, in1=rk[i][:],
                                                       op0=ALU.mult, op1=ALU.mult),
               r=[("b", pb), ("rk", i), "cv"], w=[outtok])

    def kvproj(self, sq, ti):
        b = self
        s = self.s
        self.norm("g_kv")
        self.s.barrier()
        kt = [self.carve(i * 512, 512) for i in range(2)]
        ksq = [self.carve(1024 + i * 512, 512) for i in range(2)]
        rk = [self.carve(2048 + i * 1024, 1024, F32) for i in range(2)]
        vt = [self.carve(4096 + i * 512, 512) for i in range(2)]
        tasks = []
        slots = {}
        for blk in range(2):
            for g in range(4):
                h = blk * 4 + g

                def pj(blk=blk, g=g):
                    if g == 0:
                        slots[blk] = self.ws_next("kv", blk)
                    return self.proj(slots[blk], 8, 512, g * 128, b.xn, "xn")

                def post(pb, h=h):
                    i = h % 2
                    self.headnorm(pb, b.col("kng"), kt[i][:], ("kt", i), (ksq, rk), i)
                    s.dma("sp", lambda e: e.dma_start(out=b.kT_d[sq, h, :, ti * T:(ti + 1) * T], in_=kt[i][:]),
                          r=[("kt", i)], w=[("kTd", sq, h, ti)], key=("kst", i), fenced=True)
                tasks.append((pj, post))
        self.pipeline(tasks)
        for blk in range(2, 4):
            slot = self.ws_next("kv", blk)
            nh = blk - 2
            for c in range(4):
                pb = self.pbank()
                i = c % 2
                for kc in range(8):
                    self.O("pe", lambda e, kc=kc, c=c, pb=pb, slot=slot: e.matmul(b.banks[pb][:], lhsT=b.xn[:, kc, c * 128:(c + 1) * 128],
                                                                       rhs=b.wsl[slot][:, kc * 512:(kc + 1) * 512],
                                                                       start=(kc == 0), stop=(kc == 7)),
                           r=[("ws", slot), ("xn", kc)], w=[("b", pb)])
                self.O("act", lambda e, i=i, pb=pb: e.activation(out=vt[i][:], in_=b.banks[pb][:], func=AF.Identity),
                       r=[("b", pb)], w=[("vt", i)])
                s.dma("sp", lambda e, c=c, i=i, nh=nh: e.dma_start(
                    out=b.v_d[sq, nh * 4:(nh + 1) * 4, :, ti * 4 + c, :].rearrange("h p d -> p h d"),
                    in_=vt[i][:].rearrange("p (h d) -> p h d", d=128)),
                    r=[("vt", i)], w=[("vd", sq, ti, c, nh)], key=("vst", i), fenced=True)

    def attn(self, sq, ti):
        b = self
        s = self.s
        self.norm("g_attn")
        self.s.barrier()
        v3 = lambda off: self.carve(off, 8 * T).rearrange("p (a t) -> p a t", t=T)
        qT = [v3(0), v3(4096)]
        oT = v3(8192)
        KT = [self.carve(12288 + i * 2048, 2048) for i in range(2)]
        VV = [self.carve(16384 + i * 2048, 2048).rearrange("p (k d) -> p k d", d=128) for i in range(2)]
        PT = [self.carve(20480 + i * 512, 512) for i in range(4)]
        tmp = [self.carve(22528 + i * 1024, 1024, F32) for i in range(10)]
        ksq = [self.carve(32768 + i * 512, 512) for i in range(2)]
        rk = [tmp[8], tmp[9]]
        self.O("pool", lambda e: e.memset(qT[0][64:128, :, :], 0.0), w=[("qT0z",)])
        self.O("pool", lambda e: e.memset(qT[1][0:64, :, :], 0.0), w=[("qT1z",)])
        tasks = []
        slots = {}
        for blk in range(2):
            for g in range(4):
                h = blk * 4 + g

                def pj(blk=blk, g=g):
                    if g == 0:
                        slots[blk] = self.ws_next("q", blk)
                    return self.proj(slots[blk], 8, 512, g * 128, b.xn, "xn")

                def post(pb, h=h):
                    i = h % 2
                    self.O("act", lambda e: e.activation(out=ksq[i][:], in_=b.banks[pb][:], func=AF.Square),
                           r=[("b", pb)], w=[("ksq", i)])
                    self.O("pe", lambda e: e.matmul(b.banks[6][:], lhsT=b.bones[:], rhs=ksq[i][:], start=True, stop=True),
                           r=[("ksq", i), "c0"], w=[("b", 6)])
                    self.rstd(rk[i][:], ("rk", i), 6, 1.0 / 64)
                    for c in range(2):
                        ps_ = slice(c * 64, (c + 1) * 64)
                        self.O("dve", lambda e, c=c, ps_=ps_: e.scalar_tensor_tensor(
                            out=qT[c][ps_, h, :], in0=b.banks[pb][ps_, :], scalar=b.cv[ps_, CVOFF["qng"]:CVOFF["qng"] + 1],
                            in1=rk[i][ps_, :], op0=ALU.mult, op1=ALU.mult),
                            r=[("b", pb), ("rk", i), "cv", ("qT0z",), ("qT1z",)], w=[("qT", h, c)])
                tasks.append((pj, post))
        self.pipeline(tasks)
        nk = (ti + 1) * T
        nkb = nk // 128
        tR = [tmp[2], tmp[3]]
        tAB = [tmp[4], tmp[5]]
        tC, tD = tmp[6], tmp[7]
        osq = ksq[0]
        its = [(h, c, kb) for h in range(8) for c in range(2) for kb in range(nkb)]
        N = len(its)
        ptt_of = {}
        deferred = []

        def loads(h):
            i = h % 2
            s.dma("sp", lambda e: e.dma_start(out=KT[i][:, 0:nk], in_=b.kT_d[sq, h, :, 0:nk]),
                  r=[("kTd", sq, h, t2) for t2 in range(ti + 1)], w=[("KT", i)], key=("kl", i), fenced=True)
            s.dma("sp", lambda e: e.dma_start(
                out=VV[i][:, 0:nkb, :],
                in_=b.v_d[sq, h, :, 0:nkb, :]),
                r=[("vd", sq, t2, c2, h // 4) for t2 in range(ti + 1) for c2 in range(4)], w=[("VV", i)],
                key=("vl", i), fenced=True)

        def qk(n):
            h, c, kb = its[n]
            i = h % 2
            c0 = max(0, kb - ti * 4) * 128
            sb = n % 3
            far = kb <= ti * 4 - 2
            self.O("pe", lambda e: e.matmul(b.banks[sb][:, c0:T], lhsT=KT[i][:, kb * 128:(kb + 1) * 128],
                                            rhs=qT[c][:, h, c0:T], start=True, stop=far),
                   r=[("KT", i), ("qT", h, c)], w=[("b", sb)])
            if not far:
                d0 = ti * 4 + c0 // 128 - kb
                w0 = d0 * 128
                self.O("pe", lambda e: e.matmul(b.banks[sb][:, c0:T], lhsT=b.ident[:], rhs=b.bias8[:, h, w0:w0 + (T - c0)],
                                                start=False, stop=True),
                       r=["c1", "bias8"], w=[("b", sb)])

        def post(n):
            h, c, kb = its[n]
            c0 = max(0, kb - ti * 4) * 128
            sb = n % 3
            pt = PT[n % 4]
            ptt = ("PT", n % 4)
            far = kb <= ti * 4 - 2
            if far:
                self.O("act", lambda e: e.activation(out=pt[:], in_=b.banks[sb][:], func=AF.Exp, bias=b.col("b31", h),
                                                     scale=0.125), r=[("b", sb), "cv"], w=[ptt])
            else:
                self.O("act", lambda e: e.activation(out=pt[:, c0:T], in_=b.banks[sb][:, c0:T], func=AF.Exp, scale=0.125),
                       r=[("b", sb)], w=[ptt])

        def pv(n):
            h, c, kb = its[n]
            i = h % 2
            c0 = max(0, kb - ti * 4) * 128
            pt = PT[n % 4]
            ptt = ("PT", n % 4)
            self.O("pe", lambda e: e.matmul(b.banks[3 + c][:, c0:T], lhsT=VV[i][:, kb, :], rhs=pt[:, c0:T],
                                            start=(kb == 0), stop=(kb == nkb - 1)),
                   r=[("VV", i), ptt], w=[("b", 3 + c)])
            self.O("pe", lambda e: e.matmul(b.banks[5 + c][:, c0:T], lhsT=b.ones[:], rhs=pt[:, c0:T],
                                            start=(kb == 0), stop=(kb == nkb - 1)),
                   r=["c0", ptt], w=[("b", 5 + c)])

        def epi1(h, c):
            self.O("act", lambda e: e.activation(out=tR[c][:], in_=b.banks[5 + c][:], func=AF.Ln), r=[("b", 5 + c)],
                   w=[("tR", c)])
            self.O("act", lambda e: e.activation(out=tR[c][:], in_=tR[c][:], func=AF.Exp, scale=-1.0), r=[("tR", c)],
                   w=[("tR", c)])
            self.O("dve", lambda e: e.tensor_tensor(out=tAB[c][:], in0=b.banks[3 + c][:], in1=tR[c][:], op=ALU.mult),
                   r=[("b", 3 + c), ("tR", c)], w=[("tAB", c)])
            if c == 1:
                self.O("dve", lambda e: e.scalar_tensor_tensor(out=tC[:], in0=tAB[1][:], scalar=b.smallc[:, 0:1],
                                                               in1=tAB[0][:], op0=ALU.mult, op1=ALU.add),
                       r=[("tAB", 0), ("tAB", 1), "neglam"], w=["tC"])
                self.O("act", lambda e: e.activation(out=osq[:], in_=tC[:], func=AF.Square), r=["tC"], w=[("ksq", 0)])

        def epi2(h):
            self.O("pe", lambda e: e.matmul(b.banks[7][:], lhsT=b.ones[:], rhs=osq[:], start=True, stop=True),
                   r=[("ksq", 0), "c0"], w=[("b", 7)])
            self.rstd(tD[:], "tD", 7, 1.0 / 128)
            self.O("dve", lambda e: e.scalar_tensor_tensor(out=oT[:, h, :], in0=tC[:], scalar=b.smallc[:, 1:2], in1=tD[:],
                                                           op0=ALU.mult, op1=ALU.mult),
                   r=["tC", "tD", "slg2"], w=[("oT", h)])

        def run_deferred(n):
            keep = []
            for at, fn in deferred:
                if at <= n:
                    fn()
                else:
                    keep.append((at, fn))
            deferred[:] = keep

        loads(0)
        if 1 < 8:
            loads(1)
        qk(0)
        if N > 1:
            qk(1)
        for n in range(N):
            h, c, kb = its[n]
            post(n)
            run_deferred(n)
            pv(n)
            if n + 2 < N:
                h2, c2, kb2 = its[n + 2]
                qk(n + 2)
            if kb == nkb - 1:
                deferred.append((n + 1, lambda h=h, c=c: epi1(h, c)))
                if c == 1:
                    deferred.append((n + 5, lambda h=h: epi2(h)))
                    if h + 2 < 8:
                        deferred.append((n, lambda h=h: loads(h + 2)))
        run_deferred(10 ** 9)
        for blk in range(2):
            slot = self.ws_next("ao", blk)
            for g in range(4):
                j = blk * 4 + g
                pb = self.proj(slot, 8, 512, g * 128, oT, "oT")
                self.O("dve", lambda e, j=j, pb=pb: e.tensor_tensor(out=b.x[:, j, :], in0=b.banks[pb][:], in1=b.x[:, j, :],
                                                                    op=ALU.add),
                       r=[("b", pb), ("x", j)], w=[("x", j)])

    def build(self):
        s = self.s
        ph = self.phases
        self.setup()
        s.barrier()
        cast_plan = {"M": [], "F": [], "K": [], "A": [], "G": []}
        order = [p for p in "MFKAG" if p in ph]
        wof = {"M": ["in", "dt", "out"], "F": ["up0", "dn0"], "K": ["kv"], "A": ["q", "ao"], "G": ["up1", "dn1"]}
        if order:
            self.casts(wof[order[0]])
            for a_, b_ in zip(order, order[1:]):
                cast_plan[a_] = wof[b_]
        per_tile = []
        if "M" in ph:
            per_tile += [("in", i) for i in range(10)] + [("dt", 0)] + [("out", i) for i in range(4)]
        if "F" in ph:
            per_tile += [("up0", i) for i in range(11)] + [("dn0", i) for i in range(4)]
        if "K" in ph:
            per_tile += [("kv", i) for i in range(4)]
        if "A" in ph:
            per_tile += [("q", i) for i in range(2)] + [("ao", i) for i in range(2)]
        if "G" in ph:
            per_tile += [("up1", i) for i in range(11)] + [("dn1", i) for i in range(4)]
        self.ws_init(per_tile * (self.nseq * self.ntiles))
        for sq in range(self.nseq):
            for ti in range(self.ntiles):
                first = ti == 0
                self.load_x(sq, ti)
                if "M" in ph:
                    if sq == 0 and ti == 0:
                        self.casts(cast_plan["M"])
                    self.mixer(sq, ti)
                if "F" in ph:
                    if sq == 0 and ti == 0:
                        self.casts(cast_plan["F"])
                    self.ffn(0, first)
                if "K" in ph:
                    if sq == 0 and ti == 0:
                        self.casts(cast_plan["K"])
                    self.kvproj(sq, ti)
                if "A" in ph:
                    if sq == 0 and ti == 0:
                        self.casts(cast_plan["A"])
                    self.attn(sq, ti)
                if "G" in ph:
                    if sq == 0 and ti == 0:
                        self.casts(cast_plan["G"])
                    self.ffn(1, first)
                self.store_x(sq, ti)
        s.emit()


def _t5_bucket_np(n):
    n = np.maximum(n, 0)
    max_exact = 16
    nf = np.maximum(n, 1).astype(np.float32)
    large = max_exact + (np.log(nf / max_exact) / math.log(128 / max_exact) * (32 - max_exact)).astype(np.int32)
    large = np.minimum(large, 31)
    return np.where(n < max_exact, n, large)


def _cols(v):
    return np.ascontiguousarray(v.reshape(-1, 128).T)


def prepare_shared(inp):
    f = np.float32
    sh = {}
    in_w = inp["ssm_in_w"][0]
    sh["w_in"] = _blockify(in_w[:, :5120], 4)
    sh["w_dt"] = np.ascontiguousarray(in_w[:, 5120:5152].reshape(8, 128, 32).transpose(1, 0, 2)).reshape(1, 128, 256)
    sh["w_out"] = _blockify(inp["ssm_out_w"][0], 2)
    for L in range(2):
        up = inp["ffn_up_w"][L]
        order = []
        for blk in range(11):
            order += [2 * blk, 2 * blk + 1, 22 + 2 * blk, 22 + 2 * blk + 1]
        idx = np.concatenate([np.arange(c * 128, (c + 1) * 128) for c in order])
        sh["w_up%d" % L] = _blockify(up[:, idx], 4)
        sh["w_dn%d" % L] = _blockify(inp["ffn_down_w"][L], 2)
    sh["w_kv"] = _blockify(inp["kv_w"], 4)
    sh["w_q"] = _blockify(inp["q_w"][0], 4)
    sh["w_ao"] = _blockify(inp["attn_out_w"][0], 4)
    cv = np.zeros((128, NCV), f)

    def put(name, arr):
        arr = np.asarray(arr, f)
        cv[:, CVOFF[name]:CVOFF[name] + arr.shape[1]] = arr

    put("g_ssm", _cols(inp["ssm_ln_g"][0]))
    put("g_ffn0", _cols(inp["ffn_ln_g"][0]))
    put("g_ffn1", _cols(inp["ffn_ln_g"][1]))
    put("g_kv", _cols(inp["kv_ln_g"]))
    put("g_attn", _cols(inp["attn_ln_g"][0]))
    mcw = inp["ssm_conv_w"][0]
    put("mcw", mcw.reshape(4, 24, 128).transpose(2, 1, 0).reshape(128, 96))
    put("mcb", _cols(inp["ssm_conv_b"][0]))
    for L in range(2):
        fw = inp["ffn_conv_w"][L]
        put("fcw%d" % L, fw.reshape(3, 44, 128).transpose(2, 1, 0).reshape(128, 132))
        put("fcb%d" % L, _cols(inp["ffn_conv_b"][L]))
    put("ng", _cols(inp["ssm_norm_g"][0]))
    put("dsk", _cols(np.repeat(inp["ssm_d"][0], 64)))
    put("kng", np.tile(inp["k_norm_g"], 2).reshape(128, 1))
    put("qng", np.tile(inp["q_norm_g"][0], 2).reshape(128, 1))
    put("slg", inp["subln_g"][0].reshape(128, 1))
    put("b31", np.tile(inp["rel_bias"][31][None, :], (128, 1)))
    put("dtb", np.tile(inp["ssm_dt_bias"][0][None, :], (128, 4)))
    put("alog", np.tile(inp["ssm_a_log"][0][None, :], (128, 4)))
    put("lamv", np.tile(inp["lam_vecs"][0].reshape(1, 256), (128, 1)))
    sh["cvec"] = cv
    k = np.arange(128)[:, None]
    j = np.arange(256)[None, :]
    bucket = _t5_bucket_np(j - k)
    rb = inp["rel_bias"]
    sh["biast"] = np.ascontiguousarray(rb[bucket].transpose(0, 2, 1)).astype(f)
    return sh


_NC_CACHE = {}


def kernel(**inputs):
    inp = {k: np.asarray(v) for k, v in inputs.items()}
    sh = prepare_shared(inp)
    x = inp["x"]
    if "nc" not in _NC_CACHE:
        nc = bass.Bass("TRN2", target_bir_lowering=False)
        Builder(nc).build()
        _NC_CACHE["nc"] = nc
    nc = _NC_CACHE["nc"]
    in_maps = []
    for c in range(NCORES):
        m = dict(sh)
        m["xT"] = np.ascontiguousarray(x[2 * c:2 * c + 2].transpose(0, 2, 1))
        in_maps.append(m)
    res = run_bass_kernel_spmd(nc, in_maps, core_ids=list(range(NCORES)))
    out = np.empty_like(x)
    for c in range(NCORES):
        out[2 * c:2 * c + 2] = res.results[c]["yT"].transpose(0, 2, 1)
    return out
```

```python
import math
from contextlib import ExitStack
import numpy as np
import concourse.bass as bass
import concourse.mybir as mybir
from concourse.bass_utils import run_bass_kernel_spmd

F32 = mybir.dt.float32
BF16 = mybir.dt.bfloat16
AF = mybir.ActivationFunctionType
ALU = mybir.AluOpType
AX = mybir.AxisListType

NCORES = 8
D = 1024
SEQ = 2048
T = 512
NT = SEQ // T
FFN = 2816
DIN = 2048
EPS = 1e-6
LAM_INIT = 0.8 - 0.6 * math.exp(-0.3 * 1)
WSLOT = 5632
NEG = -30000.0

ENGS = ("pe", "act", "dve", "pool", "sp")


class _Op:
    __slots__ = ("id", "eng", "fn", "deps", "dma", "key", "signaled", "count", "pos", "rawdeps")


class Sched:
    def __init__(self, nc):
        self.nc = nc
        self.ops = []
        self.eng_ops = {e: [] for e in ENGS}
        self.last_w = {}
        self.readers = {}
        self.key_count = {}
        self.bar_deps = set()
        self.bar_pending = set()
        self.fence_dmas = []

    def _add(self, eng, fn, r, w, dma, key, fenced):
        op = _Op()
        op.id = len(self.ops)
        op.eng = eng
        op.fn = fn
        op.dma = dma
        op.key = key
        op.signaled = False
        op.count = 0
        op.pos = len(self.eng_ops[eng])
        deps = set()
        raw = set()
        for t in r:
            lw = self.last_w.get(t)
            if lw is not None:
                deps.add(lw)
                raw.add(lw)
        for t in w:
            lw = self.last_w.get(t)
            if lw is not None:
                deps.add(lw)
            for rd in self.readers.get(t, ()):
                deps.add(rd)
        for t in r:
            self.readers.setdefault(t, []).append(op.id)
        for t in w:
            self.last_w[t] = op.id
            self.readers[t] = []
        if dma:
            if fenced:
                deps |= self.bar_deps
                self.fence_dmas.append(op.id)
        elif eng in self.bar_pending:
            deps |= self.bar_deps
            self.bar_pending.discard(eng)
        deps.discard(op.id)
        op.deps = deps
        op.rawdeps = raw
        if dma:
            self.key_count[key] = self.key_count.get(key, 0) + 1
            op.count = self.key_count[key]
        self.ops.append(op)
        self.eng_ops[eng].append(op)
        return op

    def op(self, eng, fn, r=(), w=()):
        return self._add(eng, fn, tuple(r), tuple(w), False, None, False)

    def dma(self, q, fn, r=(), w=(), key=None, fenced=False):
        assert key is not None
        return self._add(q, fn, tuple(r), tuple(w), True, key, fenced)

    def barrier(self):
        deps = set()
        for e in ("pe", "act", "dve", "pool"):
            for op in reversed(self.eng_ops[e]):
                if not op.dma:
                    deps.add(op.id)
                    break
        deps |= set(self.fence_dmas)
        self.fence_dmas = []
        self.bar_deps = deps
        self.bar_pending = {"pe", "act", "dve", "pool"}

    def emit(self):
        nc = self.nc
        ops = self.ops
        need = {}
        for op in ops:
            lst = []
            for d in op.deps:
                p = ops[d]
                if p.dma:
                    lst.append(p)
                elif p.eng != op.eng or op.dma:
                    lst.append(p)
                    p.signaled = True
                else:
                    if op.eng != "pe" and d in op.rawdeps and op.pos - p.pos <= 2:
                        lst.append(p)
                        p.signaled = True
            need[op.id] = lst
        cnt = {e: 0 for e in ENGS}
        for op in ops:
            if not op.dma and op.signaled:
                cnt[op.eng] += 1
                op.count = cnt[op.eng]
        keys = sorted(self.key_count.keys(), key=str)
        with ExitStack() as st:
            esem = {e: st.enter_context(nc.semaphore("s_" + e)) for e in ENGS}
            ksem = {k: st.enter_context(nc.semaphore("d_%d" % i)) for i, k in enumerate(keys)}
            block = st.enter_context(nc.Block())

            def body(ename):
                def _f(e):
                    waited = {}
                    for op in self.eng_ops[ename]:
                        for p in need[op.id]:
                            if p.dma:
                                sem, val, sk = ksem[p.key], 16 * p.count, ("k", p.key)
                            else:
                                sem, val, sk = esem[p.eng], p.count, ("e", p.eng)
                            if waited.get(sk, 0) >= val:
                                continue
                            waited[sk] = val
                            e.wait_ge(sem, val)
                        ins = op.fn(e)
                        if op.dma:
                            ins.then_inc(ksem[op.key], 16)
                        elif op.signaled:
                            ins.then_inc(esem[op.eng], 1)
                    if ename == "sp":
                        for k in keys:
                            e.wait_ge(ksem[k], 16 * self.key_count[k])
                return _f

            block.tensor(body("pe"))
            block.scalar(body("act"))
            block.vector(body("dve"))
            block.gpsimd(body("pool"))
            block.sync(body("sp"))


WSPEC = {
    "in": (8, 4, 10),
    "dt": (8, None, 1),
    "out": (16, 2, 4),
    "up0": (8, 4, 11),
    "dn0": (22, 2, 4),
    "kv": (8, 4, 4),
    "q": (8, 4, 2),
    "ao": (8, 4, 2),
    "up1": (8, 4, 11),
    "dn1": (22, 2, 4),
}
WORDER = ["in", "dt", "out", "up0", "dn0", "kv", "q", "ao", "up1", "dn1"]


def _welems(name):
    kc, g, nb = WSPEC[name]
    return kc * (32 if g is None else g * 128)


def _blockify(w, g):
    K, N = w.shape
    kc = K // 128
    nb = N // (g * 128)
    a = w.reshape(kc, 128, nb, g * 128).transpose(2, 1, 0, 3)
    return np.ascontiguousarray(a).reshape(nb, 128, kc * g * 128)


def _cv_layout():
    off = {}
    n = 0

    def add(name, cnt):
        nonlocal n
        off[name] = n
        n += cnt

    add("g_ssm", 8)
    add("g_ffn0", 8)
    add("g_ffn1", 8)
    add("g_kv", 8)
    add("g_attn", 8)
    add("mcw", 24 * 4)
    add("mcb", 24)
    add("fcw0", 44 * 3)
    add("fcb0", 44)
    add("fcw1", 44 * 3)
    add("fcb1", 44)
    add("ng", 16)
    add("dsk", 16)
    add("kng", 1)
    add("qng", 1)
    add("slg", 1)
    add("b31", 8)
    add("dtb", 128)
    add("alog", 128)
    add("lamv", 256)
    return off, n


CVOFF, NCV = _cv_layout()


class Builder:
    def __init__(self, nc, ntiles=NT, nseq=2, phases="MFKAG", dbg=None):
        self.nc = nc
        self.s = Sched(nc)
        self.ntiles = ntiles
        self.nseq = nseq
        self.phases = phases
        self.dbg = dbg
        self._uid = 0
        self.dram()
        self.sbuf()

    def dram(self):
        nc = self.nc
        self.xT = nc.dram_tensor("xT", [2, D, SEQ], F32, kind="ExternalInput").ap()
        self.yT = nc.dram_tensor("yT", [2, D, SEQ], F32, kind="ExternalOutput").ap()
        self.cv_d = nc.dram_tensor("cvec", [128, NCV], F32, kind="ExternalInput").ap()
        self.bt_d = nc.dram_tensor("biast", [128, 8, 256], F32, kind="ExternalInput").ap()
        self.w32 = {}
        self.wbf = {}
        for n in WORDER:
            kc, g, nb = WSPEC[n]
            el = _welems(n)
            self.w32[n] = nc.dram_tensor("w_" + n, [nb, 128, el], F32, kind="ExternalInput").ap()
            self.wbf[n] = nc.dram_tensor("wb_" + n, [nb, 128, el], BF16).ap()
        self.kT_d = nc.dram_tensor("kT_s", [2, 8, 128, SEQ], BF16).ap()
        self.v_d = nc.dram_tensor("v_s", [2, 8, 128, SEQ // 128, 128], BF16).ap()

    def sb(self, name, shape, dt):
        return self.nc.alloc_sbuf_tensor(name, list(shape), dt)

    def sbuf(self):
        nc = self.nc
        self.x = self.sb("x", [128, 8, T], F32)
        self.xn = self.sb("xn", [128, 8, T], BF16)
        self.xsq = self.sb("xsq", [128, 8, T], BF16)
        self.rs = self.sb("rs", [128, T], F32)
        self.wsl = [self.sb("wsl%d" % i, [128, WSLOT], BF16) for i in range(3)]
        self.cv = self.sb("cv", [128, NCV], F32)
        self.ident = self.sb("ident", [128, 128], BF16)
        self.ones = self.sb("ones", [128, 128], BF16)
        self.bones = self.sb("bones", [128, 128], BF16)
        self.onesf = self.sb("onesf", [128, 128], F32)
        self.tri = self.sb("tri", [128, 128], F32)
        self.trib = self.sb("trib", [128, 128], BF16)
        self.ustr = self.sb("ustr", [128, 128], BF16)
        self.maskneg = self.sb("maskneg", [128, 128], F32)
        self.bias8 = self.sb("bias8", [128, 8, 640], BF16)
        self.ones3 = self.sb("ones3", [128, 384], BF16)
        self.smallc = self.sb("smallc", [128, 64], F32)
        self.Ab = self.sb("Ab", [128, 128], F32)
        self.dgs = [self.sb("dg%d" % i, [128, 128], BF16) for i in range(8)]
        self.uext = [self.sb("uext%d" % i, [128, T + 4], BF16) for i in range(3)]
        self.halo_m = self.sb("halo_m", [128, 24, 4], BF16)
        self.halo_f = [self.sb("halo_f%d" % l, [128, 44, 2], BF16) for l in range(2)]
        ARENA = 52000
        self.arena = self.sb("arena", [128, ARENA], BF16)
        self.S = self.sb("S", [128, DIN], F32)
        self.Sb = self.sb("Sb", [128, DIN], BF16)
        self.banks = [nc.alloc_psum_tensor("bank%d" % i, [128, 512], F32) for i in range(8)]

    def carve(self, off, nelem_bf16, dt=BF16):
        ap = self.arena[:, off:off + nelem_bf16]
        if dt == F32:
            ap = ap.bitcast(F32)
        return ap

    def uid(self):
        self._uid += 1
        return self._uid

    def dump(self, name, ap, rtoks, shape, dt=F32):
        if not self.dbg:
            return
        d = self.nc.dram_tensor("dbg_" + name, list(shape), dt, kind="ExternalOutput").ap()
        self.s.dma("sp", lambda e: e.dma_start(out=d, in_=ap), r=rtoks, key=("dbg", name))

    def O(self, eng, fn, r=(), w=()):
        return self.s.op(eng, fn, r, w)

    @staticmethod
    def bc_last(ap, n):
        return bass.AP(ap.tensor, ap.offset, [list(ap.ap[0]), list(ap.ap[1]), [0, n]])

    @staticmethod
    def bc_mid(ap, n):
        return bass.AP(ap.tensor, ap.offset, [list(ap.ap[0]), [0, n], list(ap.ap[1])])

    def col(self, name, i=0):
        o = CVOFF[name] + i
        return self.cv[:, o:o + 1]

    def setup(self):
        s = self.s
        b = self
        s.dma("sp", lambda e: e.dma_start(out=b.cv[:], in_=b.cv_d), w=["cv"], key="cvl")
        b.biast = self.carve(0, 8 * 256 * 2, F32).rearrange("p (h j) -> p h j", j=256)
        s.dma("sp", lambda e: e.dma_start(out=b.biast, in_=b.bt_d), w=["biast"], key="btl")
        P = "pool"

        def sel(t, pattern, op, base, cm, fill=0.0):
            return lambda e: e.affine_select(out=t[:], in_=t[:], pattern=pattern, compare_op=op, fill=fill,
                                             base=base, channel_multiplier=cm)
        for t in (b.ident, b.ones, b.onesf, b.tri, b.trib, b.ustr):
            self.O(P, lambda e, t=t: e.memset(t[:], 1.0), w=["c0"])
        self.O(P, lambda e: e.memset(b.maskneg[:], 0.0), w=["c0"])
        self.O(P, lambda e: e.memset(b.bones[:], 0.0), w=["c0"])
        self.O(P, lambda e: e.memset(b.bones[0:64, 0:64], 1.0), w=["c0"])
        self.O(P, lambda e: e.memset(b.bones[64:128, 64:128], 1.0), w=["c0"])
        self.O(P, sel(b.ident, [[1, 128]], ALU.is_equal, 0, -1), r=["c0"], w=["c1"])
        self.O(P, sel(b.tri, [[1, 128]], ALU.is_ge, 0, -1), r=["c0"], w=["c1"])
        self.O(P, sel(b.trib, [[1, 128]], ALU.is_ge, 0, -1), r=["c0"], w=["c1"])
        self.O(P, sel(b.ustr, [[-1, 128]], ALU.is_ge, -1, 1), r=["c0"], w=["c1"])
        self.O(P, sel(b.maskneg, [[1, 128]], ALU.is_ge, 0, -1, fill=NEG), r=["c0"], w=["c1"])
        self.O(P, lambda e: e.memset(b.halo_m[:], 0.0), w=["halo_m"])
        for l in range(2):
            self.O(P, lambda e, l=l: e.memset(b.halo_f[l][:], 0.0), w=[("halo_f", l)])
        for h in range(8):
            self.O("dve", lambda e, h=h: e.tensor_tensor(out=b.biast[:, h, 0:128], in0=b.biast[:, h, 0:128],
                                                         in1=b.maskneg[:], op=ALU.add),
                   r=["biast", "c1"], w=["biast"])
        self.O(P, lambda e: e.memset(b.ones3[:], 1.0), w=["ones3"])
        for h in range(8):
            self.O("dve", lambda e, h=h: e.tensor_scalar(out=b.bias8[:, h, 0:256], in0=b.biast[:, h, :], scalar1=8.0,
                                                         scalar2=None, op0=ALU.mult), r=["biast"], w=["bias8"])
            self.O("dve", lambda e, h=h: e.tensor_scalar(out=b.bias8[:, h, 256:640], in0=b.ones3[:], scalar1=b.col("b31", h),
                                                         scalar2=8.0, op0=ALU.mult, op1=ALU.mult),
                   r=["ones3", "cv"], w=["bias8"])
        ao = CVOFF["alog"]
        self.O("act", lambda e: e.activation(out=b.Ab[:], in_=b.cv[:, ao:ao + 128], func=AF.Exp), r=["cv"], w=["Ab"])
        self.O("dve", lambda e: e.tensor_scalar(out=b.Ab[:], in0=b.Ab[:], scalar1=-1.0, scalar2=None, op0=ALU.mult),
               r=["Ab"], w=["Ab"])
        lo = CVOFF["lamv"]
        sc = b.smallc
        self.tmp64 = self.sb("tmp64", [128, 128], F32)
        self.O("dve", lambda e: e.tensor_tensor(out=b.tmp64[:, 0:64], in0=b.cv[:, lo:lo + 64], in1=b.cv[:, lo + 64:lo + 128],
                                                op=ALU.mult), r=["cv"], w=["t64a"])
        self.O("dve", lambda e: e.tensor_tensor(out=b.tmp64[:, 64:128], in0=b.cv[:, lo + 128:lo + 192],
                                                in1=b.cv[:, lo + 192:lo + 256], op=ALU.mult), r=["cv"], w=["t64b"])
        self.O("dve", lambda e: e.reduce_sum(out=sc[:, 2:3], in_=b.tmp64[:, 0:64], axis=AX.X), r=["t64a"], w=["sc2"])
        self.O("dve", lambda e: e.reduce_sum(out=sc[:, 3:4], in_=b.tmp64[:, 64:128], axis=AX.X), r=["t64b"], w=["sc3"])
        self.O("act", lambda e: e.activation(out=sc[:, 4:6], in_=sc[:, 2:4], func=AF.Exp), r=["sc2", "sc3"], w=["sc4"])
        self.O("dve", lambda e: e.scalar_tensor_tensor(out=sc[:, 0:1], in0=sc[:, 5:6], scalar=-LAM_INIT, in1=sc[:, 4:5],
                                                       op0=ALU.add, op1=ALU.subtract), r=["sc4"], w=["neglam"])
        self.O("pool", lambda e: e.memset(sc[:, 6:7], EPS), w=["epsc"])
        self.O("dve", lambda e: e.tensor_scalar(out=sc[:, 1:2], in0=b.col("slg"), scalar1=(1.0 - LAM_INIT), scalar2=None,
                                                op0=ALU.mult), r=["cv"], w=["slg2"])

    def casts(self, names):
        s = self.s
        i = getattr(self, "_cast_i", 0)
        for n in names:
            kc, g, nb = WSPEC[n]
            el = _welems(n)
            cb = 2048 if el % 2048 == 0 else (1024 if el % 1024 == 0 else (el if el <= 2048 else 1408))
            assert el % cb == 0, (n, el, cb)
            for blk in range(nb):
                s.dma("pool",
                      lambda e, n=n, blk=blk, cb=cb: e.dma_start(
                          out=self.wbf[n][blk].rearrange("p (a b) -> p a b", b=cb),
                          in_=self.w32[n][blk].rearrange("p (a b) -> p a b", b=cb)),
                      r=[], w=[("wd", n, blk), ("castslot", i % 8)], key=("cast", n, blk))
                i += 1
        self._cast_i = i

    def ws_init(self, seq):
        self.wseq = seq
        self.wi = 0
        self.wissued = 0

    def ws_issue(self, idx):
        n, blk = self.wseq[idx]
        slot = idx % 3
        el = _welems(n)
        self.s.dma("sp", lambda e: e.dma_start(out=self.wsl[slot][:, 0:el], in_=self.wbf[n][blk]),
                   r=[("wd", n, blk)], w=[("ws", slot)], key=("ws", slot))

    def ws_next(self, n, blk):
        idx = self.wi
        assert self.wseq[idx] == (n, blk), (self.wseq[idx], n, blk)
        while self.wissued <= min(idx + 2, len(self.wseq) - 1):
            self.ws_issue(self.wissued)
            self.wissued += 1
        self.wi += 1
        return idx % 3

    def norm(self, gname):
        b = self
        for kc in range(8):
            self.O("act", lambda e, kc=kc: e.activation(out=b.xsq[:, kc, :], in_=b.x[:, kc, :], func=AF.Square),
                   r=[("x", kc)], w=[("xsq", kc)])
        for kc in range(8):
            self.O("pe", lambda e, kc=kc: e.matmul(b.banks[6][:], lhsT=b.ones[:], rhs=b.xsq[:, kc, :],
                                                   start=(kc == 0), stop=(kc == 7)),
                   r=[("xsq", kc), "c0"], w=[("b", 6)])
        self.rstd(b.rs[:], "rs", 6, 1.0 / D)
        for kc in range(8):
            self.O("dve", lambda e, kc=kc: e.scalar_tensor_tensor(out=b.xn[:, kc, :], in0=b.x[:, kc, :],
                                                                  scalar=b.col(gname, kc), in1=b.rs[:],
                                                                  op0=ALU.mult, op1=ALU.mult),
                   r=[("x", kc), "rs", "cv"], w=[("xn", kc)])

    def rstd(self, out, tok, bank, scale):
        b = self
        self.O("act", lambda e: e.activation(out=out, in_=b.banks[bank][:], func=AF.Ln, bias=b.smallc[:, 6:7], scale=scale),
               r=[("b", bank), "epsc"], w=[tok])
        self.O("act", lambda e: e.activation(out=out, in_=out, func=AF.Exp, scale=-0.5), r=[tok], w=[tok])

    def pipeline(self, tasks):
        res = {}
        if tasks:
            res[0] = tasks[0][0]()
        for i in range(len(tasks)):
            if i + 1 < len(tasks):
                res[i + 1] = tasks[i + 1][0]()
            tasks[i][1](res.pop(i))

    def pbank(self):
        self._pb = (getattr(self, "_pb", -1) + 1) % 4
        return self._pb

    def cbank(self):
        self._cb = (getattr(self, "_cb", -1) + 1) % 2
        return 4 + self._cb

    def dg(self):
        self._dg = (getattr(self, "_dg", -1) + 1) % 8
        return self._dg

    def ue(self):
        self._ue = (getattr(self, "_ue", -1) + 1) % 3
        return self._ue

    def proj(self, slot, KC, stride, c0, act, acttok):
        b = self
        pb = self.pbank()
        for kc in range(KC):
            self.O("pe", lambda e, kc=kc: e.matmul(b.banks[pb][:], lhsT=b.wsl[slot][:, kc * stride + c0:kc * stride + c0 + 128],
                                                   rhs=act[:, kc, :], start=(kc == 0), stop=(kc == KC - 1)),
                   r=[("ws", slot), (acttok, kc)], w=[("b", pb)])
        return pb

    def conv(self, pb, K, halo, htok, cj, wname, first):
        b = self
        u = self.ue()
        ut = ("uext", u)
        H = K - 1
        ub = b.uext[u]
        if first:
            self.O("dve", lambda e: e.memset(ub[:, 0:H], 0.0), w=[ut])
        else:
            self.O("dve", lambda e: e.tensor_copy(out=ub[:, 0:H], in_=halo[:, cj, 0:H]), r=[htok], w=[ut])
        self.O("act", lambda e: e.activation(out=ub[:, H:H + T], in_=b.banks[pb][:], func=AF.Identity),
               r=[("b", pb)], w=[ut])
        self.O("dve", lambda e: e.tensor_copy(out=halo[:, cj, 0:H], in_=ub[:, T:T + H]), r=[ut], w=[htok])
        cb = self.cbank()
        for k in range(K):
            d = self.dg()
            self.O("dve", lambda e, d=d, k=k: e.tensor_scalar(out=b.dgs[d][:], in0=b.ident[:], scalar1=b.col(wname, cj * K + k),
                                                              scalar2=None, op0=ALU.mult),
                   r=["c1", "cv"], w=[("dg", d)])
            self.O("pe", lambda e, d=d, k=k: e.matmul(b.banks[cb][:], lhsT=b.dgs[d][:], rhs=ub[:, k:k + T],
                                                      start=(k == 0), stop=(k == K - 1)),
                   r=[("dg", d), ut], w=[("b", cb)])
        return cb

    def ffn(self, L, first):
        b = self
        self.norm("g_ffn%d" % L)
        self.s.barrier()
        h = self.carve(0, 22 * T).rearrange("p (a t) -> p a t", t=T)
        up = "up%d" % L
        dn = "dn%d" % L
        tasks = []
        slots = {}
        for blk in range(11):
            for g in range(4):
                is_gate = g < 2
                hj = 2 * blk + (g % 2)
                cj = hj if is_gate else 22 + hj

                def pj(blk=blk, g=g):
                    if g == 0:
                        slots[blk] = self.ws_next(up, blk)
                    return self.proj(slots[blk], 8, 512, g * 128, b.xn, "xn")

                def post(pb, is_gate=is_gate, hj=hj, cj=cj):
                    cb = self.conv(pb, 3, b.halo_f[L], ("halo_f", L, cj), cj, "fcw%d" % L, first)
                    bias = b.col("fcb%d" % L, cj)
                    if is_gate:
                        self.O("act", lambda e: e.activation(out=h[:, hj, :], in_=b.banks[cb][:], func=AF.Silu, bias=bias),
                               r=[("b", cb), "cv"], w=[("h", hj)])
                    else:
                        self.O("dve", lambda e: e.scalar_tensor_tensor(
                            out=h[:, hj, :], in0=b.banks[cb][:], scalar=bias, in1=h[:, hj, :], op0=ALU.add, op1=ALU.mult),
                            r=[("b", cb), "cv", ("h", hj)], w=[("h", hj)])
                tasks.append((pj, post))
        self.pipeline(tasks)
        for blk in range(4):
            slot = self.ws_next(dn, blk)
            for g in range(2):
                j = blk * 2 + g
                pb = self.pbank()
                for fc in range(22):
                    self.O("pe", lambda e, fc=fc, pb=pb, g=g, slot=slot: e.matmul(
                        b.banks[pb][:], lhsT=b.wsl[slot][:, fc * 256 + g * 128:fc * 256 + g * 128 + 128],
                        rhs=h[:, fc, :], start=(fc == 0), stop=(fc == 21)),
                        r=[("ws", slot), ("h", fc)], w=[("b", pb)])
                self.O("dve", lambda e, j=j, pb=pb: e.tensor_tensor(out=b.x[:, j, :], in0=b.banks[pb][:], in1=b.x[:, j, :],
                                                                    op=ALU.add),
                       r=[("b", pb), ("x", j)], w=[("x", j)])

    def load_x(self, sq, ti):
        b = self
        src = b.xT[sq].rearrange("(kc p) t -> p kc t", p=128)[:, :, ti * T:(ti + 1) * T]
        for hf in range(2):
            ks = slice(hf * 4, hf * 4 + 4)
            self.s.dma("sp", lambda e, ks=ks: e.dma_start(out=b.x[:, ks, :], in_=src[:, ks, :]),
                       w=[("x", kc) for kc in range(hf * 4, hf * 4 + 4)], key=("xl", hf))

    def store_x(self, sq, ti):
        b = self
        dst = b.yT[sq].rearrange("(kc p) t -> p kc t", p=128)[:, :, ti * T:(ti + 1) * T]
        for hf in range(2):
            ks = slice(hf * 4, hf * 4 + 4)
            self.s.dma("sp", lambda e, ks=ks: e.dma_start(out=dst[:, ks, :], in_=b.x[:, ks, :]),
                       r=[("x", kc) for kc in range(hf * 4, hf * 4 + 4)], key=("xs", hf))

    def mixer(self, sq, ti):
        b = self
        first = ti == 0
        self.norm("g_ssm")
        self.s.barrier()
        v3 = lambda off, n: self.carve(off, n * T).rearrange("p (a t) -> p a t", t=T)
        zs = v3(0, 16)
        xbc = v3(8192, 24)
        yg = zs
        xdt = self.carve(28672, 2048)
        xdtw = self.carve(30720, 2048)
        btok = self.carve(32768, 512).rearrange("p (g n) -> p g n", n=128)
        Ah = [self.carve(20480 + i * 1024, 1024).rearrange("p (h l) -> p h l", l=128) for i in range(4)]
        E = [self.carve(24576 + i * 1024, 1024).rearrange("p (h l) -> p h l", l=128) for i in range(4)]
        M = [self.carve(33280 + i * 1024, 1024).rearrange("p (h l) -> p h l", l=128) for i in range(4)]
        cbm = self.carve(39424, 512).rearrange("p (g l) -> p g l", l=128)
        ytok = self.carve(39936, 2048)
        t1 = [self.carve(41984 + i * 1024, 1024, F32) for i in range(2)]
        tmpf = [self.carve(44032 + i * 256, 256, F32) for i in range(2)]
        sm = [self.carve(44544 + i * 256, 256, F32) for i in range(8)]
        tmpf2 = [self.carve(46592 + i * 2048, 2048, F32) for i in range(2)]
        dtr, dtt, aa, e1 = sm[0], sm[1], sm[2], sm[3]
        acs_sb, eacs, wdec, dec = sm[4], sm[5], sm[6], sm[7]
        bf = lambda i: b.banks[i][:].bitcast(BF16)
        if first:
            self.O("dve", lambda e: e.memset(b.S[:], 0.0), w=["S"])
            self.O("dve", lambda e: e.memset(b.Sb[:], 0.0), w=["Sb"])
        tasks = []
        slots = {}
        for blk in range(10):
            for g in range(4):
                j = blk * 4 + g

                def pj(blk=blk, g=g):
                    if g == 0:
                        slots[blk] = self.ws_next("in", blk)
                    return self.proj(slots[blk], 8, 512, g * 128, b.xn, "xn")

                def post(pb, j=j):
                    if j < 16:
                        self.O("act", lambda e: e.activation(out=zs[:, j, :], in_=b.banks[pb][:], func=AF.Silu),
                               r=[("b", pb)], w=[("zs", j)])
                    else:
                        cj = j - 16
                        cb = self.conv(pb, 4, b.halo_m, ("halo_m", cj), cj, "mcw", first)
                        self.O("act", lambda e: e.activation(out=xbc[:, cj, :], in_=b.banks[cb][:], func=AF.Silu,
                                                             bias=b.col("mcb", cj)),
                               r=[("b", cb), "cv"], w=[("xbc", cj)])
                tasks.append((pj, post))
        self.pipeline(tasks)
        slot = self.ws_next("dt", 0)
        for c in range(4):
            for kc in range(8):
                self.O("pe", lambda e, c=c, kc=kc, slot=slot: e.matmul(b.banks[7][:, c * 32:(c + 1) * 32],
                                                            lhsT=b.xn[:, kc, c * 128:(c + 1) * 128],
                                                            rhs=b.wsl[slot][:, kc * 32:(kc + 1) * 32],
                                                            start=(kc == 0), stop=(kc == 7)),
                       r=[("ws", slot), ("xn", kc)], w=[("b", 7)])
        do = CVOFF["dtb"]
        self.O("dve", lambda e: e.tensor_tensor(out=dtr[:], in0=b.banks[7][:, 0:128], in1=b.cv[:, do:do + 128], op=ALU.add),
               r=[("b", 7), "cv"], w=["dtr"])
        self.O("act", lambda e: e.activation(out=e1[:], in_=dtr[:], func=AF.Exp), r=["dtr"], w=["e1"])
        self.O("act", lambda e: e.activation(out=dtt[:], in_=e1[:], func=AF.Ln, bias=1.0), r=["e1"], w=["dt"])
        self.O("dve", lambda e: e.tensor_tensor(out=aa[:], in0=dtt[:], in1=b.Ab[:], op=ALU.mult), r=["dt", "Ab"], w=["aa"])
        self.dump("dtt", dtt[:], ["dt"], [128, 128])
        self.dump("dtr", dtr[:], ["dtr"], [128, 128])
        self.dump("x0", xbc[:, 0, :], [("xbc", 0)], [128, 512], BF16)
        self.dump("B0", xbc[:, 16, :], [("xbc", 16)], [128, 512], BF16)
        self.dump("z0", zs[:, 0, :], [("zs", 0)], [128, 512], BF16)
        def chunk(c):
            cs = slice(c * 128, (c + 1) * 128)
            a_c = aa[:, c * 32:(c + 1) * 32]
            dt_c = dtt[:, c * 32:(c + 1) * 32]
            self.O("pe", lambda e, a_c=a_c: e.matmul(b.banks[6][:, 0:32], lhsT=b.tri[:], rhs=a_c, start=True, stop=True),
                   r=["aa", "c1"], w=[("b", 6)])
            self.O("pe", lambda e, a_c=a_c: e.matmul(b.banks[6][:, 32:64], lhsT=b.onesf[:], rhs=a_c, start=True, stop=True),
                   r=["aa", "c0"], w=[("b", 6)])
            self.O("act", lambda e: e.activation(out=eacs[:, 0:32], in_=b.banks[6][:, 0:32], func=AF.Exp),
                   r=[("b", 6)], w=["eacs"])
            self.O("dve", lambda e: e.tensor_copy(out=acs_sb[:, 0:32], in_=b.banks[6][:, 0:32]), r=[("b", 6)], w=["acs"])
            self.O("dve", lambda e: e.tensor_tensor(out=wdec[:, 0:32], in0=b.banks[6][:, 32:64], in1=acs_sb[:, 0:32],
                                                    op=ALU.subtract), r=[("b", 6), "acs"], w=["wd"])
            self.O("act", lambda e: e.activation(out=wdec[:, 0:32], in_=wdec[:, 0:32], func=AF.Exp), r=["wd"], w=["wd"])
            self.O("act", lambda e: e.activation(out=dec[:, 0:32], in_=b.banks[6][:, 32:64], func=AF.Exp),
                   r=[("b", 6)], w=["dec"])
            for j in range(16):
                bk = 4 + j // 8
                self.O("pe", lambda e, j=j, bk=bk: e.transpose(out=bf(bk)[:, (j % 8) * 128:(j % 8 + 1) * 128],
                                                               in_=xbc[:, j, cs], identity=b.ident[:]),
                       r=[("xbc", j), "c1"], w=[("b", bk)])
            for half in range(2):
                self.O("dve", lambda e, half=half: e.tensor_tensor(
                    out=xdt[:, half * 1024:(half + 1) * 1024].rearrange("p (h d) -> p h d", d=64),
                    in0=bf(4 + half)[:, 0:1024].rearrange("p (h d) -> p h d", d=64),
                    in1=self.bc_last(dt_c[:, half * 16:(half + 1) * 16], 64), op=ALU.mult),
                    r=[("b", 4 + half), "dt"], w=[("xdt", 2 * half), ("xdt", 2 * half + 1)])
            self.O("pool", lambda e: e.tensor_tensor(out=xdtw[:].rearrange("p (h d) -> p h d", d=64),
                                                     in0=xdt[:].rearrange("p (h d) -> p h d", d=64),
                                                     in1=self.bc_last(wdec[:, 0:32], 64), op=ALU.mult),
                   r=[("xdt", q) for q in range(4)] + ["wd"], w=[("xdtw", q) for q in range(4)])
            for g in range(4):
                self.O("pe", lambda e, g=g: e.transpose(out=bf(4)[:, g * 128:(g + 1) * 128], in_=xbc[:, 16 + g, cs],
                                                        identity=b.ident[:]),
                       r=[("xbc", 16 + g), "c1"], w=[("b", 4)])
            self.O("act", lambda e: e.activation(out=btok[:].rearrange("p g n -> p (g n)"), in_=bf(4)[:, 0:512],
                                                 func=AF.Identity), r=[("b", 4)], w=["btok"])
            for g in range(4):
                self.O("pe", lambda e, g=g: e.matmul(b.banks[0][:, g * 128:(g + 1) * 128], lhsT=xbc[:, 16 + g, cs],
                                                     rhs=xbc[:, 20 + g, cs], start=True, stop=True),
                       r=[("xbc", 16 + g), ("xbc", 20 + g)], w=[("b", 0)])
            for g in range(4):
                self.O("dve", lambda e, g=g: e.tensor_tensor(out=cbm[:, g, :], in0=b.banks[0][:, g * 128:(g + 1) * 128],
                                                             in1=b.tri[:], op=ALU.mult),
                       r=[("b", 0), "c1"], w=[("cbm", g)])
            for g in range(4):
                self.O("pool", lambda e, g=g: e.tensor_tensor(out=Ah[g][:], in0=self.bc_mid(b.trib[:], 8),
                                                              in1=self.bc_last(a_c[:, g * 8:(g + 1) * 8], 128), op=ALU.mult),
                       r=["c1", "aa"], w=[("Ah", g)])
            for g in range(4):
                self.O("pool", lambda e, g=g: e.tensor_tensor(
                    out=b.S[:, g * 512:(g + 1) * 512].rearrange("p (h d) -> p h d", d=64),
                    in0=b.S[:, g * 512:(g + 1) * 512].rearrange("p (h d) -> p h d", d=64),
                    in1=self.bc_last(dec[:, g * 8:(g + 1) * 8], 64), op=ALU.mult),
                    r=["dec", ("S", g)], w=[("S", g)])
            for g in range(4):
                for i in range(2):
                    self.O("pe", lambda e, i=i, g=g: e.matmul(b.banks[1 + i][:], lhsT=b.ustr[:],
                                                              rhs=Ah[g][:, i * 4:(i + 1) * 4, :].rearrange("p h l -> p (h l)"),
                                                              start=True, stop=True),
                           r=[("Ah", g), "c1"], w=[("b", 1 + i)])
                    self.O("act", lambda e, i=i, g=g: e.activation(
                        out=E[g][:, i * 4:(i + 1) * 4, :].rearrange("p h l -> p (h l)"), in_=b.banks[1 + i][:], func=AF.Exp),
                        r=[("b", 1 + i)], w=[("E", g, i)])
            for g in range(4):
                for hh in range(8):
                    self.O("dve", lambda e, hh=hh, g=g: e.tensor_tensor(out=M[g][:, hh, :], in0=E[g][:, hh, :],
                                                                        in1=cbm[:, g, :], op=ALU.mult),
                           r=[("E", g, hh // 4), ("cbm", g)], w=[("M", g)])
            for g in range(4):
                ob = (3, 1)[g % 2]
                db = (7, 2)[g % 2]
                self.O("pe", lambda e, g=g, ob=ob: e.matmul(b.banks[ob][:], lhsT=xbc[:, 20 + g, cs],
                                                            rhs=b.Sb[:, g * 512:(g + 1) * 512], start=True, stop=True),
                       r=[("xbc", 20 + g), ("Sb", g)], w=[("b", ob)])
                for hh in range(8):
                    h = g * 8 + hh
                    self.O("pe", lambda e, hh=hh, h=h, g=g, db=db: e.matmul(b.banks[db][:, hh * 64:(hh + 1) * 64],
                                                                            lhsT=M[g][:, hh, :],
                                                                            rhs=xdt[:, h * 64:(h + 1) * 64], start=True, stop=True),
                           r=[("M", g), ("xdt", g)], w=[("b", db)])
                tt = t1[g % 2]
                self.O("dve", lambda e, g=g, tt=tt, ob=ob: e.tensor_tensor(
                    out=tt[:].rearrange("p (h d) -> p h d", d=64), in0=b.banks[ob][:].rearrange("p (h d) -> p h d", d=64),
                    in1=self.bc_last(eacs[:, g * 8:(g + 1) * 8], 64), op=ALU.mult),
                    r=[("b", ob), "eacs"], w=[("t1", g % 2)])
                self.O("dve", lambda e, g=g, tt=tt, db=db: e.tensor_tensor(out=ytok[:, g * 512:(g + 1) * 512], in0=b.banks[db][:],
                                                                           in1=tt[:], op=ALU.add),
                       r=[("b", db), ("t1", g % 2)], w=[("ytok", g)])
            for g in range(4):
                sbk = (0, 6)[g % 2]
                self.O("pe", lambda e, g=g, sbk=sbk: e.matmul(b.banks[sbk][:], lhsT=btok[:, g, :], rhs=xdtw[:, g * 512:(g + 1) * 512],
                                                              start=True, stop=True),
                       r=["btok", ("xdtw", g)] + [("cbm", q) for q in range(4)], w=[("b", sbk)])
                self.O("dve", lambda e, g=g, sbk=sbk: e.tensor_tensor(out=b.S[:, g * 512:(g + 1) * 512], in0=b.banks[sbk][:],
                                                                      in1=b.S[:, g * 512:(g + 1) * 512], op=ALU.add),
                       r=[("b", sbk), ("S", g)], w=[("S", g)])
                self.O("act", lambda e, g=g: e.activation(out=b.Sb[:, g * 512:(g + 1) * 512], in_=b.S[:, g * 512:(g + 1) * 512],
                                                          func=AF.Identity), r=[("S", g)], w=[("Sb", g)])
            if c == 0:
                self.dump("ytok0", ytok[:], [("ytok", q) for q in range(4)], [128, 2048], BF16)
                self.dump("eacs0", eacs[:, 0:32], ["eacs"], [128, 32])
                self.dump("acs0", acs_sb[:, 0:32], ["acs"], [128, 32])
                self.dump("S0", b.S[:], [("S", q) for q in range(4)], [128, 2048])
                self.dump("xdt0", xdt[:], [("xdt", q) for q in range(4)], [128, 2048], BF16)
            for j in range(16):
                bk = 4 + j // 8
                self.O("pe", lambda e, j=j, bk=bk: e.transpose(out=bf(bk)[:, (j % 8) * 128:(j % 8 + 1) * 128],
                                                               in_=ytok[:, j * 128:(j + 1) * 128], identity=b.ident[:]),
                       r=[("ytok", j // 4), "c1"], w=[("b", bk)])
            for half in range(2):
                j0 = half * 8
                tf = tmpf2[half]
                self.O("pool", lambda e, j0=j0, tf=tf: e.tensor_tensor(
                    out=tf[:].rearrange("p (j l) -> p j l", l=128), in0=xbc[:, j0:j0 + 8, cs],
                    in1=self.bc_last(b.cv[:, CVOFF["dsk"] + j0:CVOFF["dsk"] + j0 + 8], 128), op=ALU.mult),
                    r=[("xbc", j0 + q) for q in range(8)] + ["cv"], w=[("tmpf", half)])
                self.O("dve", lambda e, half=half, tf=tf: e.tensor_tensor(
                    out=tf[:].rearrange("p (j l) -> p j l", l=128), in0=bf(4 + half)[:, 0:1024].rearrange("p (j l) -> p j l", l=128),
                    in1=tf[:].rearrange("p (j l) -> p j l", l=128), op=ALU.add),
                    r=[("b", 4 + half), ("tmpf", half)], w=[("tmpf", half)])
                self.O("pool", lambda e, j0=j0, tf=tf: e.tensor_tensor(
                    out=yg[:, j0:j0 + 8, cs], in0=tf[:].rearrange("p (j l) -> p j l", l=128), in1=zs[:, j0:j0 + 8, cs],
                    op=ALU.mult),
                    r=[("tmpf", half)] + [("zs", j0 + q) for q in range(8)], w=[("zs", j0 + q) for q in range(8)])
        for c in range(4):
            chunk(c)
        for g in range(4):
            for q in range(4):
                j = g * 4 + q
                self.O("act", lambda e, j=j, q=q: e.activation(out=b.xsq[:, q, :], in_=yg[:, j, :], func=AF.Square),
                       r=[("zs", j)], w=[("xsq", q)])
            for q in range(4):
                self.O("pe", lambda e, q=q: e.matmul(b.banks[6][:], lhsT=b.ones[:], rhs=b.xsq[:, q, :], start=(q == 0),
                                                     stop=(q == 3)), r=[("xsq", q), "c0"], w=[("b", 6)])
            self.rstd(b.rs[:], "rs", 6, 1.0 / 512)
            for q in range(4):
                j = g * 4 + q
                self.O("dve", lambda e, j=j: e.scalar_tensor_tensor(out=yg[:, j, :], in0=yg[:, j, :], scalar=b.col("ng", j),
                                                                    in1=b.rs[:], op0=ALU.mult, op1=ALU.mult),
                       r=[("zs", j), "rs", "cv"], w=[("zs", j)])
        for blk in range(4):
            slot = self.ws_next("out", blk)
            for g2 in range(2):
                j = blk * 2 + g2
                pb = self.proj(slot, 16, 256, g2 * 128, yg, "zs")
                self.O("dve", lambda e, j=j, pb=pb: e.tensor_tensor(out=b.x[:, j, :], in0=b.banks[pb][:], in1=b.x[:, j, :],
                                                                    op=ALU.add),
                       r=[("b", pb), ("x", j)], w=[("x", j)])

    def headnorm(self, pb, gcol, outap, outtok, bufs, i):
        b = self
        ksq, rk = bufs
        self.O("act", lambda e: e.activation(out=ksq[i][:], in_=b.banks[pb][:], func=AF.Square), r=[("b", pb)], w=[("ksq", i)])
        self.O("pe", lambda e: e.matmul(b.banks[6][:], lhsT=b.bones[:], rhs=ksq[i][:], start=True, stop=True),
               r=[("ksq", i), "c0"], w=[("b", 6)])
        self.rstd(rk[i][:], ("rk", i), 6, 1.0 / 64)
        self.O("dve", lambda e: e.scalar_tensor_tensor(out=outap, in0=b.banks[pb][:], scalar=gcol, in1=rk[i][:],
                                                       op0=ALU.mult, op1=ALU.mult),
               r=[("b", pb), ("rk", i), "cv"], w=[outtok])

    def kvproj(self, sq, ti):
        b = self
        s = self.s
        self.norm("g_kv")
        self.s.barrier()
        kt = [self.carve(i * 512, 512) for i in range(2)]
        ksq = [self.carve(1024 + i * 512, 512) for i in range(2)]
        rk = [self.carve(2048 + i * 1024, 1024, F32) for i in range(2)]
        vt = [self.carve(4096 + i * 512, 512) for i in range(2)]
        tasks = []
        slots = {}
        for blk in range(2):
            for g in range(4):
                h = blk * 4 + g

                def pj(blk=blk, g=g):
                    if g == 0:
                        slots[blk] = self.ws_next("kv", blk)
                    return self.proj(slots[blk], 8, 512, g * 128, b.xn, "xn")

                def post(pb, h=h):
                    i = h % 2
                    self.headnorm(pb, b.col("kng"), kt[i][:], ("kt", i), (ksq, rk), i)
                    s.dma("sp", lambda e: e.dma_start(out=b.kT_d[sq, h, :, ti * T:(ti + 1) * T], in_=kt[i][:]),
                          r=[("kt", i)], w=[("kTd", sq, h, ti)], key=("kst", i), fenced=True)
                tasks.append((pj, post))
        self.pipeline(tasks)
        for blk in range(2, 4):
            slot = self.ws_next("kv", blk)
            nh = blk - 2
            for c in range(4):
                pb = self.pbank()
                i = c % 2
                for kc in range(8):
                    self.O("pe", lambda e, kc=kc, c=c, pb=pb, slot=slot: e.matmul(b.banks[pb][:], lhsT=b.xn[:, kc, c * 128:(c + 1) * 128],
                                                                       rhs=b.wsl[slot][:, kc * 512:(kc + 1) * 512],
                                                                       start=(kc == 0), stop=(kc == 7)),
                           r=[("ws", slot), ("xn", kc)], w=[("b", pb)])
                self.O("act", lambda e, i=i, pb=pb: e.activation(out=vt[i][:], in_=b.banks[pb][:], func=AF.Identity),
                       r=[("b", pb)], w=[("vt", i)])
                s.dma("sp", lambda e, c=c, i=i, nh=nh: e.dma_start(
                    out=b.v_d[sq, nh * 4:(nh + 1) * 4, :, ti * 4 + c, :].rearrange("h p d -> p h d"),
                    in_=vt[i][:].rearrange("p (h d) -> p h d", d=128)),
                    r=[("vt", i)], w=[("vd", sq, ti, c, nh)], key=("vst", i), fenced=True)

    def attn(self, sq, ti):
        b = self
        s = self.s
        self.norm("g_attn")
        self.s.barrier()
        v3 = lambda off: self.carve(off, 8 * T).rearrange("p (a t) -> p a t", t=T)
        qT = [v3(0), v3(4096)]
        oT = v3(8192)
        KT = [self.carve(12288 + i * 2048, 2048) for i in range(2)]
        VV = [self.carve(16384 + i * 2048, 2048).rearrange("p (k d) -> p k d", d=128) for i in range(2)]
        PT = [self.carve(20480 + i * 512, 512) for i in range(4)]
        tmp = [self.carve(22528 + i * 1024, 1024, F32) for i in range(10)]
        ksq = [self.carve(32768 + i * 512, 512) for i in range(2)]
        rk = [tmp[8], tmp[9]]
        self.O("pool", lambda e: e.memset(qT[0][64:128, :, :], 0.0), w=[("qT0z",)])
        self.O("pool", lambda e: e.memset(qT[1][0:64, :, :], 0.0), w=[("qT1z",)])
        tasks = []
        slots = {}
        for blk in range(2):
            for g in range(4):
                h = blk * 4 + g

                def pj(blk=blk, g=g):
                    if g == 0:
                        slots[blk] = self.ws_next("q", blk)
                    return self.proj(slots[blk], 8, 512, g * 128, b.xn, "xn")

                def post(pb, h=h):
                    i = h % 2
                    self.O("act", lambda e: e.activation(out=ksq[i][:], in_=b.banks[pb][:], func=AF.Square),
                           r=[("b", pb)], w=[("ksq", i)])
                    self.O("pe", lambda e: e.matmul(b.banks[6][:], lhsT=b.bones[:], rhs=ksq[i][:], start=True, stop=True),
                           r=[("ksq", i), "c0"], w=[("b", 6)])
                    self.rstd(rk[i][:], ("rk", i), 6, 1.0 / 64)
                    for c in range(2):
                        ps_ = slice(c * 64, (c + 1) * 64)
                        self.O("dve", lambda e, c=c, ps_=ps_: e.scalar_tensor_tensor(
                            out=qT[c][ps_, h, :], in0=b.banks[pb][ps_, :], scalar=b.cv[ps_, CVOFF["qng"]:CVOFF["qng"] + 1],
                            in1=rk[i][ps_, :], op0=ALU.mult, op1=ALU.mult),
                            r=[("b", pb), ("rk", i), "cv", ("qT0z",), ("qT1z",)], w=[("qT", h, c)])
                tasks.append((pj, post))
        self.pipeline(tasks)
        nk = (ti + 1) * T
        nkb = nk // 128
        tR = [tmp[2], tmp[3]]
        tAB = [tmp[4], tmp[5]]
        tC, tD = tmp[6], tmp[7]
        osq = ksq[0]
        its = [(h, c, kb) for h in range(8) for c in range(2) for kb in range(nkb)]
        N = len(its)
        ptt_of = {}
        deferred = []

        def loads(h):
            i = h % 2
            s.dma("sp", lambda e: e.dma_start(out=KT[i][:, 0:nk], in_=b.kT_d[sq, h, :, 0:nk]),
                  r=[("kTd", sq, h, t2) for t2 in range(ti + 1)], w=[("KT", i)], key=("kl", i), fenced=True)
            s.dma("sp", lambda e: e.dma_start(
                out=VV[i][:, 0:nkb, :],
                in_=b.v_d[sq, h, :, 0:nkb, :]),
                r=[("vd", sq, t2, c2, h // 4) for t2 in range(ti + 1) for c2 in range(4)], w=[("VV", i)],
                key=("vl", i), fenced=True)

        def qk(n):
            h, c, kb = its[n]
            i = h % 2
            c0 = max(0, kb - ti * 4) * 128
            sb = n % 3
            far = kb <= ti * 4 - 2
            self.O("pe", lambda e: e.matmul(b.banks[sb][:, c0:T], lhsT=KT[i][:, kb * 128:(kb + 1) * 128],
                                            rhs=qT[c][:, h, c0:T], start=True, stop=far),
                   r=[("KT", i), ("qT", h, c)], w=[("b", sb)])
            if not far:
                d0 = ti * 4 + c0 // 128 - kb
                w0 = d0 * 128
                self.O("pe", lambda e: e.matmul(b.banks[sb][:, c0:T], lhsT=b.ident[:], rhs=b.bias8[:, h, w0:w0 + (T - c0)],
                                                start=False, stop=True),
                       r=["c1", "bias8"], w=[("b", sb)])

        def post(n):
            h, c, kb = its[n]
            c0 = max(0, kb - ti * 4) * 128
            sb = n % 3
            pt = PT[n % 4]
            ptt = ("PT", n % 4)
            far = kb <= ti * 4 - 2
            if far:
                self.O("act", lambda e: e.activation(out=pt[:], in_=b.banks[sb][:], func=AF.Exp, bias=b.col("b31", h),
                                                     scale=0.125), r=[("b", sb), "cv"], w=[ptt])
            else:
                self.O("act", lambda e: e.activation(out=pt[:, c0:T], in_=b.banks[sb][:, c0:T], func=AF.Exp, scale=0.125),
                       r=[("b", sb)], w=[ptt])

        def pv(n):
            h, c, kb = its[n]
            i = h % 2
            c0 = max(0, kb - ti * 4) * 128
            pt = PT[n % 4]
            ptt = ("PT", n % 4)
            self.O("pe", lambda e: e.matmul(b.banks[3 + c][:, c0:T], lhsT=VV[i][:, kb, :], rhs=pt[:, c0:T],
                                            start=(kb == 0), stop=(kb == nkb - 1)),
                   r=[("VV", i), ptt], w=[("b", 3 + c)])
            self.O("pe", lambda e: e.matmul(b.banks[5 + c][:, c0:T], lhsT=b.ones[:], rhs=pt[:, c0:T],
                                            start=(kb == 0), stop=(kb == nkb - 1)),
                   r=["c0", ptt], w=[("b", 5 + c)])

        def epi1(h, c):
            self.O("act", lambda e: e.activation(out=tR[c][:], in_=b.banks[5 + c][:], func=AF.Ln), r=[("b", 5 + c)],
                   w=[("tR", c)])
            self.O("act", lambda e: e.activation(out=tR[c][:], in_=tR[c][:], func=AF.Exp, scale=-1.0), r=[("tR", c)],
                   w=[("tR", c)])
            self.O("dve", lambda e: e.tensor_tensor(out=tAB[c][:], in0=b.banks[3 + c][:], in1=tR[c][:], op=ALU.mult),
                   r=[("b", 3 + c), ("tR", c)], w=[("tAB", c)])
            if c == 1:
                self.O("dve", lambda e: e.scalar_tensor_tensor(out=tC[:], in0=tAB[1][:], scalar=b.smallc[:, 0:1],
                                                               in1=tAB[0][:], op0=ALU.mult, op1=ALU.add),
                       r=[("tAB", 0), ("tAB", 1), "neglam"], w=["tC"])
                self.O("act", lambda e: e.activation(out=osq[:], in_=tC[:], func=AF.Square), r=["tC"], w=[("ksq", 0)])

        def epi2(h):
            self.O("pe", lambda e: e.matmul(b.banks[7][:], lhsT=b.ones[:], rhs=osq[:], start=True, stop=True),
                   r=[("ksq", 0), "c0"], w=[("b", 7)])
            self.rstd(tD[:], "tD", 7, 1.0 / 128)
            self.O("dve", lambda e: e.scalar_tensor_tensor(out=oT[:, h, :], in0=tC[:], scalar=b.smallc[:, 1:2], in1=tD[:],
                                                           op0=ALU.mult, op1=ALU.mult),
                   r=["tC", "tD", "slg2"], w=[("oT", h)])

        def run_deferred(n):
            keep = []
            for at, fn in deferred:
                if at <= n:
                    fn()
                else:
                    keep.append((at, fn))
            deferred[:] = keep

        loads(0)
        if 1 < 8:
            loads(1)
        qk(0)
        if N > 1:
            qk(1)
        for n in range(N):
            h, c, kb = its[n]
            post(n)
            run_deferred(n)
            pv(n)
            if n + 2 < N:
                h2, c2, kb2 = its[n + 2]
                qk(n + 2)
            if kb == nkb - 1:
                deferred.append((n + 1, lambda h=h, c=c: epi1(h, c)))
                if c == 1:
                    deferred.append((n + 5, lambda h=h: epi2(h)))
                    if h + 2 < 8:
                        deferred.append((n, lambda h=h: loads(h + 2)))
        run_deferred(10 ** 9)
        for blk in range(2):
            slot = self.ws_next("ao", blk)
            for g in range(4):
                j = blk * 4 + g
                pb = self.proj(slot, 8, 512, g * 128, oT, "oT")
                self.O("dve", lambda e, j=j, pb=pb: e.tensor_tensor(out=b.x[:, j, :], in0=b.banks[pb][:], in1=b.x[:, j, :],
                                                                    op=ALU.add),
                       r=[("b", pb), ("x", j)], w=[("x", j)])

    def build(self):
        s = self.s
        ph = self.phases
        self.setup()
        s.barrier()
        cast_plan = {"M": [], "F": [], "K": [], "A": [], "G": []}
        order = [p for p in "MFKAG" if p in ph]
        wof = {"M": ["in", "dt", "out"], "F": ["up0", "dn0"], "K": ["kv"], "A": ["q", "ao"], "G": ["up1", "dn1"]}
        if order:
            self.casts(wof[order[0]])
            for a_, b_ in zip(order, order[1:]):
                cast_plan[a_] = wof[b_]
        per_tile = []
        if "M" in ph:
            per_tile += [("in", i) for i in range(10)] + [("dt", 0)] + [("out", i) for i in range(4)]
        if "F" in ph:
            per_tile += [("up0", i) for i in range(11)] + [("dn0", i) for i in range(4)]
        if "K" in ph:
            per_tile += [("kv", i) for i in range(4)]
        if "A" in ph:
            per_tile += [("q", i) for i in range(2)] + [("ao", i) for i in range(2)]
        if "G" in ph:
            per_tile += [("up1", i) for i in range(11)] + [("dn1", i) for i in range(4)]
        self.ws_init(per_tile * (self.nseq * self.ntiles))
        for sq in range(self.nseq):
            for ti in range(self.ntiles):
                first = ti == 0
                self.load_x(sq, ti)
                if "M" in ph:
                    if sq == 0 and ti == 0:
                        self.casts(cast_plan["M"])
                    self.mixer(sq, ti)
                if "F" in ph:
                    if sq == 0 and ti == 0:
                        self.casts(cast_plan["F"])
                    self.ffn(0, first)
                if "K" in ph:
                    if sq == 0 and ti == 0:
                        self.casts(cast_plan["K"])
                    self.kvproj(sq, ti)
                if "A" in ph:
                    if sq == 0 and ti == 0:
                        self.casts(cast_plan["A"])
                    self.attn(sq, ti)
                if "G" in ph:
                    if sq == 0 and ti == 0:
                        self.casts(cast_plan["G"])
                    self.ffn(1, first)
                self.store_x(sq, ti)
        s.emit()


def _t5_bucket_np(n):
    n = np.maximum(n, 0)
    max_exact = 16
    nf = np.maximum(n, 1).astype(np.float32)
    large = max_exact + (np.log(nf / max_exact) / math.log(128 / max_exact) * (32 - max_exact)).astype(np.int32)
    large = np.minimum(large, 31)
    return np.where(n < max_exact, n, large)


def _cols(v):
    return np.ascontiguousarray(v.reshape(-1, 128).T)


def prepare_shared(inp):
    f = np.float32
    sh = {}
    in_w = inp["ssm_in_w"][0]
    sh["w_in"] = _blockify(in_w[:, :5120], 4)
    sh["w_dt"] = np.ascontiguousarray(in_w[:, 5120:5152].reshape(8, 128, 32).transpose(1, 0, 2)).reshape(1, 128, 256)
    sh["w_out"] = _blockify(inp["ssm_out_w"][0], 2)
    for L in range(2):
        up = inp["ffn_up_w"][L]
        order = []
        for blk in range(11):
            order += [2 * blk, 2 * blk + 1, 22 + 2 * blk, 22 + 2 * blk + 1]
        idx = np.concatenate([np.arange(c * 128, (c + 1) * 128) for c in order])
        sh["w_up%d" % L] = _blockify(up[:, idx], 4)
        sh["w_dn%d" % L] = _blockify(inp["ffn_down_w"][L], 2)
    sh["w_kv"] = _blockify(inp["kv_w"], 4)
    sh["w_q"] = _blockify(inp["q_w"][0], 4)
    sh["w_ao"] = _blockify(inp["attn_out_w"][0], 4)
    cv = np.zeros((128, NCV), f)

    def put(name, arr):
        arr = np.asarray(arr, f)
        cv[:, CVOFF[name]:CVOFF[name] + arr.shape[1]] = arr

    put("g_ssm", _cols(inp["ssm_ln_g"][0]))
    put("g_ffn0", _cols(inp["ffn_ln_g"][0]))
    put("g_ffn1", _cols(inp["ffn_ln_g"][1]))
    put("g_kv", _cols(inp["kv_ln_g"]))
    put("g_attn", _cols(inp["attn_ln_g"][0]))
    mcw = inp["ssm_conv_w"][0]
    put("mcw", mcw.reshape(4, 24, 128).transpose(2, 1, 0).reshape(128, 96))
    put("mcb", _cols(inp["ssm_conv_b"][0]))
    for L in range(2):
        fw = inp["ffn_conv_w"][L]
        put("fcw%d" % L, fw.reshape(3, 44, 128).transpose(2, 1, 0).reshape(128, 132))
        put("fcb%d" % L, _cols(inp["ffn_conv_b"][L]))
    put("ng", _cols(inp["ssm_norm_g"][0]))
    put("dsk", _cols(np.repeat(inp["ssm_d"][0], 64)))
    put("kng", np.tile(inp["k_norm_g"], 2).reshape(128, 1))
    put("qng", np.tile(inp["q_norm_g"][0], 2).reshape(128, 1))
    put("slg", inp["subln_g"][0].reshape(128, 1))
    put("b31", np.tile(inp["rel_bias"][31][None, :], (128, 1)))
    put("dtb", np.tile(inp["ssm_dt_bias"][0][None, :], (128, 4)))
    put("alog", np.tile(inp["ssm_a_log"][0][None, :], (128, 4)))
    put("lamv", np.tile(inp["lam_vecs"][0].reshape(1, 256), (128, 1)))
    sh["cvec"] = cv
    k = np.arange(128)[:, None]
    j = np.arange(256)[None, :]
    bucket = _t5_bucket_np(j - k)
    rb = inp["rel_bias"]
    sh["biast"] = np.ascontiguousarray(rb[bucket].transpose(0, 2, 1)).astype(f)
    return sh


_NC_CACHE = {}


def kernel(**inputs):
    inp = {k: np.asarray(v) for k, v in inputs.items()}
    sh = prepare_shared(inp)
    x = inp["x"]
    if "nc" not in _NC_CACHE:
        nc = bass.Bass("TRN2", target_bir_lowering=False)
        Builder(nc).build()
        _NC_CACHE["nc"] = nc
    nc = _NC_CACHE["nc"]
    in_maps = []
    for c in range(NCORES):
        m = dict(sh)
        m["xT"] = np.ascontiguousarray(x[2 * c:2 * c + 2].transpose(0, 2, 1))
        in_maps.append(m)
    res = run_bass_kernel_spmd(nc, in_maps, core_ids=list(range(NCORES)))
    out = np.empty_like(x)
    for c in range(NCORES):
        out[2 * c:2 * c + 2] = res.results[c]["yT"].transpose(0, 2, 1)
    return out
```
